# Optimizing a Trainium2 kernel written in Bass

```python
import math
import jax, jax.numpy as jnp
from jax import lax
import numpy as np

D_MODEL = 1024
BATCH = 4
SEQ = 4096
DEPTH = 1

GLA_HEADS = 4
GLA_DK = 64
GLA_DV = 128
GLA_RANK = 16
GLA_TAU = 16.0
GLA_CHUNK = 64
MOBA_HEADS = 8
MOBA_HD = 64
MOBA_BLOCK = 256
MOBA_TOPK = 3
MOBA_QCHUNK = 64
ROPE_THETA = 10000.0
MIX_WIDTH = GLA_HEADS * GLA_DV + MOBA_HEADS * MOBA_HD
IN_SPLITS = (GLA_HEADS * GLA_DK, GLA_HEADS * GLA_DK, GLA_HEADS * GLA_DV, GLA_HEADS * GLA_DV,
             GLA_RANK, MOBA_HEADS * MOBA_HD, MOBA_HEADS * MOBA_HD, MOBA_HEADS * MOBA_HD)
IN_WIDTH = sum(IN_SPLITS)
PEER_HEADS = 8
PEER_NKEYS = 128
PEER_N = PEER_NKEYS * PEER_NKEYS
PEER_QDIM = 256
PEER_TOPK = 16
PEER_TOKCHUNK = 128
EPS = 1e-6

kernel_name = "hymba_gla_moba_peer_layer"


def rmsnorm(x, w):
    xf = x.astype(jnp.float32)
    y = xf * lax.rsqrt(jnp.mean(xf * xf, axis=-1, keepdims=True) + EPS)
    return (y * w.astype(jnp.float32)).astype(x.dtype)


def rope(x, pos):
    hd = x.shape[-1]
    half = hd // 2
    inv = ROPE_THETA ** (-jnp.arange(half, dtype=jnp.float32) / half)
    ang = pos.astype(jnp.float32)[:, None] * inv[None, :]
    cos, sin = jnp.cos(ang), jnp.sin(ang)
    xf = x.astype(jnp.float32)
    x1, x2 = xf[..., :half], xf[..., half:]
    return jnp.concatenate([x1 * cos - x2 * sin, x2 * cos + x1 * sin], axis=-1).astype(x.dtype)


def to_heads(t, h):
    b, s, _ = t.shape
    return t.reshape(b, s, h, -1).transpose(0, 2, 1, 3)


def from_heads(t):
    b, h, s, d = t.shape
    return t.transpose(0, 2, 1, 3).reshape(b, s, h * d)


def gla_chunked(q, k, v, log_a):
    B, H, S, dk = q.shape
    dv = v.shape[-1]
    C = GLA_CHUNK
    nc = S // C
    q, k, g = (t.reshape(B, H, nc, C, dk) for t in (q, k, log_a))
    v = v.reshape(B, H, nc, C, dv)
    b = jnp.cumsum(g, axis=3)
    b_last = b[:, :, :, -1:, :]
    q_d = q * jnp.exp(b) * (dk ** -0.5)
    k_in = k * jnp.exp(-b)
    k_st = k * jnp.exp(b_last - b)
    causal = jnp.tril(jnp.ones((C, C), dtype=bool))
    A = jnp.where(causal, jnp.einsum('bhnid,bhnjd->bhnij', q_d, k_in), 0.0)
    o_intra = jnp.einsum('bhnij,bhnje->bhnie', A, v)
    upd = jnp.einsum('bhncd,bhnce->bhnde', k_st, v)
    decay = jnp.exp(b_last[:, :, :, 0, :])

    def step(state, inp):
        dec, u = inp
        return dec[..., None] * state + u, state

    init = jnp.zeros((B, H, dk, dv), jnp.float32)
    _, s_before = lax.scan(step, init, (jnp.moveaxis(decay, 2, 0), jnp.moveaxis(upd, 2, 0)))
    s_before = jnp.moveaxis(s_before, 0, 2)
    o_inter = jnp.einsum('bhncd,bhnde->bhnce', q_d, s_before)
    return (o_intra + o_inter).reshape(B, H, S, dv)


def moba_attention(q, k, v):
    B, H, S, hd = q.shape
    Sp = ((S + MOBA_BLOCK - 1) // MOBA_BLOCK) * MOBA_BLOCK
    padw = ((0, 0), (0, 0), (0, Sp - S), (0, 0))
    q, k, v = (jnp.pad(t, padw) for t in (q, k, v))
    nb = Sp // MOBA_BLOCK
    K = min(MOBA_TOPK, nb)
    kb = k.reshape(B, H, nb, MOBA_BLOCK, hd)
    vb = v.reshape(B, H, nb, MOBA_BLOCK, hd)
    kbar = jnp.mean(kb.astype(jnp.float32), axis=3)
    scale = hd ** -0.5
    bi = jnp.arange(B)[:, None, None, None]
    hi = jnp.arange(H)[None, :, None, None]
    nq = Sp // MOBA_QCHUNK

    def one_chunk(c):
        q0 = c * MOBA_QCHUNK
        qc = lax.dynamic_slice_in_dim(q, q0, MOBA_QCHUNK, axis=2)
        blk = q0 // MOBA_BLOCK
        bscore = jnp.einsum('bhqd,bhnd->bhqn', qc.astype(jnp.float32), kbar)
        bscore = jnp.where(jnp.arange(nb) < blk, bscore, -jnp.inf)
        _, idx = lax.top_k(bscore, K)
        valid = jnp.arange(K) < blk
        kg = kb[bi, hi, idx]
        vg = vb[bi, hi, idx]
        s_sel = jnp.einsum('bhqd,bhqkld->bhqkl', qc, kg).astype(jnp.float32) * scale
        s_sel = jnp.where(valid[:, None], s_sel, -jnp.inf)
        k_own = lax.dynamic_index_in_dim(kb, blk, axis=2, keepdims=False)
        v_own = lax.dynamic_index_in_dim(vb, blk, axis=2, keepdims=False)
        s_own = jnp.einsum('bhqd,bhld->bhql', qc, k_own).astype(jnp.float32) * scale
        qpos = q0 + jnp.arange(MOBA_QCHUNK)
        kpos = blk * MOBA_BLOCK + jnp.arange(MOBA_BLOCK)
        s_own = jnp.where(kpos[None, :] <= qpos[:, None], s_own, -jnp.inf)
        logits = jnp.concatenate([s_sel.reshape(B, H, MOBA_QCHUNK, K * MOBA_BLOCK), s_own], axis=-1)
        p = jax.nn.softmax(logits, axis=-1).astype(v.dtype)
        p_sel = p[..., :K * MOBA_BLOCK].reshape(B, H, MOBA_QCHUNK, K, MOBA_BLOCK)
        p_own = p[..., K * MOBA_BLOCK:]
        return (jnp.einsum('bhqkl,bhqkld->bhqd', p_sel, vg)
                + jnp.einsum('bhql,bhld->bhqd', p_own, v_own))

    out = lax.map(one_chunk, jnp.arange(nq))
    out = jnp.moveaxis(out, 0, 2).reshape(B, H, Sp, hd)
    return out[:, :, :S]


def peer_ffn(xn, w_query, subkeys, u_tab, v_tab):
    B, S, D = xn.shape
    T = B * S
    xt = xn.reshape(T, D)
    q = (xt @ w_query).reshape(T, PEER_HEADS, 2, PEER_QDIM // 2)
    s = jnp.einsum('thpd,hpkd->thpk', q, subkeys).astype(jnp.float32)
    sv, si = lax.top_k(s, PEER_TOPK)
    cand = (sv[:, :, 0, :, None] + sv[:, :, 1, None, :]).reshape(T, PEER_HEADS, PEER_TOPK * PEER_TOPK)
    cand_id = (si[:, :, 0, :, None] * PEER_NKEYS + si[:, :, 1, None, :]).reshape(T, PEER_HEADS, PEER_TOPK * PEER_TOPK)
    top_s, pos = lax.top_k(cand, PEER_TOPK)
    expert = jnp.take_along_axis(cand_id, pos, axis=-1)
    gate = jax.nn.softmax(top_s, axis=-1)
    nchunk = T // PEER_TOKCHUNK

    def apply(args):
        xc, ec, gc = args
        ug = u_tab[ec]
        vg = v_tab[ec]
        act = jax.nn.gelu(jnp.einsum('thkd,td->thk', ug, xc).astype(jnp.float32), approximate=False)
        w = (gc * act).astype(vg.dtype)
        return jnp.einsum('thk,thkd->td', w, vg)

    out = lax.map(apply, (xt.reshape(nchunk, PEER_TOKCHUNK, D),
                          expert.reshape(nchunk, PEER_TOKCHUNK, PEER_HEADS, PEER_TOPK),
                          gate.reshape(nchunk, PEER_TOKCHUNK, PEER_HEADS, PEER_TOPK)))
    return out.reshape(B, S, D)


def setup_inputs(seed: int = 0) -> dict:
    key = jax.random.key(seed)
    ks = jax.random.split(key, 15)
    L, D = DEPTH, D_MODEL
    nrm = jax.random.normal
    return {
        "x": nrm(ks[0], (BATCH, SEQ, D), jnp.float32),
        "norm1_w": 1.0 + 0.02 * nrm(ks[1], (L, D), jnp.float32),
        "w_in": nrm(ks[2], (L, D, IN_WIDTH), jnp.float32) * D ** -0.5,
        "gla_w_alpha": nrm(ks[3], (L, GLA_RANK, GLA_HEADS * GLA_DK), jnp.float32) * GLA_RANK ** -0.5,
        "gla_b_alpha": 0.1 * nrm(ks[4], (L, GLA_HEADS * GLA_DK), jnp.float32),
        "gla_out_norm_w": 1.0 + 0.02 * nrm(ks[5], (L, GLA_DV), jnp.float32),
        "moba_q_norm_w": 1.0 + 0.02 * nrm(ks[6], (L, MOBA_HD), jnp.float32),
        "moba_k_norm_w": 1.0 + 0.02 * nrm(ks[7], (L, MOBA_HD), jnp.float32),
        "mix_scale": 1.0 + 0.02 * nrm(ks[8], (L, MIX_WIDTH), jnp.float32),
        "w_out": nrm(ks[9], (L, MIX_WIDTH, D), jnp.float32) * MIX_WIDTH ** -0.5,
        "norm2_w": 1.0 + 0.02 * nrm(ks[10], (L, D), jnp.float32),
        "peer_w_query": nrm(ks[11], (L, D, PEER_HEADS * PEER_QDIM), jnp.float32) * D ** -0.5,
        "peer_subkeys": nrm(ks[12], (L, PEER_HEADS, 2, PEER_NKEYS, PEER_QDIM // 2), jnp.float32) * (PEER_QDIM // 2) ** -0.5,
        "peer_u": nrm(ks[13], (L, PEER_N, D), jnp.float32) * D ** -0.5,
        "peer_v": nrm(ks[14], (L, PEER_N, D), jnp.float32) * (PEER_HEADS * PEER_TOPK) ** -0.5,
    }


def reference(x, norm1_w, w_in, gla_w_alpha, gla_b_alpha, gla_out_norm_w, moba_q_norm_w,
              moba_k_norm_w, mix_scale, w_out, norm2_w, peer_w_query, peer_subkeys, peer_u, peer_v):
    B, S, _ = x.shape
    pos = jnp.arange(S)
    offsets = []
    acc = 0
    for w in IN_SPLITS[:-1]:
        acc += w
        offsets.append(acc)
    for l in range(DEPTH):
        xn = rmsnorm(x, norm1_w[l])
        proj = xn @ w_in[l]
        gq, gk, gv, ggate, gr, mq, mk, mv = jnp.split(proj, offsets, axis=-1)
        log_a = jax.nn.log_sigmoid((gr @ gla_w_alpha[l] + gla_b_alpha[l]).astype(jnp.float32)) / GLA_TAU
        o_gla = gla_chunked(to_heads(gq, GLA_HEADS).astype(jnp.float32),
                            to_heads(gk, GLA_HEADS).astype(jnp.float32),
                            to_heads(gv, GLA_HEADS).astype(jnp.float32),
                            to_heads(log_a, GLA_HEADS))
        o_gla = from_heads(rmsnorm(o_gla, gla_out_norm_w[l])).astype(x.dtype)
        o_gla = o_gla * jax.nn.silu(ggate)
        qh = rope(rmsnorm(to_heads(mq, MOBA_HEADS), moba_q_norm_w[l]), pos)
        kh = rope(rmsnorm(to_heads(mk, MOBA_HEADS), moba_k_norm_w[l]), pos)
        o_moba = from_heads(moba_attention(qh, kh, to_heads(mv, MOBA_HEADS)))
        mixed = jnp.concatenate([o_gla, o_moba], axis=-1) * mix_scale[l]
        x = x + mixed @ w_out[l]
        x = x + peer_ffn(rmsnorm(x, norm2_w[l]), peer_w_query[l], peer_subkeys[l], peer_u[l], peer_v[l])
    return x
```

```python
import numpy as np
import ml_dtypes
from contextlib import ExitStack
import concourse.bass as bass
import concourse.mybir as mybir
from concourse.bass_utils import run_bass_kernel_spmd

F32 = mybir.dt.float32
BF16 = mybir.dt.bfloat16
AF = mybir.ActivationFunctionType
ALU = mybir.AluOpType
AX = mybir.AxisListType
EPS = 1e-6
NDS = 8
NT = 32
NOWN = 16
GQ0, GK0, GV0, GG0, GR0 = 0, 256, 512, 1024, 1536
NGC = 1552
NMC = 1536
BIGM = 30000.0


class _Stop(Exception):
    pass


class Sched:
    def __init__(self, nc, es):
        self.nc = nc
        self.eng = {'pe': nc.tensor, 'act': nc.scalar, 'dve': nc.vector,
                    'pool': nc.gpsimd, 'sp': nc.sync}
        self.semh = {}
        for k in self.eng:
            self.semh['E:' + k] = es.enter_context(nc.semaphore('s_' + k))
        self.cnt = {k: 0 for k in self.eng}
        self.seen = {k: {} for k in self.eng}
        self.dcnt = {}
        self.drr = {}
        for q in ('sp', 'pool', 'act'):
            self.dcnt[q] = [0] * NDS
            self.drr[q] = 0
            for i in range(NDS):
                self.semh['D:%s%d' % (q, i)] = es.enter_context(nc.semaphore('d_%s%d' % (q, i)))
        self.res = {}
        self.nins = 0
        self.nstage = 0
        self.stopped = False

    def _waits(self, e, reads, writes):
        need = {}

        def add(ev, kind):
            key, val, src = ev
            if src == e and e == 'pe':
                return
            if self.seen[e].get(key, 0) >= val:
                return
            if need.get(key, 0) < val:
                need[key] = val
        for r in reads:
            st = self.res.get(r)
            if st and st[0]:
                add(st[0], 'raw')
            if st and r.startswith('ps'):
                for ev in st[1].values():
                    add(ev, 'war')
        for w in writes:
            st = self.res.get(w)
            if st:
                if st[0]:
                    add(st[0], 'waw')
                for ev in st[1].values():
                    add(ev, 'war')
        for key, val in need.items():
            self.eng[e].wait_ge(self.semh[key], val)
            self.seen[e][key] = val

    def _record(self, ev, reads, writes):
        for x in writes:
            self.res[x] = [ev, {}]
        for x in reads:
            st = self.res.setdefault(x, [None, {}])
            st[1][ev[0]] = ev

    @staticmethod
    def _scan(kw):
        reads, writes = [], []
        for k, v in kw.items():
            if isinstance(v, bass.AP):
                if k in ('out', 'accum_out', 'ap'):
                    writes.append(v.name)
                else:
                    reads.append(v.name)
        return reads, writes

    def op(self, e, meth, **kw):
        if self.stopped:
            return
        xr = kw.pop('xr', ())
        reads, writes = self._scan(kw)
        reads += list(xr)
        self._waits(e, reads, writes)
        ins = getattr(self.eng[e], meth)(**kw)
        self.cnt[e] += 1
        ins.then_inc(self.semh['E:' + e], 1)
        self._record(('E:' + e, self.cnt[e], e), reads, writes)
        self.nins += 1

    def dma(self, q, out, in_):
        if self.stopped:
            return
        i = self.drr[q]
        self.drr[q] = (i + 1) % NDS
        key = 'D:%s%d' % (q, i)
        prev = self.dcnt[q][i] * 16
        if prev and self.seen[q].get(key, 0) < prev:
            self.eng[q].wait_ge(self.semh[key], prev)
            self.seen[q][key] = prev
        reads, writes = self._scan({'out': out, 'in_': in_})
        self._waits(q, reads, writes)
        self.eng[q].dma_start(out=out, in_=in_).then_inc(self.semh[key], 16)
        self.dcnt[q][i] += 1
        self._record((key, self.dcnt[q][i] * 16, 'dma'), reads, writes)
        self.nins += 1

    def barrier(self):
        if self.stopped:
            return
        for e in ('pe', 'act', 'dve', 'pool', 'sp'):
            for o in ('pe', 'act', 'dve', 'pool', 'sp'):
                key = 'E:' + o
                if o != e and self.cnt[o] and self.seen[e].get(key, 0) < self.cnt[o]:
                    self.eng[e].wait_ge(self.semh[key], self.cnt[o])
                    self.seen[e][key] = self.cnt[o]
            for q in ('sp', 'pool', 'act'):
                for i in range(NDS):
                    v = self.dcnt[q][i] * 16
                    key = 'D:%s%d' % (q, i)
                    if v and self.seen[e].get(key, 0) < v:
                        self.eng[e].wait_ge(self.semh[key], v)
                        self.seen[e][key] = v

    def finish(self):
        for q in ('sp', 'pool', 'act'):
            for i in range(NDS):
                v = self.dcnt[q][i] * 16
                key = 'D:%s%d' % (q, i)
                if v and self.seen['sp'].get(key, 0) < v:
                    self.eng['sp'].wait_ge(self.semh[key], v)
                    self.seen['sp'][key] = v
        for e in ('pe', 'act', 'dve', 'pool'):
            if self.cnt[e]:
                self.eng['sp'].wait_ge(self.semh['E:' + e], self.cnt[e])


def build(debug=False):
    nc = bass.Bass("TRN2", target_bir_lowering=False)

    def din(name, shape, dt=F32):
        return nc.dram_tensor(name, list(shape), dt, kind="ExternalInput").ap()
    xl = din("xl", [4096, 1024])
    w_in = din("w_in", [1024, 3088])
    n1w = din("n1w", [128, 8])
    wal = din("wal", [32, 256])
    gonw = din("gonw", [1, 128])
    mixs = din("mixs", [1, 1024])
    qnw = din("qnw", [1, 64])
    knw = din("knw", [1, 64])
    n2w = din("n2w", [1, 1024])
    w_out = din("w_out", [1024, 1024])
    wq = din("wq", [1024, 2048])
    subk = din("subk", [16, 128, 128])
    pu = din("pu", [16384, 1024])
    pv = din("pv", [16384, 1024])
    ropeC = din("ropeC", [4096, 64])
    ropeS = din("ropeS", [4096, 64])
    c_tri = din("c_tri", [128, 128])
    c_us = din("c_us", [128, 128])
    c_ind = din("c_ind", [128, 2])
    c_m2 = din("c_m2", [128, 128])
    c_idb = din("c_idb", [128, 128], BF16)
    c_idf = din("c_idf", [128, 128])
    c_cm = din("c_cm", [128, 512], BF16)
    c_boh = din("c_boh", [16, 2048], BF16)
    c_valid = din("c_valid", [128, 256])
    c_negv = din("c_negv", [128, 256])
    c_own = din("c_own", [128, 256])
    out = nc.dram_tensor("out", [2048, 1024], F32, kind="ExternalOutput").ap()
    dbg = None
    if debug:
        dbg = nc.dram_tensor("dbg", [1024, 2048], F32, kind="ExternalOutput").ap()

    with ExitStack() as es:
      try:
        S = Sched(nc, es)

        def chk(tag):
            if debug == tag:
                S.stopped = True

        def sb(name, shape, dt=F32):
            return es.enter_context(nc.sbuf_tensor(name, list(shape), dt))
        PS = [es.enter_context(nc.psum_tensor("ps%d" % i, [128, 512], F32)) for i in range(8)]

        idb = sb("idb", [128, 128], BF16); idf = sb("idf", [128, 128])
        mixT = sb("mixT", [128, 8, 2048], BF16)
        es_c = ExitStack()
        es.enter_context(es_c)
        sb_persist = sb

        def sb(name, shape, dt=F32):
            return es_c.enter_context(nc.sbuf_tensor(name, list(shape), dt))
        tri = sb("tri", [128, 128]); us = sb("us", [128, 128]); ind = sb("ind", [128, 2])
        m2 = sb("m2", [128, 128])
        cm = sb("cm", [128, 512], BF16); boh = sb("boh", [16, 2048], BF16)
        validm = sb("validm", [128, 256]); negv = sb("negv", [128, 256]); ownm = sb("ownm", [128, 256])
        n1ws = sb("n1ws", [128, 8]); wals = sb("wals", [32, 256])
        ones256 = sb("ones256", [128, 1])
        for t_, d_ in ((tri, c_tri), (us, c_us), (ind, c_ind), (m2, c_m2), (idb, c_idb), (idf, c_idf),
                       (cm, c_cm), (boh, c_boh), (validm, c_valid), (negv, c_negv), (ownm, c_own),
                       (n1ws, n1w), (wals, wal)):
            S.dma('sp', out=t_[:], in_=d_)
        S.op('dve', 'memset', ap=ones256[:], constant=1.0 / 256.0, xr=())
        S.res['ones256'] = [('E:dve', S.cnt['dve'], 'dve'), {}]
        cwg = sb("cwg", [128, 512]); mixm = sb("mixm", [128, 512]); tmpb = sb("tmpb", [128, 512])
        qwb = sb("qwb", [128, 64]); kwb = sb("kwb", [128, 64]); n2wb = sb("n2wb", [128, 1024])
        S.dma('sp', out=cwg[:], in_=mixs[0:1, 0:512].to_broadcast([128, 512]))
        S.dma('sp', out=mixm[:], in_=mixs[0:1, 512:1024].to_broadcast([128, 512]))
        for h in range(4):
            S.dma('sp', out=tmpb[:, h * 128:(h + 1) * 128], in_=gonw[0:1, :].to_broadcast([128, 128]))
        S.dma('sp', out=qwb[:], in_=qnw[0:1, :].to_broadcast([128, 64]))
        S.dma('sp', out=kwb[:], in_=knw[0:1, :].to_broadcast([128, 64]))
        S.dma('sp', out=n2wb[:], in_=n2w[0:1, :].to_broadcast([128, 1024]))
        S.op('dve', 'tensor_tensor', out=cwg[:], in0=cwg[:], in1=tmpb[:], op=ALU.mult)

        es_att = ExitStack()
        es.enter_context(es_att)

        def sba(name, shape, dt=F32):
            return es_att.enter_context(nc.sbuf_tensor(name, list(shape), dt))
        xts = [sb("xt0", [128, 1024]), sb("xt1", [128, 1024])]
        sq = sb("sq", [128, 1024])
        ss = sb("ss", [128, 1]); rs = sb("rs", [128, 1])
        xn = sb("xn", [128, 1024], BF16)
        xnT = sb("xnT", [128, 8, 128], BF16)

        xs_d = nc.dram_tensor("xs_d", [NT, 128, 1024], BF16, kind="Internal").ap()

        def prep(t, save=False):
            xt = xts[t % 2]
            S.dma('sp', out=xt[:], in_=xl[t * 128:(t + 1) * 128, :])
            S.op('act', 'activation', out=sq[:], in_=xt[:], func=AF.Square)
            S.op('dve', 'reduce_sum', out=ss[:], in_=sq[:], axis=AX.X)
            S.op('dve', 'tensor_scalar', out=rs[:], in0=ss[:], scalar1=1.0 / 1024.0, scalar2=EPS,
                 op0=ALU.mult, op1=ALU.add)
            S.op('act', 'activation', out=rs[:], in_=rs[:], func=AF.Ln)
            S.op('act', 'activation', out=rs[:], in_=rs[:], func=AF.Exp, scale=-0.5)
            S.op('dve', 'tensor_scalar', out=xn[:], in0=xt[:], scalar1=rs[:, 0:1], scalar2=None, op0=ALU.mult)
            pb = PS[0][:].bitcast(BF16)
            for kc in range(8):
                S.op('pe', 'transpose', out=pb[:, kc * 128:(kc + 1) * 128], in_=xn[:, kc * 128:(kc + 1) * 128],
                     identity=idb[:])
            S.op('act', 'activation', out=xnT[:].rearrange("p a b -> p (a b)"), in_=pb[:, 0:1024], func=AF.Copy)
            if save:
                S.dma('pool', out=xs_d[t, :, :], in_=xnT[:].rearrange("p a b -> p (a b)"))

        def load_w(dst, src, c0, ncols, scale_col):
            S.nstage += 1
            with ExitStack() as el:
                stage = [el.enter_context(nc.sbuf_tensor("stg%d_%d" % (S.nstage, i), [128, ncols], F32))
                         for i in range(2)]
                for kc in range(8):
                    st = stage[kc % 2]
                    S.dma('sp', out=st[:, 0:ncols], in_=src[kc * 128:(kc + 1) * 128, c0:c0 + ncols])
                    if scale_col is not None:
                        S.op('dve', 'tensor_scalar', out=dst[:, kc, :], in0=st[:, 0:ncols],
                             scalar1=scale_col[:, kc:kc + 1], scalar2=None, op0=ALU.mult)
                    elif kc % 2 == 0:
                        S.op('dve', 'tensor_copy', out=dst[:, kc, :], in_=st[:, 0:ncols])
                    else:
                        S.op('act', 'activation', out=dst[:, kc, :], in_=st[:, 0:ncols], func=AF.Copy)
                S.barrier()

        with ExitStack() as eg:
            def sg_(name, shape, dt=F32):
                return eg.enter_context(nc.sbuf_tensor(name, list(shape), dt))
            wg = sg_("wg", [128, 8, NGC], BF16)
            load_w(wg, w_in, 0, NGC, n1ws)
            grT = sg_("grT", [32, 128])
            S.op('dve', 'memset', ap=grT[:], constant=1.0)
            S.res['grT'] = [('E:dve', S.cnt['dve'], 'dve'), {}]
            gp = sg_("gp", [128, 256]); er = sg_("er", [128, 256])
            kst = sg_("kst", [128, 256], BF16); vb = sg_("vb", [128, 512], BF16)
            dec = sg_("dec", [64, 8])
            Eb = sg_("Eb", [64, 512]); Enb = sg_("Enb", [64, 512])
            qd = sg_("qd", [64, 512], BF16); kin = sg_("kin", [64, 512], BF16)
            Q2 = sg_("Q2", [64, 4, 2, 128], BF16)
            S.op('dve', 'memset', ap=Q2[:], constant=0.0)
            S.res['Q2'] = [('E:dve', S.cnt['dve'], 'dve'), {}]
            ATs = sg_("ATs", [128, 4, 128], BF16)
            sgl = sg_("sgl", [128, 512])
            S32 = sg_("S32", [64, 4, 128])
            S.op('dve', 'memset', ap=S32[:], constant=0.0)
            S.res['S32'] = [('E:dve', S.cnt['dve'], 'dve'), {}]
            Sb0 = sg_("Sb0", [64, 4, 128], BF16); Sb1 = sg_("Sb1", [64, 4, 128], BF16)
            Sb2 = sg_("Sb2", [64, 4, 128], BF16)
            Sbs = [Sb0, Sb1, Sb2]
            S.op('dve', 'memset', ap=Sb0[:], constant=0.0)
            S.res['Sb0'] = [('E:dve', S.cnt['dve'], 'dve'), {}]
            so = sg_("so", [128, 4]); on = sg_("on", [128, 512]); mg = sg_("mg", [128, 512], BF16)

            sqo = sg_("sqo", [128, 512])
            prep(0, save=True)
            for t in range(NT):
                own = t >= NOWN
                ti = t - NOWN
                p_gk, p_gv, p_gr = PS[1], PS[2], PS[3]
                for kc in range(8):
                    S.op('pe', 'matmul', out=p_gk[:, 0:256], lhsT=xnT[:, kc, :], rhs=wg[:, kc, GK0:GK0 + 256],
                         start=(kc == 0), stop=(kc == 7))
                for kc in range(8):
                    S.op('pe', 'matmul', out=p_gv[:, 0:512], lhsT=xnT[:, kc, :], rhs=wg[:, kc, GV0:GV0 + 512],
                         start=(kc == 0), stop=(kc == 7))
                for kc in range(8):
                    S.op('pe', 'matmul', out=p_gr[0:16, 0:128], lhsT=wg[:, kc, GR0:GR0 + 16], rhs=xnT[:, kc, :],
                         start=(kc == 0), stop=(kc == 7))
                S.op('dve', 'tensor_copy', out=grT[0:16, :], in_=p_gr[0:16, 0:128])
                S.op('pe', 'matmul', out=p_gr[:, 256:512], lhsT=grT[:, :], rhs=wals[:, :], start=True, stop=True)
                S.op('act', 'activation', out=gp[:], in_=p_gr[:, 256:512], func=AF.Exp, scale=-1.0)
                S.op('dve', 'tensor_scalar', out=gp[:], in0=gp[:], scalar1=1.0, scalar2=None, op0=ALU.add)
                S.op('act', 'activation', out=gp[:], in_=gp[:], func=AF.Ln)
                p_r = PS[4]
                S.op('pe', 'matmul', out=p_r[:, 0:256], lhsT=us[:], rhs=gp[:], start=True, stop=True)
                for h in range(4):
                    S.op('pe', 'matmul', out=p_r[0:64, 256 + h * 2:256 + h * 2 + 2], lhsT=gp[:, h * 64:(h + 1) * 64],
                         rhs=ind[:], start=True, stop=True)
                S.op('act', 'activation', out=er[:], in_=p_r[:, 0:256], func=AF.Exp)
                S.op('act', 'activation', out=dec[:], in_=p_r[0:64, 256:264], func=AF.Exp)
                S.op('dve', 'tensor_tensor', out=kst[:], in0=p_gk[:, 0:256], in1=er[:], op=ALU.mult)
                S.op('act', 'activation', out=vb[:], in_=p_gv[:, 0:512], func=AF.Copy)
                SbE, SbM, SbN = Sbs[(2 * t) % 3], Sbs[(2 * t + 1) % 3], Sbs[(2 * t + 2) % 3]
                p_u = PS[6]
                S32v = S32[:]
                for cj in range(2):
                    for h in range(4):
                        S.op('pe', 'matmul', out=p_u[0:64, h * 128:(h + 1) * 128],
                             lhsT=kst[cj * 64:(cj + 1) * 64, h * 64:(h + 1) * 64],
                             rhs=vb[cj * 64:(cj + 1) * 64, h * 128:(h + 1) * 128], start=True, stop=True)
                    decv = dec[:].rearrange("p (h j) -> p h j", j=2)[:, :, cj:cj + 1].to_broadcast([64, 4, 128])
                    S.op('dve', 'tensor_tensor', out=S32v, in0=S32v, in1=decv, op=ALU.mult)
                    S.op('dve', 'tensor_tensor', out=S32v, in0=S32v,
                         in1=p_u[0:64, :].rearrange("p (h c) -> p h c", h=4), op=ALU.add)
                    if own or (cj == 1 and t == NOWN - 1):
                        S.op('act', 'activation', out=(SbM if cj == 0 else SbN)[:], in_=S32v, func=AF.Copy)
                if not own and t + 1 < NT:
                    prep(t + 1, save=True)
                if own:
                    p_q, p_k, p_b, p_gg = PS[5], PS[6], PS[7], PS[1]
                    for h in range(4):
                        for kc in range(8):
                            S.op('pe', 'matmul', out=p_q[0:64, h * 128:(h + 1) * 128],
                                 lhsT=wg[:, kc, GQ0 + h * 64:GQ0 + (h + 1) * 64], rhs=xnT[:, kc, :],
                                 start=(kc == 0), stop=(kc == 7))
                    for h in range(4):
                        for kc in range(8):
                            S.op('pe', 'matmul', out=p_k[0:64, h * 128:(h + 1) * 128],
                                 lhsT=wg[:, kc, GK0 + h * 64:GK0 + (h + 1) * 64], rhs=xnT[:, kc, :],
                                 start=(kc == 0), stop=(kc == 7))
                    for h in range(4):
                        S.op('pe', 'matmul', out=p_b[0:64, h * 128:(h + 1) * 128], lhsT=gp[:, h * 64:(h + 1) * 64],
                             rhs=tri[:], start=True, stop=True)
                    S.op('act', 'activation', out=Eb[:], in_=p_b[0:64, :], func=AF.Exp)
                    S.op('act', 'activation', out=Enb[:], in_=p_b[0:64, :], func=AF.Exp, scale=-1.0)
                    S.op('dve', 'scalar_tensor_tensor', out=qd[:], in0=p_q[0:64, :], scalar=0.125, in1=Eb[:],
                         op0=ALU.mult, op1=ALU.mult)
                    qdv = qd[:].rearrange("p (h c) -> p h c", h=4)
                    S.op('pool', 'tensor_copy', out=Q2[:, :, 0, 0:64], in_=qdv[:, :, 0:64])
                    S.op('pool', 'tensor_copy', out=Q2[:, :, 1, 64:128], in_=qdv[:, :, 64:128])
                    S.op('dve', 'tensor_tensor', out=kin[:], in0=p_k[0:64, :], in1=Enb[:], op=ALU.mult)
                    for kc in range(8):
                        S.op('pe', 'matmul', out=p_gg[:, 0:512], lhsT=xnT[:, kc, :], rhs=wg[:, kc, GG0:GG0 + 512],
                             start=(kc == 0), stop=(kc == 7))
                    S.op('act', 'activation', out=sgl[:], in_=p_gg[:, 0:512], func=AF.Silu)
                    if t + 1 < NT:
                        prep(t + 1, save=True)
                    p_at = PS[5]
                    for h in range(4):
                        S.op('pe', 'matmul', out=p_at[:, h * 128:(h + 1) * 128], lhsT=kin[:, h * 128:(h + 1) * 128],
                             rhs=qd[:, h * 128:(h + 1) * 128], start=True, stop=True)
                    S.op('dve', 'tensor_tensor', out=ATs[:], in0=p_at[:].rearrange("p (h c) -> p h c", h=4),
                         in1=m2[:].unsqueeze(1).to_broadcast([128, 4, 128]), op=ALU.mult)
                if own:
                    p_o = PS[7]
                    for h in range(4):
                        S.op('pe', 'matmul', out=p_o[:, h * 128:(h + 1) * 128], lhsT=ATs[:, h, :],
                             rhs=vb[:, h * 128:(h + 1) * 128], start=True, stop=False)
                        S.op('pe', 'matmul', out=p_o[:, h * 128:(h + 1) * 128], lhsT=Q2[:, h, 0, :],
                             rhs=SbE[:, h, :], start=False, stop=False)
                        S.op('pe', 'matmul', out=p_o[:, h * 128:(h + 1) * 128], lhsT=Q2[:, h, 1, :],
                             rhs=SbM[:, h, :], start=False, stop=True)
                if own:
                    p_o = PS[7]
                    S.op('act', 'activation', out=sqo[:], in_=p_o[:, 0:512], func=AF.Square)
                    S.op('dve', 'tensor_reduce', out=so[:], in_=sqo[:].rearrange("p (h c) -> p h c", h=4),
                         axis=AX.X, op=ALU.add)
                    S.op('dve', 'tensor_scalar', out=so[:], in0=so[:], scalar1=1.0 / 128.0, scalar2=EPS,
                         op0=ALU.mult, op1=ALU.add)
                    S.op('act', 'activation', out=so[:], in_=so[:], func=AF.Ln)
                    S.op('act', 'activation', out=so[:], in_=so[:], func=AF.Exp, scale=-0.5)
                    S.op('dve', 'tensor_tensor', out=on[:].rearrange("p (h c) -> p h c", h=4),
                         in0=p_o[:, 0:512].rearrange("p (h c) -> p h c", h=4),
                         in1=so[:].unsqueeze(2).to_broadcast([128, 4, 128]), op=ALU.mult)
                    S.op('pool', 'tensor_tensor', out=on[:], in0=on[:], in1=cwg[:], op=ALU.mult)
                    S.op('dve', 'tensor_tensor', out=mg[:], in0=on[:], in1=sgl[:], op=ALU.mult)
                    pb = PS[0][:].bitcast(BF16)
                    for c in range(4):
                        S.op('pe', 'transpose', out=pb[:, c * 128:(c + 1) * 128], in_=mg[:, c * 128:(c + 1) * 128],
                             identity=idb[:])
                    S.op('act', 'activation', out=mixT[:, 0:4, ti * 128:(ti + 1) * 128],
                         in_=pb[:, 0:512].rearrange("p (c k) -> p c k", c=4), func=AF.Copy)
            S.barrier()

        if debug == 'gla':
            dst = sb("dbgt", [128, 2048])
            for c in range(4):
                S.op('dve', 'tensor_copy', out=dst[:], in_=mixT[:, c, :])
                S.dma('sp', out=dbg[c * 128:(c + 1) * 128, :], in_=dst[:])
            S.finish()
            return nc
        es_m = ExitStack()
        es.enter_context(es_m)
        KT = es_m.enter_context(nc.sbuf_tensor("KT", [128, 4, 4096], BF16))
        Vaug = es_m.enter_context(nc.sbuf_tensor("Vaug", [128, 32, 8, 65], BF16))
        QTb = es_m.enter_context(nc.sbuf_tensor("QTb", [128, 4, 2048], BF16))
        negm = es_m.enter_context(nc.sbuf_tensor("negm", [128, 16, 8, 16], BF16))
        kbarT = es_m.enter_context(nc.sbuf_tensor("kbarT", [128, 4, 16], F32))
        S.op('pool', 'memset', ap=Vaug[:, :, :, 64:65], constant=1.0)
        S.op('pool', 'memset', ap=kbarT[:], constant=0.0)
        with ExitStack() as em:
            def sm_(name, shape, dt=F32):
                return em.enter_context(nc.sbuf_tensor(name, list(shape), dt))
            wm = sm_("wm", [128, 8, NMC], BF16)
            load_w(wm, w_in, NGC, NMC, n1ws)
            chk('m0')
            xnTm = [sm_("xnTm0", [128, 8, 128], BF16), sm_("xnTm1", [128, 8, 128], BF16)]
            rC = [sm_("rC0", [128, 64]), sm_("rC1", [128, 64])]
            rS = [sm_("rS0", [128, 64]), sm_("rS1", [128, 64])]
            ssq = sm_("ssq", [128, 8]); kn = sm_("kn", [128, 512]); t1 = sm_("t1", [128, 512])
            t2 = sm_("t2", [128, 512]); kr = sm_("kr", [128, 512]); qr = sm_("qr", [128, 512])
            qT32 = sm_("qT32", [128, 4, 128]); kb_tmp = sm_("kb_tmp", [128, 4])
            bsm = sm_("bsm", [128, 8, 16]); mx = sm_("mx", [128, 8, 8]); sel = sm_("sel", [128, 8, 16])

            ssq2 = sm_("ssq2", [128, 8])

            def nr_steps(ps, wb_, cc, sn, dst, sq_, ssq_, kn_, t1_, t2_):
                knv = kn_.rearrange("p (h c) -> p h c", h=8)
                t1v = t1_.rearrange("p (h c) -> p h c", h=8)
                t2v = t2_.rearrange("p (h c) -> p h c", h=8)
                return [
                    lambda: S.op('act', 'activation', out=sq_, in_=ps[:, 0:512], func=AF.Square),
                    lambda: S.op('dve', 'tensor_reduce', out=ssq_[:], in_=sq_.rearrange("p (h c) -> p h c", h=8),
                                 axis=AX.X, op=ALU.add),
                    lambda: S.op('dve', 'tensor_scalar', out=ssq_[:], in0=ssq_[:], scalar1=1.0 / 64.0, scalar2=EPS,
                                 op0=ALU.mult, op1=ALU.add),
                    lambda: S.op('act', 'activation', out=ssq_[:], in_=ssq_[:], func=AF.Ln),
                    lambda: S.op('act', 'activation', out=ssq_[:], in_=ssq_[:], func=AF.Exp, scale=-0.5),
                    lambda: S.op('dve', 'tensor_tensor', out=knv, in0=ps[:, 0:512].rearrange("p (h c) -> p h c", h=8),
                                 in1=ssq_[:].unsqueeze(2).to_broadcast([128, 8, 64]), op=ALU.mult),
                    lambda: S.op('pool', 'tensor_tensor', out=knv, in0=knv,
                                 in1=wb_[:].unsqueeze(1).to_broadcast([128, 8, 64]), op=ALU.mult),
                    lambda: S.op('dve', 'tensor_tensor', out=t1v, in0=knv,
                                 in1=cc[:].unsqueeze(1).to_broadcast([128, 8, 64]), op=ALU.mult),
                    lambda: S.op('pool', 'tensor_tensor', out=t2v[:, :, 0:32], in0=knv[:, :, 32:64],
                                 in1=sn[:, 0:32].unsqueeze(1).to_broadcast([128, 8, 32]), op=ALU.mult),
                    lambda: S.op('pool', 'tensor_tensor', out=t2v[:, :, 32:64], in0=knv[:, :, 0:32],
                                 in1=sn[:, 32:64].unsqueeze(1).to_broadcast([128, 8, 32]), op=ALU.mult),
                    lambda: S.op('dve', 'tensor_tensor', out=dst[:], in0=t1_, in1=t2_, op=ALU.add),
                ]

            def norm_rope_kq(t, own, cc, sn):
                ka = nr_steps(PS[1], kwb, cc, sn, kr, sq[:, 0:512], ssq, kn[:], t1[:], t2[:])
                if not own:
                    for f in ka:
                        f()
                    return
                qa = nr_steps(PS[3], qwb, cc, sn, qr, xts[1][:, 0:512], ssq2, xts[0][:, 0:512], xts[0][:, 512:1024],
                              xts[1][:, 512:1024])
                for fk, fq in zip(ka, qa):
                    fk()
                    fq()

            for t in range(NT):
                own = t >= NOWN
                ti = t - NOWN
                nblk = t // 2
                if t == 16:
                    chk('n0')
                if t == 0:
                    S.dma('sp', out=xnTm[0][:].rearrange("p a b -> p (a b)"), in_=xs_d[0, :, :])
                if t + 1 < NT:
                    S.dma('sp', out=xnTm[(t + 1) % 2][:].rearrange("p a b -> p (a b)"), in_=xs_d[t + 1, :, :])
                xnT_ = xnTm[t % 2]
                cc, sn = rC[t % 2], rS[t % 2]
                S.dma('sp', out=cc[:], in_=ropeC[t * 128:(t + 1) * 128, :])
                S.dma('sp', out=sn[:], in_=ropeS[t * 128:(t + 1) * 128, :])
                chk('m1')
                p_k, p_v, p_q = PS[1], PS[2], PS[3]
                for kc in range(8):
                    S.op('pe', 'matmul', out=p_k[:, 0:512], lhsT=xnT_[:, kc, :], rhs=wm[:, kc, 512:1024],
                         start=(kc == 0), stop=(kc == 7))
                for kc in range(8):
                    S.op('pe', 'matmul', out=p_v[:, 0:512], lhsT=xnT_[:, kc, :], rhs=wm[:, kc, 1024:1536],
                         start=(kc == 0), stop=(kc == 7))
                if own:
                    for kc in range(8):
                        S.op('pe', 'matmul', out=p_q[:, 0:512], lhsT=xnT_[:, kc, :], rhs=wm[:, kc, 0:512],
                             start=(kc == 0), stop=(kc == 7))
                S.op('act', 'activation', out=Vaug[:, t, :, 0:64],
                     in_=p_v[:, 0:512].rearrange("p (h c) -> p h c", h=8), func=AF.Copy)
                chk('m3')
                norm_rope_kq(t, own, cc, sn)
                chk('m4')
                p_t = PS[4]
                for m in range(4):
                    S.op('pe', 'transpose', out=p_t[:, m * 128:(m + 1) * 128], in_=kr[:, m * 128:(m + 1) * 128],
                         identity=idf[:])
                S.op('act', 'activation', out=KT[:, :, t * 128:(t + 1) * 128],
                     in_=p_t[:, 0:512].rearrange("p (m c) -> p m c", m=4), func=AF.Copy)
                chk('m5')
                p_kb = PS[5]
                for m in range(4):
                    c_ = (t % 2) * 4 + m
                    S.op('pe', 'matmul', out=p_kb[:, c_:c_ + 1], lhsT=kr[:, m * 128:(m + 1) * 128],
                         rhs=ones256[:, 0:1], start=True, stop=True)
                if t % 2 == 1:
                    S.op('dve', 'tensor_copy', out=kb_tmp[:], in_=p_kb[:, 4:8])
                    S.op('dve', 'tensor_tensor', out=kbarT[:, :, nblk], in0=p_kb[:, 0:4], in1=kb_tmp[:], op=ALU.add)
                    chk('m7')
                if own:
                    chk('n1')
                    p_t2 = PS[6]
                    for m in range(4):
                        S.op('pe', 'transpose', out=p_t2[:, m * 128:(m + 1) * 128], in_=qr[:, m * 128:(m + 1) * 128],
                             identity=idf[:])
                    S.op('act', 'activation', out=QTb[:, :, ti * 128:(ti + 1) * 128],
                         in_=p_t2[:, 0:512].rearrange("p (m c) -> p m c", m=4), func=AF.Copy)
                    chk('n2')
                    S.op('act', 'activation', out=qT32[:].rearrange("p m c -> p (m c)"), in_=p_t2[:, 0:512],
                         func=AF.Copy)
                    chk('m8')
                    p_bs = PS[7]
                    for h in range(8):
                        m, off = h // 2, (h % 2) * 64
                        S.op('pe', 'matmul', out=p_bs[:, h * 16:(h + 1) * 16], lhsT=qT32[off:off + 64, m, :],
                             rhs=kbarT[off:off + 64, m, :], start=True, stop=True)
                    vb_ = validm[:, ti * 16:(ti + 1) * 16].unsqueeze(1).to_broadcast([128, 8, 16])
                    nb_ = negv[:, ti * 16:(ti + 1) * 16].unsqueeze(1).to_broadcast([128, 8, 16])
                    ob_ = ownm[:, ti * 16:(ti + 1) * 16].unsqueeze(1).to_broadcast([128, 8, 16])
                    S.op('dve', 'tensor_tensor', out=bsm[:], in0=p_bs[:, 0:128].rearrange("p (h n) -> p h n", h=8),
                         in1=vb_, op=ALU.mult)
                    S.op('dve', 'tensor_tensor', out=bsm[:], in0=bsm[:], in1=nb_, op=ALU.add)
                    chk('m9')
                    for h in range(8):
                        S.op('dve', 'max', out=mx[:, h, :], in_=bsm[:, h, :])
                    S.op('dve', 'tensor_tensor', out=sel[:], in0=bsm[:], in1=mx[:, :, 2:3].to_broadcast([128, 8, 16]),
                         op=ALU.is_ge)
                    S.op('dve', 'tensor_tensor', out=sel[:], in0=sel[:], in1=vb_, op=ALU.mult)
                    S.op('dve', 'tensor_tensor', out=sel[:], in0=sel[:], in1=ob_, op=ALU.add)
                    S.op('dve', 'tensor_scalar', out=negm[:, ti, :, :], in0=sel[:], scalar1=-1.0, scalar2=BIGM,
                         op0=ALU.add, op1=ALU.mult)
                    if debug == 'sel' and ti == 8:
                        S.dma('sp', out=dbg[0:128, 0:128], in_=bsm[:].rearrange("p h n -> p (h n)"))
                        S.dma('sp', out=dbg[0:128, 128:192], in_=mx[:].rearrange("p h n -> p (h n)"))
                        S.dma('sp', out=dbg[0:128, 256:384], in_=sel[:].rearrange("p h n -> p (h n)"))
                        S.dma('sp', out=dbg[0:128, 384:448], in_=kbarT[:].rearrange("p h n -> p (h n)"))
                        S.stopped = True
            S.barrier()

        if debug == 'pm':
            dst = es_m.enter_context(nc.sbuf_tensor("dbgt", [128, 2048], F32))
            for c in range(4):
                S.op('dve', 'tensor_copy', out=dst[:], in_=KT[:, c, 2048:4096])
                S.dma('sp', out=dbg[c * 128:(c + 1) * 128, :], in_=dst[:])
            for c in range(4):
                S.op('dve', 'tensor_copy', out=dst[:], in_=QTb[:, c, :])
                S.dma('sp', out=dbg[(4 + c) * 128:(5 + c) * 128, :], in_=dst[:])
            S.finish()
            es_m.close()
            return nc
        with ExitStack() as ea:
            def sa_(name, shape, dt=F32):
                return ea.enter_context(nc.sbuf_tensor(name, list(shape), dt))
            negT = sa_("negT", [16, 8, 512], BF16)
            pts = [sa_("pt%d" % i, [128, 512], BF16) for i in range(3)]
            rd = sa_("rd", [128, 8]); mo = sa_("mo", [128, 512]); mmb = sa_("mmb", [128, 512], BF16)
            nst = 0
            for qc in range(8):
                b = 8 + qc
                p_n = PS[7]
                pnb = p_n[:].bitcast(BF16)
                for g in range(2):
                    for hh in range(4):
                        h = g * 4 + hh
                        for qt in range(2):
                            S.op('pe', 'transpose', out=pnb[0:16, hh * 256 + qt * 128:hh * 256 + (qt + 1) * 128],
                                 in_=negm[:, 2 * qc + qt, h, :], identity=idb[:])
                    src = pnb[0:16, 0:1024].rearrange("p (h q) -> p h q", h=4)
                    S.op('dve', 'tensor_copy', out=negT[:, g * 4:(g + 1) * 4, 0:256], in_=src)
                    S.op('act', 'activation', out=negT[:, g * 4:(g + 1) * 4, 256:512], in_=src, func=AF.Copy)
                for h in range(8):
                    m, off = h // 2, (h % 2) * 64

                    def st_s(n):
                        p_s = PS[4 + (nst + n) % 3]
                        pt = pts[(nst + n) % 3]
                        S.op('pe', 'matmul', out=p_s[:, 0:512], lhsT=boh[0:16, n * 128:(n + 1) * 128],
                             rhs=negT[0:16, h, :], start=True, stop=False)
                        for a in range(2):
                            kt = 2 * n + a
                            S.op('pe', 'matmul', out=p_s[:, a * 256:(a + 1) * 256],
                                 lhsT=KT[off:off + 64, m, kt * 128:(kt + 1) * 128],
                                 rhs=QTb[off:off + 64, m, qc * 256:(qc + 1) * 256], start=False, stop=(a == 1))
                        S.op('act', 'activation', out=pt[:], in_=p_s[:, 0:512], func=AF.Exp, scale=0.125)
                        if n == b:
                            S.op('dve', 'tensor_tensor', out=pt[:], in0=pt[:], in1=cm[:], op=ALU.mult)

                    def st_pv(n):
                        pt = pts[(nst + n) % 3]
                        for a in range(2):
                            kt = 2 * n + a
                            for qt in range(2):
                                p_o = PS[qt * 2 + h // 4]
                                S.op('pe', 'matmul', out=p_o[:, (h % 4) * 65:(h % 4) * 65 + 65],
                                     lhsT=pt[:, a * 256 + qt * 128:a * 256 + (qt + 1) * 128], rhs=Vaug[:, kt, h, :],
                                     start=(n == 0 and a == 0), stop=(n == b and a == 1))
                    st_s(0)
                    for n in range(b + 1):
                        if n + 1 <= b:
                            st_s(n + 1)
                        st_pv(n)
                    nst += b + 1
                for qt in range(2):
                    ti = 2 * qc + qt
                    for g in range(2):
                        pov = PS[qt * 2 + g][:, 0:260].rearrange("p (h c) -> p h c", h=4)
                        S.op('dve', 'reciprocal', out=rd[:, g * 4:(g + 1) * 4], in_=pov[:, :, 64])
                        S.op('dve', 'tensor_tensor',
                             out=mo[:, g * 256:(g + 1) * 256].rearrange("p (h c) -> p h c", h=4),
                             in0=pov[:, :, 0:64],
                             in1=rd[:, g * 4:(g + 1) * 4].unsqueeze(2).to_broadcast([128, 4, 64]), op=ALU.mult)
                    S.op('pool', 'tensor_tensor', out=mmb[:], in0=mo[:], in1=mixm[:], op=ALU.mult)
                    pb = PS[0][:].bitcast(BF16)
                    for c in range(4):
                        S.op('pe', 'transpose', out=pb[:, c * 128:(c + 1) * 128], in_=mmb[:, c * 128:(c + 1) * 128],
                             identity=idb[:])
                    S.op('act', 'activation', out=mixT[:, 4:8, ti * 128:(ti + 1) * 128],
                         in_=pb[:, 0:512].rearrange("p (c k) -> p c k", c=4), func=AF.Copy)
            S.barrier()
        es_m.close()
        if debug == 'mix':
            dst = sb("dbgt", [128, 2048])
            for c in range(8):
                S.op('dve', 'tensor_copy', out=dst[:], in_=mixT[:, c, :])
                S.dma('sp', out=dbg[c * 128:(c + 1) * 128, :], in_=dst[:])
            S.finish()
            return nc
        yT = mixT
        with ExitStack() as e3:
            woutb = e3.enter_context(nc.sbuf_tensor("woutb", [128, 8, 1024], BF16))
            x1s = e3.enter_context(nc.sbuf_tensor("x1s", [128, 1024], F32))
            yb = e3.enter_context(nc.sbuf_tensor("yb", [128, 1024], BF16))
            load_w(woutb, w_out, 0, 1024, None)
            for ti in range(NOWN):
                t = NOWN + ti
                xt = xts[ti % 2]
                S.dma('sp', out=xt[:], in_=xl[t * 128:(t + 1) * 128, :])
                for half in range(2):
                    for c in range(8):
                        S.op('pe', 'matmul', out=PS[1 + half][:, 0:512], lhsT=mixT[:, c, ti * 128:(ti + 1) * 128],
                             rhs=woutb[:, c, half * 512:(half + 1) * 512], start=(c == 0), stop=(c == 7))
                for half in range(2):
                    S.op('dve', 'tensor_tensor', out=x1s[:, half * 512:(half + 1) * 512], in0=PS[1 + half][:, 0:512],
                         in1=xt[:, half * 512:(half + 1) * 512], op=ALU.add)
                S.dma('sp', out=out[ti * 128:(ti + 1) * 128, :], in_=x1s[:])
                S.op('act', 'activation', out=sq[:], in_=x1s[:], func=AF.Square)
                S.op('dve', 'reduce_sum', out=ss[:], in_=sq[:], axis=AX.X)
                S.op('dve', 'tensor_scalar', out=rs[:], in0=ss[:], scalar1=1.0 / 1024.0, scalar2=EPS,
                     op0=ALU.mult, op1=ALU.add)
                S.op('act', 'activation', out=rs[:], in_=rs[:], func=AF.Ln)
                S.op('act', 'activation', out=rs[:], in_=rs[:], func=AF.Exp, scale=-0.5)
                S.op('dve', 'tensor_scalar', out=sq[:], in0=x1s[:], scalar1=rs[:, 0:1], scalar2=None, op0=ALU.mult)
                S.op('pool', 'tensor_tensor', out=yb[:], in0=sq[:], in1=n2wb[:], op=ALU.mult)
                pb = PS[0][:].bitcast(BF16)
                for kc in range(8):
                    S.op('pe', 'transpose', out=pb[:, kc * 128:(kc + 1) * 128], in_=yb[:, kc * 128:(kc + 1) * 128],
                         identity=idb[:])
                S.op('act', 'activation', out=yT[:, :, ti * 128:(ti + 1) * 128],
                     in_=pb[:, 0:1024].rearrange("p (a b) -> p a b", a=8), func=AF.Copy)
            S.barrier()
        if debug == 'x1':
            S.finish()
            return nc
        S.barrier()
        es_c.close()
        sb = sb_persist

        gd = nc.dram_tensor("gd", [2048, 16384], BF16, kind="Internal").ap()
        sas = [nc.dram_tensor("sa%d" % i, [128, 128, 128], BF16, kind="Internal").ap() for i in range(2)]
        sbs = [nc.dram_tensor("sb%d" % i, [128, 128, 128], BF16, kind="Internal").ap() for i in range(2)]
        with ExitStack() as ep:
            def sp_(name, shape, dt=F32):
                return ep.enter_context(nc.sbuf_tensor(name, list(shape), dt))
            wqb = sp_("wqb", [128, 8, 2048], BF16)
            load_w(wqb, wq, 0, 2048, None)
            skT = sp_("skT", [128, 16, 128])
            skst = [sp_("skst0", [128, 128]), sp_("skst1", [128, 128])]
            for hp in range(16):
                st = skst[hp % 2]
                S.dma('sp', out=st[:], in_=subk[hp, :, :])
                S.op('pe', 'transpose', out=PS[1][:, 0:128], in_=st[:], identity=idf[:])
                S.op('act', 'activation', out=skT[:, hp, :], in_=PS[1][:, 0:128], func=AF.Copy)
            qT = sp_("qT", [128, 8, 128])
            scs = [sp_("sc%d" % i, [128, 16, 128]) for i in range(2)]
            wks = [sp_("wk%d" % i, [128, 256]) for i in range(2)]
            sv8s = [sp_("sv8_%d" % i, [128, 8]) for i in range(2)]
            svs = [sp_("sv%d" % i, [128, 16, 16]) for i in range(2)]
            cand = sp_("cand", [128, 8, 256])
            ctop = sp_("ctop", [128, 8, 16])
            ediff = sp_("ediff", [128, 8, 16])
            zz = sp_("zz", [128, 8])
            nbs = [sp_("nb%d" % i, [128, 8]) for i in range(2)]
            taus = [sp_("tau%d" % i, [128, 8]) for i in range(2)]
            AAs = [sp_("AA%d" % i, [128, 32, 128], BF16) for i in range(2)]
            BBs = [sp_("BB%d" % i, [128, 32, 128], BF16) for i in range(2)]
            tmpAs = [sp_("tmpA%d" % i, [128, 16, 128]) for i in range(2)]
            efs = [sp_("ef%d" % i, [128, 16, 128]) for i in range(2)]
            NG = 16
            ATs = [sp_("AT%d" % i, [128, NG, 128], BF16) for i in range(3)]
            BTs = [sp_("BT%d" % i, [128, NG, 128], BF16) for i in range(3)]
            Gsbs = [sp_("Gsb%d" % i, [128, NG, 128], BF16) for i in range(2)]
            cnt = {'g': 0, 'ev': 0}

            def p1b_load(tj, g):
                i = (tj * (128 // NG) + g) % 3
                sa_, sb_ = sas[tj % 2], sbs[tj % 2]
                t0 = g * NG
                S.dma('sp', out=ATs[i][:], in_=sa_[t0:t0 + NG, :, :].rearrange("t k c -> k t c"))
                S.dma('sp', out=BTs[i][:], in_=sb_[t0:t0 + NG, :, :].rearrange("t k c -> k t c"))

            ngr = 128 // NG

            def p1b_group(tj, g):
                i = (tj * ngr + g) % 3
                AT, BT, Gsb = ATs[i], BTs[i], Gsbs[g % 2]
                t0 = g * NG
                for q4 in range(NG // 4):
                    pg = PS[4 + cnt['ev'] % 4]
                    for tt in range(4):
                        t = q4 * 4 + tt
                        S.op('pe', 'matmul', out=pg[:, tt * 128:(tt + 1) * 128], lhsT=AT[:, t, :], rhs=BT[:, t, :],
                             start=True, stop=True)
                    dst = Gsb[:, q4 * 4:(q4 + 1) * 4, :].rearrange("p t j -> p (t j)")
                    S.op('act', 'activation', out=dst, in_=pg[:, 0:512], func=AF.Copy)
                    cnt['ev'] += 1
                if g + 2 < ngr:
                    p1b_load(tj, g + 2)
                S.dma('sp', out=gd[tj * 128 + t0:tj * 128 + t0 + NG, :].rearrange("t (c j) -> c t j", c=128),
                      in_=Gsb[:])

            def front_a(ti, parts=(0, 1, 2, 3)):
                sc, sv, nb, tau = scs[ti % 2], svs[ti % 2], nbs[ti % 2], taus[ti % 2]
                for g4 in parts:
                    for j in range(4):
                        hp = g4 * 4 + j
                        pq = PS[1 + hp % 2]
                        for kc in range(8):
                            S.op('pe', 'matmul', out=pq[:, 0:128], lhsT=wqb[:, kc, hp * 128:(hp + 1) * 128],
                                 rhs=yT[:, kc, ti * 128:(ti + 1) * 128], start=(kc == 0), stop=(kc == 7))
                        S.op('act', 'activation', out=qT[:, (g4 % 2) * 4 + j, :], in_=pq[:, 0:128], func=AF.Copy)
                    pscr = PS[3] if g4 % 2 == 0 else PS[0]
                    for j in range(4):
                        hp = g4 * 4 + j
                        S.op('pe', 'matmul', out=pscr[:, j * 128:(j + 1) * 128], lhsT=qT[:, (g4 % 2) * 4 + j, :],
                             rhs=skT[:, hp, :], start=True, stop=True)
                    S.op('act', 'activation', out=sc[:, g4 * 4:(g4 + 1) * 4, :].rearrange("p a b -> p (a b)"),
                         in_=pscr[:, 0:512], func=AF.Copy)

            def front_b(ti):
                sc, sv, nb, tau = scs[ti % 2], svs[ti % 2], nbs[ti % 2], taus[ti % 2]
                for hp0 in range(0, 16, 2):
                    for hp in (hp0, hp0 + 1):
                        S.op('dve', 'max', out=sv8s[hp % 2][:], in_=sc[:, hp, :])
                    for hp in (hp0, hp0 + 1):
                        S.op('dve', 'match_replace', out=wks[hp % 2][:, 0:128], in_to_replace=sv8s[hp % 2][:],
                             in_values=sc[:, hp, :], imm_value=-1e30)
                    for hp in (hp0, hp0 + 1):
                        S.op('dve', 'max', out=sv[:, hp, 8:16], in_=wks[hp % 2][:, 0:128])
                    for hp in (hp0, hp0 + 1):
                        S.op('pool', 'tensor_copy', out=sv[:, hp, 0:8], in_=sv8s[hp % 2][:])
                svv = sv[:].rearrange("p (h two) k -> p h two k", two=2)
                S.op('dve', 'tensor_tensor', out=cand[:].rearrange("p h (a b) -> p h a b", a=16),
                     in0=svv[:, :, 0, :].unsqueeze(3).to_broadcast([128, 8, 16, 16]),
                     in1=svv[:, :, 1, :].unsqueeze(2).to_broadcast([128, 8, 16, 16]), op=ALU.add)
                for h0 in range(0, 8, 2):
                    for h in (h0, h0 + 1):
                        S.op('dve', 'max', out=sv8s[h % 2][:], in_=cand[:, h, :])
                    for h in (h0, h0 + 1):
                        S.op('dve', 'match_replace', out=wks[h % 2][:, 0:256], in_to_replace=sv8s[h % 2][:],
                             in_values=cand[:, h, :], imm_value=-1e30)
                    for h in (h0, h0 + 1):
                        S.op('dve', 'max', out=ctop[:, h, 8:16], in_=wks[h % 2][:, 0:256])
                    for h in (h0, h0 + 1):
                        S.op('pool', 'tensor_copy', out=ctop[:, h, 0:8], in_=sv8s[h % 2][:])
                S.op('dve', 'tensor_tensor', out=ediff[:], in0=ctop[:], in1=ctop[:, :, 0:1].to_broadcast([128, 8, 16]),
                     op=ALU.subtract)
                S.op('act', 'activation', out=ediff[:], in_=ediff[:], func=AF.Exp)
                S.op('dve', 'tensor_reduce', out=zz[:], in_=ediff[:], axis=AX.X, op=ALU.add)
                S.op('act', 'activation', out=zz[:], in_=zz[:], func=AF.Ln)
                S.op('dve', 'tensor_tensor', out=nb[:], in0=zz[:], in1=ctop[:, :, 0], op=ALU.add)
                S.op('dve', 'tensor_scalar', out=nb[:], in0=nb[:], scalar1=-1.0, scalar2=None, op0=ALU.mult)
                S.op('dve', 'tensor_copy', out=tau[:], in_=ctop[:, :, 15])

            def heads(ti):
                sc, sv, nb, tau = scs[ti % 2], svs[ti % 2], nbs[ti % 2], taus[ti % 2]
                if ti >= 1:
                    p1b_load(ti - 1, 0)
                    p1b_load(ti - 1, 1)
                for hh in range(4):
                    AA, BB = AAs[hh % 2], BBs[hh % 2]
                    def ops(h4):
                        h = hh * 2 + h4
                        return (h, tmpAs[h % 2], efs[h % 2],
                                sc[:, 2 * h, :].unsqueeze(1).to_broadcast([128, 16, 128]),
                                sc[:, 2 * h + 1, :].unsqueeze(1).to_broadcast([128, 16, 128]),
                                sv[:, 2 * h + 1, :].unsqueeze(2).to_broadcast([128, 16, 128]))
                    for h4 in range(2):
                        h, tmpA, ef, s0b, s1b, v1b = ops(h4)
                        S.op('dve', 'tensor_tensor', out=tmpA[:], in0=s0b, in1=v1b, op=ALU.add)
                        S.op('act', 'activation', out=ef[:], in_=tmpA[:], func=AF.Exp, bias=nb[:, h:h + 1], scale=1.0)
                    for h4 in range(2):
                        h, tmpA, ef, s0b, s1b, v1b = ops(h4)
                        S.op('dve', 'tensor_tensor', out=BB[:, h4 * 16:(h4 + 1) * 16, :], in0=s1b, in1=v1b,
                             op=ALU.is_equal)
                    for h4 in range(2):
                        h, tmpA, ef, s0b, s1b, v1b = ops(h4)
                        S.op('dve', 'scalar_tensor_tensor', out=AA[:, h4 * 16:(h4 + 1) * 16, :], in0=tmpA[:],
                             scalar=tau[:, h:h + 1], in1=ef[:], op0=ALU.is_ge, op1=ALU.mult)
                    S.dma('pool', out=sas[ti % 2][:, hh * 32:(hh + 1) * 32, :], in_=AA[:])
                    S.dma('pool', out=sbs[ti % 2][:, hh * 32:(hh + 1) * 32, :], in_=BB[:])
                    if ti >= 1:
                        p1b_group(ti - 1, 2 * hh)
                        p1b_group(ti - 1, 2 * hh + 1)
                    if ti + 1 < NOWN:
                        front_a(ti + 1, (hh,))

            front_a(0)
            front_b(0)
            for ti in range(NOWN):
                heads(ti)
                if ti + 1 < NOWN:
                    front_b(ti + 1)
            p1b_load(NOWN - 1, 0)
            p1b_load(NOWN - 1, 1)
            for g in range(ngr):
                p1b_group(NOWN - 1, g)
            S.barrier()

        with ExitStack() as e2:
            def s2_(name, shape, dt=F32):
                return e2.enter_context(nc.sbuf_tensor(name, list(shape), dt))
            acc = s2_("acc", [128, 16, 1024])
            for ti in range(NOWN):
                S.dma('sp', out=acc[:, ti, :], in_=out[ti * 128:(ti + 1) * 128, :])
            usts = [s2_("ust%d" % i, [128, 2, 1024]) for i in range(2)]
            vsts = [s2_("vst%d" % i, [128, 2, 1024]) for i in range(2)]
            ub = s2_("ub", [128, 4, 1024], BF16)
            vbfs = [s2_("vbf%d" % i, [128, 4, 1024], BF16) for i in range(2)]
            UTs = [s2_("UT%d" % i, [128, 8, 512], BF16) for i in range(2)]
            Gt = [s2_("Gt%d" % i, [128, 512], BF16) for i in range(3)]
            gls = [s2_("gl0", [128, 512]), s2_("gl1", [128, 512])]
            Wbs = [s2_("Wb0", [128, 512], BF16), s2_("Wb1", [128, 512], BF16)]
            WTs = [s2_("WT0", [128, 4, 128], BF16), s2_("WT1", [128, 4, 128], BF16)]
            NEB = 32

            def piece(eb, half):
                ust, vst = usts[half], vsts[half]
                r0 = eb * 512 + half * 256
                S.dma('sp', out=ust[:], in_=pu[r0:r0 + 256, :].rearrange("(a p) d -> p a d", p=128))
                S.dma('sp', out=vst[:], in_=pv[r0:r0 + 256, :].rearrange("(a p) d -> p a d", p=128))
                for a2 in range(2):
                    a = half * 2 + a2
                    S.op('dve', 'tensor_copy', out=ub[:, a, :], in_=ust[:, a2, :])
                    pb = PS[0][:].bitcast(BF16)
                    for kc in range(8):
                        S.op('pe', 'transpose', out=pb[:, kc * 128:(kc + 1) * 128], in_=ub[:, a, kc * 128:(kc + 1) * 128],
                             identity=idb[:])
                    S.op('act', 'activation', out=UTs[eb % 2][:, :, a * 128:(a + 1) * 128],
                         in_=pb[:, 0:1024].rearrange("p (k e) -> p k e", k=8), func=AF.Copy)
                    S.op('dve', 'tensor_copy', out=vbfs[eb % 2][:, a, :], in_=vst[:, a2, :])
            piece(0, 0)
            piece(0, 1)
            for eb in range(NEB):
                UT, vbf = UTs[eb % 2], vbfs[eb % 2]

                def stage_h(ti):
                    g_ = Gt[ti % 3]
                    S.dma('sp', out=g_[:], in_=gd[ti * 128:(ti + 1) * 128, eb * 512:(eb + 1) * 512])
                    ph = PS[1 + ti % 2]
                    for kc in range(8):
                        S.op('pe', 'matmul', out=ph[:, 0:512], lhsT=yT[:, kc, ti * 128:(ti + 1) * 128], rhs=UT[:, kc, :],
                             start=(kc == 0), stop=(kc == 7))
                    S.op('act', 'activation', out=gls[ti % 2][:], in_=ph[:, 0:512], func=AF.Gelu)
                    S.op('dve', 'tensor_tensor', out=Wbs[ti % 2][:], in0=gls[ti % 2][:], in1=g_[:], op=ALU.mult)

                def stage_t(ti):
                    pw = PS[3 if ti % 2 == 0 else 0][:].bitcast(BF16)
                    Wb_, WT_ = Wbs[ti % 2], WTs[ti % 2]
                    for a in range(4):
                        S.op('pe', 'transpose', out=pw[:, a * 128:(a + 1) * 128], in_=Wb_[:, a * 128:(a + 1) * 128],
                             identity=idb[:])
                    S.op('act', 'activation', out=WT_[:].rearrange("p a t -> p (a t)"), in_=pw[:, 0:512], func=AF.Copy)

                def stage_o(ti):
                    WT_ = WTs[ti % 2]
                    for half in range(2):
                        po = PS[4 + 2 * (ti % 2) + half]
                        for a in range(4):
                            S.op('pe', 'matmul', out=po[:, 0:512], lhsT=WT_[:, a, :],
                                 rhs=vbf[:, a, half * 512:(half + 1) * 512], start=(a == 0), stop=(a == 3))
                        S.op('dve', 'tensor_tensor', out=acc[:, ti, half * 512:(half + 1) * 512],
                             in0=acc[:, ti, half * 512:(half + 1) * 512], in1=po[:, 0:512], op=ALU.add)
                stage_h(0)
                stage_t(0)
                stage_h(1)
                for ti in range(NOWN):
                    if ti + 1 < NOWN:
                        stage_t(ti + 1)
                    if ti + 2 < NOWN:
                        stage_h(ti + 2)
                    stage_o(ti)
                    if eb + 1 < NEB and ti in (3, 9):
                        piece(eb + 1, 0 if ti == 3 else 1)
            for ti in range(NOWN):
                S.dma('sp', out=out[ti * 128:(ti + 1) * 128, :], in_=acc[:, ti, :])
        S.finish()
      except _Stop:
        S.finish()
    return nc


def _consts(par):
    c = {}
    p = np.arange(128)[:, None]
    f = np.arange(128)[None, :]
    same = (p // 64) == (f // 64)
    c["c_tri"] = np.where(same & (p <= f), -1.0 / 16, 0.0).astype(np.float32)
    c["c_us"] = np.where(same & (p > f), -1.0 / 16, 0.0).astype(np.float32)
    c["c_ind"] = np.where((p // 64) == np.arange(2)[None, :], -1.0 / 16, 0.0).astype(np.float32)
    c["c_m2"] = np.where(same & (p <= f), 1.0, 0.0).astype(np.float32)
    c["c_idb"] = np.eye(128).astype(ml_dtypes.bfloat16)
    c["c_idf"] = np.eye(128).astype(np.float32)
    q = np.arange(256)[None, None, :]
    a = np.arange(2)[None, :, None]
    c["c_cm"] = ((a * 128 + p[:, :, None]) <= q).astype(np.float32).reshape(128, 512).astype(ml_dtypes.bfloat16)
    boh = np.zeros((16, 16, 128), np.float32)
    for n in range(16):
        boh[n, n, :] = 1.0
    c["c_boh"] = boh.reshape(16, 2048).astype(ml_dtypes.bfloat16)
    valid = np.zeros((128, 16, 16), np.float32)
    ownm = np.zeros((128, 16, 16), np.float32)
    nfirst = 0 if par == 1 else 8
    for ti in range(16):
        b = 8 + ti // 2
        valid[:, ti, nfirst:b] = 1.0
        ownm[:, ti, b] = 1.0
    c["c_valid"] = valid.reshape(128, 256)
    c["c_negv"] = ((valid - 1.0) * 1e30).reshape(128, 256).astype(np.float32)
    c["c_own"] = ownm.reshape(128, 256)
    gpos = np.arange(4096, dtype=np.float32) - (0.0 if par == 1 else 2048.0)
    half = 32
    inv = (10000.0 ** (-np.arange(half, dtype=np.float32) / half)).astype(np.float32)
    ang = gpos[:, None].astype(np.float32) * inv[None, :]
    cos, sin = np.cos(ang).astype(np.float32), np.sin(ang).astype(np.float32)
    c["ropeC"] = np.concatenate([cos, cos], 1).astype(np.float32)
    c["ropeS"] = np.concatenate([-sin, sin], 1).astype(np.float32)
    return c


def make_in_maps(inputs, cores=range(8)):
    x = np.asarray(inputs["x"], np.float32)
    shared = {
        "w_in": np.ascontiguousarray(inputs["w_in"][0]),
        "n1w": np.ascontiguousarray(np.asarray(inputs["norm1_w"][0]).reshape(8, 128).T),
        "gonw": np.asarray(inputs["gla_out_norm_w"][0]).reshape(1, 128),
        "mixs": np.asarray(inputs["mix_scale"][0]).reshape(1, 1024),
        "qnw": np.asarray(inputs["moba_q_norm_w"][0]).reshape(1, 64),
        "knw": np.asarray(inputs["moba_k_norm_w"][0]).reshape(1, 64),
        "n2w": np.asarray(inputs["norm2_w"][0]).reshape(1, 1024),
        "w_out": np.ascontiguousarray(inputs["w_out"][0]),
        "wq": np.ascontiguousarray(inputs["peer_w_query"][0]),
        "subk": np.ascontiguousarray(np.asarray(inputs["peer_subkeys"][0]).reshape(16, 128, 128)),
        "pu": np.ascontiguousarray(inputs["peer_u"][0]),
        "pv": np.ascontiguousarray(inputs["peer_v"][0]),
    }
    wal = np.zeros((32, 256), np.float32)
    wal[0:16] = inputs["gla_w_alpha"][0]
    wal[16] = inputs["gla_b_alpha"][0]
    shared["wal"] = wal
    shared = {k: np.ascontiguousarray(v, dtype=np.float32) for k, v in shared.items()}
    cs = [_consts(0), _consts(1)]
    maps = []
    for c in cores:
        b, par = c // 2, c % 2
        if par == 1:
            xl_ = x[b]
        else:
            xl_ = np.concatenate([np.zeros((2048, 1024), np.float32), x[b, :2048]], 0)
        m = dict(shared)
        m.update(cs[par])
        m["xl"] = np.ascontiguousarray(xl_)
        maps.append(m)
    return maps


_NC = None


def kernel(**inputs):
    global _NC
    if _NC is None:
        _NC = build()
    maps = make_in_maps(inputs)
    res = run_bass_kernel_spmd(_NC, maps, core_ids=list(range(8)))
    outp = np.zeros((4, 4096, 1024), np.float32)
    for c in range(8):
        b, par = c // 2, c % 2
        outp[b, par * 2048:(par + 1) * 2048] = res.results[c]["out"]
    return outp
```

```python
import numpy as np
import ml_dtypes
from contextlib import ExitStack
import concourse.bass as bass
import concourse.mybir as mybir
from concourse.bass_utils import run_bass_kernel_spmd

F32 = mybir.dt.float32
BF16 = mybir.dt.bfloat16
AF = mybir.ActivationFunctionType
ALU = mybir.AluOpType
AX = mybir.AxisListType
EPS = 1e-6
NDS = 8
NT = 32
NOWN = 16
GQ0, GK0, GV0, GG0, GR0 = 0, 256, 512, 1024, 1536
NGC = 1552
NMC = 1536
BIGM = 30000.0


class _Stop(Exception):
    pass


class Sched:
    def __init__(self, nc, es):
        self.nc = nc
        self.eng = {'pe': nc.tensor, 'act': nc.scalar, 'dve': nc.vector,
                    'pool': nc.gpsimd, 'sp': nc.sync}
        self.semh = {}
        for k in self.eng:
            self.semh['E:' + k] = es.enter_context(nc.semaphore('s_' + k))
        self.cnt = {k: 0 for k in self.eng}
        self.seen = {k: {} for k in self.eng}
        self.dcnt = {}
        self.drr = {}
        for q in ('sp', 'pool', 'act'):
            self.dcnt[q] = [0] * NDS
            self.drr[q] = 0
            for i in range(NDS):
                self.semh['D:%s%d' % (q, i)] = es.enter_context(nc.semaphore('d_%s%d' % (q, i)))
        self.res = {}
        self.nins = 0
        self.nstage = 0
        self.stopped = False

    def _waits(self, e, reads, writes):
        need = {}

        def add(ev, kind):
            key, val, src = ev
            if src == e and e == 'pe':
                return
            if self.seen[e].get(key, 0) >= val:
                return
            if need.get(key, 0) < val:
                need[key] = val
        for r in reads:
            st = self.res.get(r)
            if st and st[0]:
                add(st[0], 'raw')
            if st and r.startswith('ps'):
                for ev in st[1].values():
                    add(ev, 'war')
        for w in writes:
            st = self.res.get(w)
            if st:
                if st[0]:
                    add(st[0], 'waw')
                for ev in st[1].values():
                    add(ev, 'war')
        for key, val in need.items():
            self.eng[e].wait_ge(self.semh[key], val)
            self.seen[e][key] = val

    def _record(self, ev, reads, writes):
        for x in writes:
            self.res[x] = [ev, {}]
        for x in reads:
            st = self.res.setdefault(x, [None, {}])
            st[1][ev[0]] = ev

    @staticmethod
    def _scan(kw):
        reads, writes = [], []
        for k, v in kw.items():
            if isinstance(v, bass.AP):
                if k in ('out', 'accum_out', 'ap'):
                    writes.append(v.name)
                else:
                    reads.append(v.name)
        return reads, writes

    def op(self, e, meth, **kw):
        if self.stopped:
            return
        xr = kw.pop('xr', ())
        alias = kw.pop('alias', None)
        reads, writes = self._scan(kw)
        reads += list(xr)
        if alias:
            reads = [alias.get(r, r) for r in reads]
            writes = [alias.get(w, w) for w in writes]
        self._waits(e, reads, writes)
        ins = getattr(self.eng[e], meth)(**kw)
        self.cnt[e] += 1
        ins.then_inc(self.semh['E:' + e], 1)
        self._record(('E:' + e, self.cnt[e], e), reads, writes)
        self.nins += 1

    def dma(self, q, out, in_):
        if self.stopped:
            return
        i = self.drr[q]
        self.drr[q] = (i + 1) % NDS
        key = 'D:%s%d' % (q, i)
        prev = self.dcnt[q][i] * 16
        if prev and self.seen[q].get(key, 0) < prev:
            self.eng[q].wait_ge(self.semh[key], prev)
            self.seen[q][key] = prev
        reads, writes = self._scan({'out': out, 'in_': in_})
        self._waits(q, reads, writes)
        self.eng[q].dma_start(out=out, in_=in_).then_inc(self.semh[key], 16)
        self.dcnt[q][i] += 1
        self._record((key, self.dcnt[q][i] * 16, 'dma'), reads, writes)
        self.nins += 1

    def barrier(self):
        if self.stopped:
            return
        for e in ('pe', 'act', 'dve', 'pool', 'sp'):
            for o in ('pe', 'act', 'dve', 'pool', 'sp'):
                key = 'E:' + o
                if o != e and self.cnt[o] and self.seen[e].get(key, 0) < self.cnt[o]:
                    self.eng[e].wait_ge(self.semh[key], self.cnt[o])
                    self.seen[e][key] = self.cnt[o]
            for q in ('sp', 'pool', 'act'):
                for i in range(NDS):
                    v = self.dcnt[q][i] * 16
                    key = 'D:%s%d' % (q, i)
                    if v and self.seen[e].get(key, 0) < v:
                        self.eng[e].wait_ge(self.semh[key], v)
                        self.seen[e][key] = v

    def finish(self):
        for q in ('sp', 'pool', 'act'):
            for i in range(NDS):
                v = self.dcnt[q][i] * 16
                key = 'D:%s%d' % (q, i)
                if v and self.seen['sp'].get(key, 0) < v:
                    self.eng['sp'].wait_ge(self.semh[key], v)
                    self.seen['sp'][key] = v
        for e in ('pe', 'act', 'dve', 'pool'):
            if self.cnt[e]:
                self.eng['sp'].wait_ge(self.semh['E:' + e], self.cnt[e])


def build(debug=False):
    nc = bass.Bass("TRN2", target_bir_lowering=False)

    def din(name, shape, dt=F32):
        return nc.dram_tensor(name, list(shape), dt, kind="ExternalInput").ap()
    xl = din("xl", [4096, 1024])
    w_in = din("w_in", [1024, 3088])
    n1w = din("n1w", [128, 8])
    wal = din("wal", [32, 256])
    gonw = din("gonw", [1, 128])
    mixs = din("mixs", [1, 1024])
    qnw = din("qnw", [1, 64])
    knw = din("knw", [1, 64])
    n2w = din("n2w", [1, 1024])
    w_out = din("w_out", [1024, 1024])
    wq = din("wq", [1024, 2048])
    subk = din("subk", [16, 128, 128])
    pu = din("pu", [16384, 1024])
    pv = din("pv", [16384, 1024])
    ropeC = din("ropeC", [4096, 64])
    ropeS = din("ropeS", [4096, 64])
    c_tri = din("c_tri", [128, 128])
    c_us = din("c_us", [128, 128])
    c_ind = din("c_ind", [128, 2])
    c_m2 = din("c_m2", [128, 128])
    c_idb = din("c_idb", [128, 128], BF16)
    c_idf = din("c_idf", [128, 128])
    c_cm = din("c_cm", [128, 512], BF16)
    c_boh = din("c_boh", [16, 2048], BF16)
    c_valid = din("c_valid", [128, 256])
    c_negv = din("c_negv", [128, 256])
    c_own = din("c_own", [128, 256])
    out = nc.dram_tensor("out", [2048, 1024], F32, kind="ExternalOutput").ap()
    dbg = None
    if debug:
        dbg = nc.dram_tensor("dbg", [1024, 2048], F32, kind="ExternalOutput").ap()

    with ExitStack() as es:
      try:
        S = Sched(nc, es)

        def chk(tag):
            if debug == tag:
                S.stopped = True

        def sb(name, shape, dt=F32):
            return es.enter_context(nc.sbuf_tensor(name, list(shape), dt))
        PS = [es.enter_context(nc.psum_tensor("ps%d" % i, [128, 512], F32)) for i in range(8)]

        idb = sb("idb", [128, 128], BF16); idf = sb("idf", [128, 128])
        mixT = sb("mixT", [128, 8, 2048], BF16)
        es_c = ExitStack()
        es.enter_context(es_c)
        sb_persist = sb

        def sb(name, shape, dt=F32):
            return es_c.enter_context(nc.sbuf_tensor(name, list(shape), dt))
        tri = sb("tri", [128, 128]); us = sb("us", [128, 128]); ind = sb("ind", [128, 2])
        m2 = sb("m2", [128, 128])
        cm = sb("cm", [128, 512], BF16); boh = sb("boh", [16, 2048], BF16)
        validm = sb("validm", [128, 256]); negv = sb("negv", [128, 256]); ownm = sb("ownm", [128, 256])
        n1ws = sb("n1ws", [128, 8]); wals = sb("wals", [32, 256])
        ones256 = sb("ones256", [128, 1])
        for t_, d_ in ((tri, c_tri), (us, c_us), (ind, c_ind), (m2, c_m2), (idb, c_idb), (idf, c_idf),
                       (cm, c_cm), (boh, c_boh), (validm, c_valid), (negv, c_negv), (ownm, c_own),
                       (n1ws, n1w), (wals, wal)):
            S.dma('sp', out=t_[:], in_=d_)
        S.op('dve', 'memset', ap=ones256[:], constant=1.0 / 256.0, xr=())
        S.res['ones256'] = [('E:dve', S.cnt['dve'], 'dve'), {}]
        cwg = sb("cwg", [128, 512]); mixm = sb("mixm", [128, 512]); tmpb = sb("tmpb", [128, 512])
        qwb = sb("qwb", [128, 64]); kwb = sb("kwb", [128, 64]); n2wb = sb("n2wb", [128, 1024])
        S.dma('sp', out=cwg[:], in_=mixs[0:1, 0:512].to_broadcast([128, 512]))
        S.dma('sp', out=mixm[:], in_=mixs[0:1, 512:1024].to_broadcast([128, 512]))
        for h in range(4):
            S.dma('sp', out=tmpb[:, h * 128:(h + 1) * 128], in_=gonw[0:1, :].to_broadcast([128, 128]))
        S.dma('sp', out=qwb[:], in_=qnw[0:1, :].to_broadcast([128, 64]))
        S.dma('sp', out=kwb[:], in_=knw[0:1, :].to_broadcast([128, 64]))
        S.dma('sp', out=n2wb[:], in_=n2w[0:1, :].to_broadcast([128, 1024]))
        S.op('dve', 'tensor_tensor', out=cwg[:], in0=cwg[:], in1=tmpb[:], op=ALU.mult)

        es_att = ExitStack()
        es.enter_context(es_att)

        def sba(name, shape, dt=F32):
            return es_att.enter_context(nc.sbuf_tensor(name, list(shape), dt))
        xts = [sb("xt0", [128, 1024]), sb("xt1", [128, 1024])]
        sq = sb("sq", [128, 1024])
        ss = sb("ss", [128, 1]); rs = sb("rs", [128, 1])
        xn = sb("xn", [128, 1024], BF16)
        xnT = sb("xnT", [128, 8, 128], BF16)

        xs_d = nc.dram_tensor("xs_d", [NT, 128, 1024], BF16, kind="Internal").ap()

        def prep(t, save=False):
            xt = xts[t % 2]
            S.dma('sp', out=xt[:], in_=xl[t * 128:(t + 1) * 128, :])
            S.op('act', 'activation', out=sq[:], in_=xt[:], func=AF.Square)
            S.op('dve', 'reduce_sum', out=ss[:], in_=sq[:], axis=AX.X)
            S.op('dve', 'tensor_scalar', out=rs[:], in0=ss[:], scalar1=1.0 / 1024.0, scalar2=EPS,
                 op0=ALU.mult, op1=ALU.add)
            S.op('act', 'activation', out=rs[:], in_=rs[:], func=AF.Ln)
            S.op('act', 'activation', out=rs[:], in_=rs[:], func=AF.Exp, scale=-0.5)
            S.op('dve', 'tensor_scalar', out=xn[:], in0=xt[:], scalar1=rs[:, 0:1], scalar2=None, op0=ALU.mult)
            pb = PS[0][:].bitcast(BF16)
            for kc in range(8):
                S.op('pe', 'transpose', out=pb[:, kc * 128:(kc + 1) * 128], in_=xn[:, kc * 128:(kc + 1) * 128],
                     identity=idb[:])
            S.op('act', 'activation', out=xnT[:].rearrange("p a b -> p (a b)"), in_=pb[:, 0:1024], func=AF.Copy)
            if save:
                S.dma('pool', out=xs_d[t, :, :], in_=xnT[:].rearrange("p a b -> p (a b)"))

        def load_w(dst, src, c0, ncols, scale_col):
            S.nstage += 1
            with ExitStack() as el:
                stage = [el.enter_context(nc.sbuf_tensor("stg%d_%d" % (S.nstage, i), [128, ncols], F32))
                         for i in range(2)]
                for kc in range(8):
                    st = stage[kc % 2]
                    S.dma('sp', out=st[:, 0:ncols], in_=src[kc * 128:(kc + 1) * 128, c0:c0 + ncols])
                    if scale_col is not None:
                        S.op('dve', 'tensor_scalar', out=dst[:, kc, :], in0=st[:, 0:ncols],
                             scalar1=scale_col[:, kc:kc + 1], scalar2=None, op0=ALU.mult)
                    elif kc % 2 == 0:
                        S.op('dve', 'tensor_copy', out=dst[:, kc, :], in_=st[:, 0:ncols])
                    else:
                        S.op('act', 'activation', out=dst[:, kc, :], in_=st[:, 0:ncols], func=AF.Copy)
                S.barrier()

        with ExitStack() as eg:
            def sg_(name, shape, dt=F32):
                return eg.enter_context(nc.sbuf_tensor(name, list(shape), dt))
            wg = sg_("wg", [128, 8, NGC], BF16)
            load_w(wg, w_in, 0, NGC, n1ws)
            grT = sg_("grT", [32, 128])
            S.op('dve', 'memset', ap=grT[:], constant=1.0)
            S.res['grT'] = [('E:dve', S.cnt['dve'], 'dve'), {}]
            gp = sg_("gp", [128, 256]); er = sg_("er", [128, 256])
            kst = sg_("kst", [128, 256], BF16); vb = sg_("vb", [128, 512], BF16)
            dec = sg_("dec", [64, 8])
            Eb = sg_("Eb", [64, 512]); Enb = sg_("Enb", [64, 512])
            qd = sg_("qd", [64, 512], BF16); kin = sg_("kin", [64, 512], BF16)
            Q2 = sg_("Q2", [64, 4, 2, 128], BF16)
            S.op('dve', 'memset', ap=Q2[:], constant=0.0)
            S.res['Q2'] = [('E:dve', S.cnt['dve'], 'dve'), {}]
            ATs = sg_("ATs", [128, 4, 128], BF16)
            sgl = sg_("sgl", [128, 512])
            S32 = sg_("S32", [64, 4, 128])
            S.op('dve', 'memset', ap=S32[:], constant=0.0)
            S.res['S32'] = [('E:dve', S.cnt['dve'], 'dve'), {}]
            Sb0 = sg_("Sb0", [64, 4, 128], BF16); Sb1 = sg_("Sb1", [64, 4, 128], BF16)
            S.op('dve', 'memset', ap=Sb0[:], constant=0.0)
            S.res['Sb0'] = [('E:dve', S.cnt['dve'], 'dve'), {}]
            so = sg_("so", [128, 4]); on = sg_("on", [128, 512]); mg = sg_("mg", [128, 512], BF16)

            sqo = sg_("sqo", [128, 512])
            prep(0, save=True)
            for t in range(NT):
                own = t >= NOWN
                ti = t - NOWN
                p_gk, p_gv, p_gr = PS[1], PS[2], PS[3]
                for kc in range(8):
                    S.op('pe', 'matmul', out=p_gk[:, 0:256], lhsT=xnT[:, kc, :], rhs=wg[:, kc, GK0:GK0 + 256],
                         start=(kc == 0), stop=(kc == 7))
                for kc in range(8):
                    S.op('pe', 'matmul', out=p_gv[:, 0:512], lhsT=xnT[:, kc, :], rhs=wg[:, kc, GV0:GV0 + 512],
                         start=(kc == 0), stop=(kc == 7))
                for kc in range(8):
                    S.op('pe', 'matmul', out=p_gr[0:16, 0:128], lhsT=wg[:, kc, GR0:GR0 + 16], rhs=xnT[:, kc, :],
                         start=(kc == 0), stop=(kc == 7))
                S.op('dve', 'tensor_copy', out=grT[0:16, :], in_=p_gr[0:16, 0:128])
                S.op('pe', 'matmul', out=p_gr[:, 256:512], lhsT=grT[:, :], rhs=wals[:, :], start=True, stop=True)
                S.op('act', 'activation', out=gp[:], in_=p_gr[:, 256:512], func=AF.Exp, scale=-1.0)
                S.op('dve', 'tensor_scalar', out=gp[:], in0=gp[:], scalar1=1.0, scalar2=None, op0=ALU.add)
                S.op('act', 'activation', out=gp[:], in_=gp[:], func=AF.Ln)
                p_r = PS[4]
                S.op('pe', 'matmul', out=p_r[:, 0:256], lhsT=us[:], rhs=gp[:], start=True, stop=True)
                for h in range(4):
                    S.op('pe', 'matmul', out=p_r[0:64, 256 + h * 2:256 + h * 2 + 2], lhsT=gp[:, h * 64:(h + 1) * 64],
                         rhs=ind[:], start=True, stop=True)
                S.op('act', 'activation', out=er[:], in_=p_r[:, 0:256], func=AF.Exp)
                S.op('act', 'activation', out=dec[:], in_=p_r[0:64, 256:264], func=AF.Exp)
                S.op('dve', 'tensor_tensor', out=kst[:], in0=p_gk[:, 0:256], in1=er[:], op=ALU.mult)
                S.op('act', 'activation', out=vb[:], in_=p_gv[:, 0:512], func=AF.Copy)
                if not own and t + 1 < NT:
                    prep(t + 1, save=True)
                if own:
                    p_q, p_k, p_b, p_gg = PS[5], PS[6], PS[7], PS[1]
                    for h in range(4):
                        for kc in range(8):
                            S.op('pe', 'matmul', out=p_q[0:64, h * 128:(h + 1) * 128],
                                 lhsT=wg[:, kc, GQ0 + h * 64:GQ0 + (h + 1) * 64], rhs=xnT[:, kc, :],
                                 start=(kc == 0), stop=(kc == 7))
                    for h in range(4):
                        for kc in range(8):
                            S.op('pe', 'matmul', out=p_k[0:64, h * 128:(h + 1) * 128],
                                 lhsT=wg[:, kc, GK0 + h * 64:GK0 + (h + 1) * 64], rhs=xnT[:, kc, :],
                                 start=(kc == 0), stop=(kc == 7))
                    for h in range(4):
                        S.op('pe', 'matmul', out=p_b[0:64, h * 128:(h + 1) * 128], lhsT=gp[:, h * 64:(h + 1) * 64],
                             rhs=tri[:], start=True, stop=True)
                    S.op('act', 'activation', out=Eb[:], in_=p_b[0:64, :], func=AF.Exp)
                    S.op('act', 'activation', out=Enb[:], in_=p_b[0:64, :], func=AF.Exp, scale=-1.0)
                    S.op('dve', 'scalar_tensor_tensor', out=qd[:], in0=p_q[0:64, :], scalar=0.125, in1=Eb[:],
                         op0=ALU.mult, op1=ALU.mult)
                    qdv = qd[:].rearrange("p (h c) -> p h c", h=4)
                    S.op('pool', 'tensor_copy', out=Q2[:, :, 0, 0:64], in_=qdv[:, :, 0:64])
                    S.op('pool', 'tensor_copy', out=Q2[:, :, 1, 64:128], in_=qdv[:, :, 64:128])
                    S.op('dve', 'tensor_tensor', out=kin[:], in0=p_k[0:64, :], in1=Enb[:], op=ALU.mult)
                    for kc in range(8):
                        S.op('pe', 'matmul', out=p_gg[:, 0:512], lhsT=xnT[:, kc, :], rhs=wg[:, kc, GG0:GG0 + 512],
                             start=(kc == 0), stop=(kc == 7))
                    S.op('act', 'activation', out=sgl[:], in_=p_gg[:, 0:512], func=AF.Silu)
                    if t + 1 < NT:
                        prep(t + 1, save=True)
                    p_at = PS[5]
                    for h in range(4):
                        S.op('pe', 'matmul', out=p_at[:, h * 128:(h + 1) * 128], lhsT=kin[:, h * 128:(h + 1) * 128],
                             rhs=qd[:, h * 128:(h + 1) * 128], start=True, stop=True)
                    S.op('dve', 'tensor_tensor', out=ATs[:], in0=p_at[:].rearrange("p (h c) -> p h c", h=4),
                         in1=m2[:].unsqueeze(1).to_broadcast([128, 4, 128]), op=ALU.mult)
                p_u = PS[6]
                S32v = S32[:]
                for cj in range(2):
                    for h in range(4):
                        S.op('pe', 'matmul', out=p_u[0:64, h * 128:(h + 1) * 128],
                             lhsT=kst[cj * 64:(cj + 1) * 64, h * 64:(h + 1) * 64],
                             rhs=vb[cj * 64:(cj + 1) * 64, h * 128:(h + 1) * 128], start=True, stop=True)
                    decv = dec[:].rearrange("p (h j) -> p h j", j=2)[:, :, cj:cj + 1].to_broadcast([64, 4, 128])
                    S.op('dve', 'tensor_tensor', out=S32v, in0=S32v, in1=decv, op=ALU.mult)
                    S.op('dve', 'tensor_tensor', out=S32v, in0=S32v,
                         in1=p_u[0:64, :].rearrange("p (h c) -> p h c", h=4), op=ALU.add)
                    Sdst = Sb1 if cj == 0 else Sb0
                    if own and cj == 1:
                        p_o = PS[7]
                        for h in range(4):
                            S.op('pe', 'matmul', out=p_o[:, h * 128:(h + 1) * 128], lhsT=ATs[:, h, :],
                                 rhs=vb[:, h * 128:(h + 1) * 128], start=True, stop=False)
                            S.op('pe', 'matmul', out=p_o[:, h * 128:(h + 1) * 128], lhsT=Q2[:, h, 0, :],
                                 rhs=Sb0[:, h, :], start=False, stop=False)
                            S.op('pe', 'matmul', out=p_o[:, h * 128:(h + 1) * 128], lhsT=Q2[:, h, 1, :],
                                 rhs=Sb1[:, h, :], start=False, stop=True)
                    S.op('act', 'activation', out=Sdst[:], in_=S32v, func=AF.Copy)
                if own:
                    p_o = PS[7]
                    S.op('act', 'activation', out=sqo[:], in_=p_o[:, 0:512], func=AF.Square)
                    S.op('dve', 'tensor_reduce', out=so[:], in_=sqo[:].rearrange("p (h c) -> p h c", h=4),
                         axis=AX.X, op=ALU.add)
                    S.op('dve', 'tensor_scalar', out=so[:], in0=so[:], scalar1=1.0 / 128.0, scalar2=EPS,
                         op0=ALU.mult, op1=ALU.add)
                    S.op('act', 'activation', out=so[:], in_=so[:], func=AF.Ln)
                    S.op('act', 'activation', out=so[:], in_=so[:], func=AF.Exp, scale=-0.5)
                    S.op('dve', 'tensor_tensor', out=on[:].rearrange("p (h c) -> p h c", h=4),
                         in0=p_o[:, 0:512].rearrange("p (h c) -> p h c", h=4),
                         in1=so[:].unsqueeze(2).to_broadcast([128, 4, 128]), op=ALU.mult)
                    S.op('pool', 'tensor_tensor', out=on[:], in0=on[:], in1=cwg[:], op=ALU.mult)
                    S.op('dve', 'tensor_tensor', out=mg[:], in0=on[:], in1=sgl[:], op=ALU.mult)
                    pb = PS[0][:].bitcast(BF16)
                    for c in range(4):
                        S.op('pe', 'transpose', out=pb[:, c * 128:(c + 1) * 128], in_=mg[:, c * 128:(c + 1) * 128],
                             identity=idb[:])
                    S.op('act', 'activation', out=mixT[:, 0:4, ti * 128:(ti + 1) * 128],
                         in_=pb[:, 0:512].rearrange("p (c k) -> p c k", c=4), func=AF.Copy)
            S.barrier()

        if debug == 'gla':
            dst = sb("dbgt", [128, 2048])
            for c in range(4):
                S.op('dve', 'tensor_copy', out=dst[:], in_=mixT[:, c, :])
                S.dma('sp', out=dbg[c * 128:(c + 1) * 128, :], in_=dst[:])
            S.finish()
            return nc
        es_m = ExitStack()
        es.enter_context(es_m)
        KT = es_m.enter_context(nc.sbuf_tensor("KT", [128, 4, 4096], BF16))
        Vaug = es_m.enter_context(nc.sbuf_tensor("Vaug", [128, 32, 8, 65], BF16))
        QTb = es_m.enter_context(nc.sbuf_tensor("QTb", [128, 4, 2048], BF16))
        negm = es_m.enter_context(nc.sbuf_tensor("negm", [128, 16, 8, 16], BF16))
        kbarT = es_m.enter_context(nc.sbuf_tensor("kbarT", [128, 4, 16], F32))
        S.op('pool', 'memset', ap=Vaug[:, :, :, 64:65], constant=1.0)
        S.op('pool', 'memset', ap=kbarT[:], constant=0.0)
        with ExitStack() as em:
            def sm_(name, shape, dt=F32):
                return em.enter_context(nc.sbuf_tensor(name, list(shape), dt))
            wm = sm_("wm", [128, 8, NMC], BF16)
            load_w(wm, w_in, NGC, NMC, n1ws)
            chk('m0')
            xnTm = [sm_("xnTm0", [128, 8, 128], BF16), sm_("xnTm1", [128, 8, 128], BF16)]
            rC = [sm_("rC0", [128, 64]), sm_("rC1", [128, 64])]
            rS = [sm_("rS0", [128, 64]), sm_("rS1", [128, 64])]
            ssq = sm_("ssq", [128, 8]); kn = sm_("kn", [128, 512]); t1 = sm_("t1", [128, 512])
            t2 = sm_("t2", [128, 512]); kr = sm_("kr", [128, 512]); qr = sm_("qr", [128, 512])
            qT32 = sm_("qT32", [128, 4, 128]); kb_tmp = sm_("kb_tmp", [128, 4])
            bsm = sm_("bsm", [128, 8, 16]); mx = sm_("mx", [128, 8, 8]); sel = sm_("sel", [128, 8, 16])

            ssq2 = sm_("ssq2", [128, 8])

            def nr_steps(ps, wb_, cc, sn, dst, sq_, ssq_, kn_, t1_, t2_):
                knv = kn_.rearrange("p (h c) -> p h c", h=8)
                t1v = t1_.rearrange("p (h c) -> p h c", h=8)
                t2v = t2_.rearrange("p (h c) -> p h c", h=8)
                return [
                    lambda: S.op('act', 'activation', out=sq_, in_=ps[:, 0:512], func=AF.Square),
                    lambda: S.op('dve', 'tensor_reduce', out=ssq_[:], in_=sq_.rearrange("p (h c) -> p h c", h=8),
                                 axis=AX.X, op=ALU.add),
                    lambda: S.op('dve', 'tensor_scalar', out=ssq_[:], in0=ssq_[:], scalar1=1.0 / 64.0, scalar2=EPS,
                                 op0=ALU.mult, op1=ALU.add),
                    lambda: S.op('act', 'activation', out=ssq_[:], in_=ssq_[:], func=AF.Ln),
                    lambda: S.op('act', 'activation', out=ssq_[:], in_=ssq_[:], func=AF.Exp, scale=-0.5),
                    lambda: S.op('dve', 'tensor_tensor', out=knv, in0=ps[:, 0:512].rearrange("p (h c) -> p h c", h=8),
                                 in1=ssq_[:].unsqueeze(2).to_broadcast([128, 8, 64]), op=ALU.mult),
                    lambda: S.op('pool', 'tensor_tensor', out=knv, in0=knv,
                                 in1=wb_[:].unsqueeze(1).to_broadcast([128, 8, 64]), op=ALU.mult),
                    lambda: S.op('dve', 'tensor_tensor', out=t1v, in0=knv,
                                 in1=cc[:].unsqueeze(1).to_broadcast([128, 8, 64]), op=ALU.mult),
                    lambda: S.op('pool', 'tensor_tensor', out=t2v[:, :, 0:32], in0=knv[:, :, 32:64],
                                 in1=sn[:, 0:32].unsqueeze(1).to_broadcast([128, 8, 32]), op=ALU.mult),
                    lambda: S.op('pool', 'tensor_tensor', out=t2v[:, :, 32:64], in0=knv[:, :, 0:32],
                                 in1=sn[:, 32:64].unsqueeze(1).to_broadcast([128, 8, 32]), op=ALU.mult),
                    lambda: S.op('dve', 'tensor_tensor', out=dst[:], in0=t1_, in1=t2_, op=ALU.add),
                ]

            def norm_rope_kq(t, own, cc, sn):
                ka = nr_steps(PS[1], kwb, cc, sn, kr, sq[:, 0:512], ssq, kn[:], t1[:], t2[:])
                if not own:
                    for f in ka:
                        f()
                    return
                qa = nr_steps(PS[3], qwb, cc, sn, qr, xts[1][:, 0:512], ssq2, xts[0][:, 0:512], xts[0][:, 512:1024],
                              xts[1][:, 512:1024])
                for fk, fq in zip(ka, qa):
                    fk()
                    fq()

            for t in range(NT):
                own = t >= NOWN
                ti = t - NOWN
                nblk = t // 2
                if t == 16:
                    chk('n0')
                if t == 0:
                    S.dma('sp', out=xnTm[0][:].rearrange("p a b -> p (a b)"), in_=xs_d[0, :, :])
                if t + 1 < NT:
                    S.dma('sp', out=xnTm[(t + 1) % 2][:].rearrange("p a b -> p (a b)"), in_=xs_d[t + 1, :, :])
                xnT_ = xnTm[t % 2]
                cc, sn = rC[t % 2], rS[t % 2]
                S.dma('sp', out=cc[:], in_=ropeC[t * 128:(t + 1) * 128, :])
                S.dma('sp', out=sn[:], in_=ropeS[t * 128:(t + 1) * 128, :])
                chk('m1')
                p_k, p_v, p_q = PS[1], PS[2], PS[3]
                for kc in range(8):
                    S.op('pe', 'matmul', out=p_k[:, 0:512], lhsT=xnT_[:, kc, :], rhs=wm[:, kc, 512:1024],
                         start=(kc == 0), stop=(kc == 7))
                for kc in range(8):
                    S.op('pe', 'matmul', out=p_v[:, 0:512], lhsT=xnT_[:, kc, :], rhs=wm[:, kc, 1024:1536],
                         start=(kc == 0), stop=(kc == 7))
                if own:
                    for kc in range(8):
                        S.op('pe', 'matmul', out=p_q[:, 0:512], lhsT=xnT_[:, kc, :], rhs=wm[:, kc, 0:512],
                             start=(kc == 0), stop=(kc == 7))
                S.op('act', 'activation', out=Vaug[:, t, :, 0:64],
                     in_=p_v[:, 0:512].rearrange("p (h c) -> p h c", h=8), func=AF.Copy)
                chk('m3')
                norm_rope_kq(t, own, cc, sn)
                chk('m4')
                p_t = PS[4]
                for m in range(4):
                    S.op('pe', 'transpose', out=p_t[:, m * 128:(m + 1) * 128], in_=kr[:, m * 128:(m + 1) * 128],
                         identity=idf[:])
                S.op('act', 'activation', out=KT[:, :, t * 128:(t + 1) * 128],
                     in_=p_t[:, 0:512].rearrange("p (m c) -> p m c", m=4), func=AF.Copy)
                chk('m5')
                p_kb = PS[5]
                for m in range(4):
                    c_ = (t % 2) * 4 + m
                    S.op('pe', 'matmul', out=p_kb[:, c_:c_ + 1], lhsT=kr[:, m * 128:(m + 1) * 128],
                         rhs=ones256[:, 0:1], start=True, stop=True)
                if t % 2 == 1:
                    S.op('dve', 'tensor_copy', out=kb_tmp[:], in_=p_kb[:, 4:8])
                    S.op('dve', 'tensor_tensor', out=kbarT[:, :, nblk], in0=p_kb[:, 0:4], in1=kb_tmp[:], op=ALU.add)
                    chk('m7')
                if own:
                    chk('n1')
                    p_t2 = PS[6]
                    for m in range(4):
                        S.op('pe', 'transpose', out=p_t2[:, m * 128:(m + 1) * 128], in_=qr[:, m * 128:(m + 1) * 128],
                             identity=idf[:])
                    S.op('act', 'activation', out=QTb[:, :, ti * 128:(ti + 1) * 128],
                         in_=p_t2[:, 0:512].rearrange("p (m c) -> p m c", m=4), func=AF.Copy)
                    chk('n2')
                    S.op('act', 'activation', out=qT32[:].rearrange("p m c -> p (m c)"), in_=p_t2[:, 0:512],
                         func=AF.Copy)
                    chk('m8')
                    p_bs = PS[7]
                    for h in range(8):
                        m, off = h // 2, (h % 2) * 64
                        S.op('pe', 'matmul', out=p_bs[:, h * 16:(h + 1) * 16], lhsT=qT32[off:off + 64, m, :],
                             rhs=kbarT[off:off + 64, m, :], start=True, stop=True)
                    vb_ = validm[:, ti * 16:(ti + 1) * 16].unsqueeze(1).to_broadcast([128, 8, 16])
                    nb_ = negv[:, ti * 16:(ti + 1) * 16].unsqueeze(1).to_broadcast([128, 8, 16])
                    ob_ = ownm[:, ti * 16:(ti + 1) * 16].unsqueeze(1).to_broadcast([128, 8, 16])
                    S.op('dve', 'tensor_tensor', out=bsm[:], in0=p_bs[:, 0:128].rearrange("p (h n) -> p h n", h=8),
                         in1=vb_, op=ALU.mult)
                    S.op('dve', 'tensor_tensor', out=bsm[:], in0=bsm[:], in1=nb_, op=ALU.add)
                    chk('m9')
                    for h in range(8):
                        S.op('dve', 'max', out=mx[:, h, :], in_=bsm[:, h, :])
                    S.op('dve', 'tensor_tensor', out=sel[:], in0=bsm[:], in1=mx[:, :, 2:3].to_broadcast([128, 8, 16]),
                         op=ALU.is_ge)
                    S.op('dve', 'tensor_tensor', out=sel[:], in0=sel[:], in1=vb_, op=ALU.mult)
                    S.op('dve', 'tensor_tensor', out=sel[:], in0=sel[:], in1=ob_, op=ALU.add)
                    S.op('dve', 'tensor_scalar', out=negm[:, ti, :, :], in0=sel[:], scalar1=-1.0, scalar2=BIGM,
                         op0=ALU.add, op1=ALU.mult)
                    if debug == 'sel' and ti == 8:
                        S.dma('sp', out=dbg[0:128, 0:128], in_=bsm[:].rearrange("p h n -> p (h n)"))
                        S.dma('sp', out=dbg[0:128, 128:192], in_=mx[:].rearrange("p h n -> p (h n)"))
                        S.dma('sp', out=dbg[0:128, 256:384], in_=sel[:].rearrange("p h n -> p (h n)"))
                        S.dma('sp', out=dbg[0:128, 384:448], in_=kbarT[:].rearrange("p h n -> p (h n)"))
                        S.stopped = True
            S.barrier()

        if debug == 'pm':
            dst = es_m.enter_context(nc.sbuf_tensor("dbgt", [128, 2048], F32))
            for c in range(4):
                S.op('dve', 'tensor_copy', out=dst[:], in_=KT[:, c, 2048:4096])
                S.dma('sp', out=dbg[c * 128:(c + 1) * 128, :], in_=dst[:])
            for c in range(4):
                S.op('dve', 'tensor_copy', out=dst[:], in_=QTb[:, c, :])
                S.dma('sp', out=dbg[(4 + c) * 128:(5 + c) * 128, :], in_=dst[:])
            S.finish()
            es_m.close()
            return nc
        with ExitStack() as ea:
            def sa_(name, shape, dt=F32):
                return ea.enter_context(nc.sbuf_tensor(name, list(shape), dt))
            negT = sa_("negT", [16, 8, 512], BF16)
            pts = [sa_("pt%d" % i, [128, 512], BF16) for i in range(3)]
            rd = sa_("rd", [128, 8]); mo = sa_("mo", [128, 512]); mmb = sa_("mmb", [128, 512], BF16)
            nst = 0
            for qc in range(8):
                b = 8 + qc
                p_n = PS[7]
                pnb = p_n[:].bitcast(BF16)
                for g in range(2):
                    for hh in range(4):
                        h = g * 4 + hh
                        for qt in range(2):
                            S.op('pe', 'transpose', out=pnb[0:16, hh * 256 + qt * 128:hh * 256 + (qt + 1) * 128],
                                 in_=negm[:, 2 * qc + qt, h, :], identity=idb[:])
                    src = pnb[0:16, 0:1024].rearrange("p (h q) -> p h q", h=4)
                    S.op('dve', 'tensor_copy', out=negT[:, g * 4:(g + 1) * 4, 0:256], in_=src)
                    S.op('act', 'activation', out=negT[:, g * 4:(g + 1) * 4, 256:512], in_=src, func=AF.Copy)
                for h in range(8):
                    m, off = h // 2, (h % 2) * 64

                    def st_s(n):
                        p_s = PS[4 + (nst + n) % 3]
                        pt = pts[(nst + n) % 3]
                        S.op('pe', 'matmul', out=p_s[:, 0:512], lhsT=boh[0:16, n * 128:(n + 1) * 128],
                             rhs=negT[0:16, h, :], start=True, stop=False)
                        for a in range(2):
                            kt = 2 * n + a
                            S.op('pe', 'matmul', out=p_s[:, a * 256:(a + 1) * 256],
                                 lhsT=KT[off:off + 64, m, kt * 128:(kt + 1) * 128],
                                 rhs=QTb[off:off + 64, m, qc * 256:(qc + 1) * 256], start=False, stop=(a == 1))
                        S.op('act', 'activation', out=pt[:], in_=p_s[:, 0:512], func=AF.Exp, scale=0.125)
                        if n == b:
                            S.op('dve', 'tensor_tensor', out=pt[:], in0=pt[:], in1=cm[:], op=ALU.mult)

                    def st_pv(n):
                        pt = pts[(nst + n) % 3]
                        for a in range(2):
                            kt = 2 * n + a
                            for qt in range(2):
                                p_o = PS[qt * 2 + h // 4]
                                S.op('pe', 'matmul', out=p_o[:, (h % 4) * 65:(h % 4) * 65 + 65],
                                     lhsT=pt[:, a * 256 + qt * 128:a * 256 + (qt + 1) * 128], rhs=Vaug[:, kt, h, :],
                                     start=(n == 0 and a == 0), stop=(n == b and a == 1))
                    st_s(0)
                    for n in range(b + 1):
                        if n + 1 <= b:
                            st_s(n + 1)
                        st_pv(n)
                    nst += b + 1
                for qt in range(2):
                    ti = 2 * qc + qt
                    for g in range(2):
                        pov = PS[qt * 2 + g][:, 0:260].rearrange("p (h c) -> p h c", h=4)
                        S.op('dve', 'reciprocal', out=rd[:, g * 4:(g + 1) * 4], in_=pov[:, :, 64])
                        S.op('dve', 'tensor_tensor',
                             out=mo[:, g * 256:(g + 1) * 256].rearrange("p (h c) -> p h c", h=4),
                             in0=pov[:, :, 0:64],
                             in1=rd[:, g * 4:(g + 1) * 4].unsqueeze(2).to_broadcast([128, 4, 64]), op=ALU.mult)
                    S.op('pool', 'tensor_tensor', out=mmb[:], in0=mo[:], in1=mixm[:], op=ALU.mult)
                    pb = PS[0][:].bitcast(BF16)
                    for c in range(4):
                        S.op('pe', 'transpose', out=pb[:, c * 128:(c + 1) * 128], in_=mmb[:, c * 128:(c + 1) * 128],
                             identity=idb[:])
                    S.op('act', 'activation', out=mixT[:, 4:8, ti * 128:(ti + 1) * 128],
                         in_=pb[:, 0:512].rearrange("p (c k) -> p c k", c=4), func=AF.Copy)
            S.barrier()
        es_m.close()
        if debug == 'mix':
            dst = sb("dbgt", [128, 2048])
            for c in range(8):
                S.op('dve', 'tensor_copy', out=dst[:], in_=mixT[:, c, :])
                S.dma('sp', out=dbg[c * 128:(c + 1) * 128, :], in_=dst[:])
            S.finish()
            return nc
        yT = mixT
        with ExitStack() as e3:
            woutb = e3.enter_context(nc.sbuf_tensor("woutb", [128, 8, 1024], BF16))
            x1ss = [e3.enter_context(nc.sbuf_tensor("x1s%d" % i, [128, 1024], F32)) for i in range(2)]
            ybs = [e3.enter_context(nc.sbuf_tensor("yb%d" % i, [128, 1024], BF16)) for i in range(2)]
            load_w(woutb, w_out, 0, 1024, None)

            def a3_mm(ti):
                t = NOWN + ti
                S.dma('sp', out=xts[ti % 2][:], in_=xl[t * 128:(t + 1) * 128, :])
                for half in range(2):
                    pso = PS[1 + 2 * (ti % 2) + half]
                    for c in range(8):
                        S.op('pe', 'matmul', out=pso[:, 0:512], lhsT=mixT[:, c, ti * 128:(ti + 1) * 128],
                             rhs=woutb[:, c, half * 512:(half + 1) * 512], start=(c == 0), stop=(c == 7),
                             alias={'mixT': 'mixT:%d' % ti})
            a3_mm(0)
            for ti in range(NOWN):
                if ti + 1 < NOWN:
                    a3_mm(ti + 1)
                xt, x1s, yb = xts[ti % 2], x1ss[ti % 2], ybs[ti % 2]
                for half in range(2):
                    S.op('dve', 'tensor_tensor', out=x1s[:, half * 512:(half + 1) * 512],
                         in0=PS[1 + 2 * (ti % 2) + half][:, 0:512], in1=xt[:, half * 512:(half + 1) * 512], op=ALU.add)
                S.dma('sp', out=out[ti * 128:(ti + 1) * 128, :], in_=x1s[:])
                S.op('act', 'activation', out=sq[:], in_=x1s[:], func=AF.Square)
                S.op('dve', 'reduce_sum', out=ss[:], in_=sq[:], axis=AX.X)
                S.op('dve', 'tensor_scalar', out=rs[:], in0=ss[:], scalar1=1.0 / 1024.0, scalar2=EPS,
                     op0=ALU.mult, op1=ALU.add)
                S.op('act', 'activation', out=rs[:], in_=rs[:], func=AF.Ln)
                S.op('act', 'activation', out=rs[:], in_=rs[:], func=AF.Exp, scale=-0.5)
                S.op('dve', 'tensor_scalar', out=sq[:], in0=x1s[:], scalar1=rs[:, 0:1], scalar2=None, op0=ALU.mult)
                S.op('dve', 'tensor_tensor', out=yb[:], in0=sq[:], in1=n2wb[:], op=ALU.mult)
                pb = PS[0][:].bitcast(BF16)
                for kc in range(8):
                    S.op('pe', 'transpose', out=pb[:, kc * 128:(kc + 1) * 128], in_=yb[:, kc * 128:(kc + 1) * 128],
                         identity=idb[:])
                S.op('act', 'activation', out=yT[:, :, ti * 128:(ti + 1) * 128],
                     in_=pb[:, 0:1024].rearrange("p (a b) -> p a b", a=8), func=AF.Copy,
                     alias={'mixT': 'mixT:%d' % ti})
            S.barrier()
        if debug == 'x1':
            S.finish()
            return nc
        S.barrier()
        es_c.close()
        sb = sb_persist

        gd = nc.dram_tensor("gd", [2048, 16384], BF16, kind="Internal").ap()
        sas = [nc.dram_tensor("sa%d" % i, [128, 128, 128], BF16, kind="Internal").ap() for i in range(2)]
        sbs = [nc.dram_tensor("sb%d" % i, [128, 128, 128], BF16, kind="Internal").ap() for i in range(2)]
        with ExitStack() as ep:
            def sp_(name, shape, dt=F32):
                return ep.enter_context(nc.sbuf_tensor(name, list(shape), dt))
            wqb = sp_("wqb", [128, 8, 2048], BF16)
            load_w(wqb, wq, 0, 2048, None)
            skT = sp_("skT", [128, 16, 128])
            skst = [sp_("skst0", [128, 128]), sp_("skst1", [128, 128])]
            for hp in range(16):
                st = skst[hp % 2]
                S.dma('sp', out=st[:], in_=subk[hp, :, :])
                S.op('pe', 'transpose', out=PS[1][:, 0:128], in_=st[:], identity=idf[:])
                S.op('act', 'activation', out=skT[:, hp, :], in_=PS[1][:, 0:128], func=AF.Copy)
            qT = sp_("qT", [128, 8, 128])
            scs = [sp_("sc%d" % i, [128, 16, 128]) for i in range(2)]
            wks = [sp_("wk%d" % i, [128, 256]) for i in range(2)]
            sv8s = [sp_("sv8_%d" % i, [128, 8]) for i in range(2)]
            svs = [sp_("sv%d" % i, [128, 16, 16]) for i in range(2)]
            cand = sp_("cand", [128, 8, 256])
            ctop = sp_("ctop", [128, 8, 16])
            ediff = sp_("ediff", [128, 8, 16])
            zz = sp_("zz", [128, 8])
            nbs = [sp_("nb%d" % i, [128, 8]) for i in range(2)]
            taus = [sp_("tau%d" % i, [128, 8]) for i in range(2)]
            AAs = [sp_("AA%d" % i, [128, 32, 128], BF16) for i in range(2)]
            BBs = [sp_("BB%d" % i, [128, 32, 128], BF16) for i in range(2)]
            tmpAs = [sp_("tmpA%d" % i, [128, 16, 128]) for i in range(2)]
            efs = [sp_("ef%d" % i, [128, 16, 128]) for i in range(2)]
            NG = 16
            ATs = [sp_("AT%d" % i, [128, NG, 128], BF16) for i in range(3)]
            BTs = [sp_("BT%d" % i, [128, NG, 128], BF16) for i in range(3)]
            Gsbs = [sp_("Gsb%d" % i, [128, NG, 128], BF16) for i in range(2)]
            cnt = {'g': 0, 'ev': 0}

            def p1b_load(tj, g):
                i = (tj * (128 // NG) + g) % 3
                sa_, sb_ = sas[tj % 2], sbs[tj % 2]
                t0 = g * NG
                S.dma('sp', out=ATs[i][:], in_=sa_[t0:t0 + NG, :, :].rearrange("t k c -> k t c"))
                S.dma('sp', out=BTs[i][:], in_=sb_[t0:t0 + NG, :, :].rearrange("t k c -> k t c"))

            ngr = 128 // NG

            def p1b_group(tj, g):
                i = (tj * ngr + g) % 3
                AT, BT, Gsb = ATs[i], BTs[i], Gsbs[g % 2]
                t0 = g * NG
                for q4 in range(NG // 4):
                    pg = PS[4 + cnt['ev'] % 4]
                    for tt in range(4):
                        t = q4 * 4 + tt
                        S.op('pe', 'matmul', out=pg[:, tt * 128:(tt + 1) * 128], lhsT=AT[:, t, :], rhs=BT[:, t, :],
                             start=True, stop=True)
                    dst = Gsb[:, q4 * 4:(q4 + 1) * 4, :].rearrange("p t j -> p (t j)")
                    S.op('act', 'activation', out=dst, in_=pg[:, 0:512], func=AF.Copy)
                    cnt['ev'] += 1
                if g + 2 < ngr:
                    p1b_load(tj, g + 2)
                S.dma('sp', out=gd[tj * 128 + t0:tj * 128 + t0 + NG, :].rearrange("t (c j) -> c t j", c=128),
                      in_=Gsb[:])

            def front_a(ti, parts=(0, 1, 2, 3)):
                sc, sv, nb, tau = scs[ti % 2], svs[ti % 2], nbs[ti % 2], taus[ti % 2]
                for g4 in parts:
                    for j in range(4):
                        hp = g4 * 4 + j
                        pq = PS[1 + hp % 2]
                        for kc in range(8):
                            S.op('pe', 'matmul', out=pq[:, 0:128], lhsT=wqb[:, kc, hp * 128:(hp + 1) * 128],
                                 rhs=yT[:, kc, ti * 128:(ti + 1) * 128], start=(kc == 0), stop=(kc == 7))
                        S.op('act', 'activation', out=qT[:, (g4 % 2) * 4 + j, :], in_=pq[:, 0:128], func=AF.Copy)
                    pscr = PS[3] if g4 % 2 == 0 else PS[0]
                    for j in range(4):
                        hp = g4 * 4 + j
                        S.op('pe', 'matmul', out=pscr[:, j * 128:(j + 1) * 128], lhsT=qT[:, (g4 % 2) * 4 + j, :],
                             rhs=skT[:, hp, :], start=True, stop=True)
                    S.op('act', 'activation', out=sc[:, g4 * 4:(g4 + 1) * 4, :].rearrange("p a b -> p (a b)"),
                         in_=pscr[:, 0:512], func=AF.Copy)

            def front_b(ti):
                sc, sv, nb, tau = scs[ti % 2], svs[ti % 2], nbs[ti % 2], taus[ti % 2]
                for hp0 in range(0, 16, 2):
                    for hp in (hp0, hp0 + 1):
                        S.op('dve', 'max', out=sv8s[hp % 2][:], in_=sc[:, hp, :])
                    for hp in (hp0, hp0 + 1):
                        S.op('dve', 'match_replace', out=wks[hp % 2][:, 0:128], in_to_replace=sv8s[hp % 2][:],
                             in_values=sc[:, hp, :], imm_value=-1e30)
                    for hp in (hp0, hp0 + 1):
                        S.op('dve', 'max', out=sv[:, hp, 8:16], in_=wks[hp % 2][:, 0:128])
                    for hp in (hp0, hp0 + 1):
                        S.op('pool', 'tensor_copy', out=sv[:, hp, 0:8], in_=sv8s[hp % 2][:])
                svv = sv[:].rearrange("p (h two) k -> p h two k", two=2)
                S.op('dve', 'tensor_tensor', out=cand[:].rearrange("p h (a b) -> p h a b", a=16),
                     in0=svv[:, :, 0, :].unsqueeze(3).to_broadcast([128, 8, 16, 16]),
                     in1=svv[:, :, 1, :].unsqueeze(2).to_broadcast([128, 8, 16, 16]), op=ALU.add)
                for h0 in range(0, 8, 2):
                    for h in (h0, h0 + 1):
                        S.op('dve', 'max', out=sv8s[h % 2][:], in_=cand[:, h, :])
                    for h in (h0, h0 + 1):
                        S.op('dve', 'match_replace', out=wks[h % 2][:, 0:256], in_to_replace=sv8s[h % 2][:],
                             in_values=cand[:, h, :], imm_value=-1e30)
                    for h in (h0, h0 + 1):
                        S.op('dve', 'max', out=ctop[:, h, 8:16], in_=wks[h % 2][:, 0:256])
                    for h in (h0, h0 + 1):
                        S.op('pool', 'tensor_copy', out=ctop[:, h, 0:8], in_=sv8s[h % 2][:])
                S.op('dve', 'tensor_tensor', out=ediff[:], in0=ctop[:], in1=ctop[:, :, 0:1].to_broadcast([128, 8, 16]),
                     op=ALU.subtract)
                S.op('act', 'activation', out=ediff[:], in_=ediff[:], func=AF.Exp)
                S.op('dve', 'tensor_reduce', out=zz[:], in_=ediff[:], axis=AX.X, op=ALU.add)
                S.op('act', 'activation', out=zz[:], in_=zz[:], func=AF.Ln)
                S.op('dve', 'tensor_tensor', out=nb[:], in0=zz[:], in1=ctop[:, :, 0], op=ALU.add)
                S.op('dve', 'tensor_scalar', out=nb[:], in0=nb[:], scalar1=-1.0, scalar2=None, op0=ALU.mult)
                S.op('dve', 'tensor_copy', out=tau[:], in_=ctop[:, :, 15])

            def heads(ti):
                sc, sv, nb, tau = scs[ti % 2], svs[ti % 2], nbs[ti % 2], taus[ti % 2]
                if ti >= 1:
                    p1b_load(ti - 1, 0)
                    p1b_load(ti - 1, 1)
                for hh in range(4):
                    AA, BB = AAs[hh % 2], BBs[hh % 2]
                    def ops(h4):
                        h = hh * 2 + h4
                        return (h, tmpAs[h % 2], efs[h % 2],
                                sc[:, 2 * h, :].unsqueeze(1).to_broadcast([128, 16, 128]),
                                sc[:, 2 * h + 1, :].unsqueeze(1).to_broadcast([128, 16, 128]),
                                sv[:, 2 * h + 1, :].unsqueeze(2).to_broadcast([128, 16, 128]))
                    for h4 in range(2):
                        h, tmpA, ef, s0b, s1b, v1b = ops(h4)
                        S.op('dve', 'tensor_tensor', out=tmpA[:], in0=s0b, in1=v1b, op=ALU.add)
                        S.op('act', 'activation', out=ef[:], in_=tmpA[:], func=AF.Exp, bias=nb[:, h:h + 1], scale=1.0)
                    for h4 in range(2):
                        h, tmpA, ef, s0b, s1b, v1b = ops(h4)
                        S.op('dve', 'tensor_tensor', out=BB[:, h4 * 16:(h4 + 1) * 16, :], in0=s1b, in1=v1b,
                             op=ALU.is_equal)
                    for h4 in range(2):
                        h, tmpA, ef, s0b, s1b, v1b = ops(h4)
                        S.op('dve', 'scalar_tensor_tensor', out=AA[:, h4 * 16:(h4 + 1) * 16, :], in0=tmpA[:],
                             scalar=tau[:, h:h + 1], in1=ef[:], op0=ALU.is_ge, op1=ALU.mult)
                    S.dma('pool', out=sas[ti % 2][:, hh * 32:(hh + 1) * 32, :], in_=AA[:])
                    S.dma('pool', out=sbs[ti % 2][:, hh * 32:(hh + 1) * 32, :], in_=BB[:])
                    if ti >= 1:
                        p1b_group(ti - 1, 2 * hh)
                        p1b_group(ti - 1, 2 * hh + 1)
                    if ti + 1 < NOWN:
                        front_a(ti + 1, (hh,))

            front_a(0)
            front_b(0)
            for ti in range(NOWN):
                heads(ti)
                if ti + 1 < NOWN:
                    front_b(ti + 1)
            p1b_load(NOWN - 1, 0)
            p1b_load(NOWN - 1, 1)
            for g in range(ngr):
                p1b_group(NOWN - 1, g)
            S.barrier()

        with ExitStack() as e2:
            def s2_(name, shape, dt=F32):
                return e2.enter_context(nc.sbuf_tensor(name, list(shape), dt))
            acc = s2_("acc", [128, 16, 1024])
            for ti in range(NOWN):
                S.dma('sp', out=acc[:, ti, :], in_=out[ti * 128:(ti + 1) * 128, :])
            usts = [s2_("ust%d" % i, [128, 2, 1024]) for i in range(2)]
            vsts = [s2_("vst%d" % i, [128, 2, 1024]) for i in range(2)]
            ub = s2_("ub", [128, 4, 1024], BF16)
            vbfs = [s2_("vbf%d" % i, [128, 4, 1024], BF16) for i in range(2)]
            UTs = [s2_("UT%d" % i, [128, 8, 512], BF16) for i in range(2)]
            Gt = [s2_("Gt%d" % i, [128, 512], BF16) for i in range(3)]
            gls = [s2_("gl0", [128, 512]), s2_("gl1", [128, 512])]
            Wbs = [s2_("Wb0", [128, 512], BF16), s2_("Wb1", [128, 512], BF16)]
            WTs = [s2_("WT0", [128, 4, 128], BF16), s2_("WT1", [128, 4, 128], BF16)]
            NEB = 32

            def piece(eb, half):
                ust, vst = usts[half], vsts[half]
                r0 = eb * 512 + half * 256
                S.dma('sp', out=ust[:], in_=pu[r0:r0 + 256, :].rearrange("(a p) d -> p a d", p=128))
                S.dma('sp', out=vst[:], in_=pv[r0:r0 + 256, :].rearrange("(a p) d -> p a d", p=128))
                for a2 in range(2):
                    a = half * 2 + a2
                    S.op('dve', 'tensor_copy', out=ub[:, a, :], in_=ust[:, a2, :])
                    pb = PS[0][:].bitcast(BF16)
                    for kc in range(8):
                        S.op('pe', 'transpose', out=pb[:, kc * 128:(kc + 1) * 128], in_=ub[:, a, kc * 128:(kc + 1) * 128],
                             identity=idb[:])
                    S.op('act', 'activation', out=UTs[eb % 2][:, :, a * 128:(a + 1) * 128],
                         in_=pb[:, 0:1024].rearrange("p (k e) -> p k e", k=8), func=AF.Copy)
                    S.op('dve', 'tensor_copy', out=vbfs[eb % 2][:, a, :], in_=vst[:, a2, :])
            piece(0, 0)
            piece(0, 1)
            for eb in range(NEB):
                UT, vbf = UTs[eb % 2], vbfs[eb % 2]

                def stage_h(ti):
                    g_ = Gt[ti % 3]
                    S.dma('sp', out=g_[:], in_=gd[ti * 128:(ti + 1) * 128, eb * 512:(eb + 1) * 512])
                    ph = PS[1 + ti % 2]
                    for kc in range(8):
                        S.op('pe', 'matmul', out=ph[:, 0:512], lhsT=yT[:, kc, ti * 128:(ti + 1) * 128], rhs=UT[:, kc, :],
                             start=(kc == 0), stop=(kc == 7))
                    S.op('act', 'activation', out=gls[ti % 2][:], in_=ph[:, 0:512], func=AF.Gelu)
                    S.op('dve', 'tensor_tensor', out=Wbs[ti % 2][:], in0=gls[ti % 2][:], in1=g_[:], op=ALU.mult)

                def stage_t(ti):
                    pw = PS[3 if ti % 2 == 0 else 0][:].bitcast(BF16)
                    Wb_, WT_ = Wbs[ti % 2], WTs[ti % 2]
                    for a in range(4):
                        S.op('pe', 'transpose', out=pw[:, a * 128:(a + 1) * 128], in_=Wb_[:, a * 128:(a + 1) * 128],
                             identity=idb[:])
                    S.op('act', 'activation', out=WT_[:].rearrange("p a t -> p (a t)"), in_=pw[:, 0:512], func=AF.Copy)

                def stage_o(ti):
                    WT_ = WTs[ti % 2]
                    for half in range(2):
                        po = PS[4 + 2 * (ti % 2) + half]
                        for a in range(4):
                            S.op('pe', 'matmul', out=po[:, 0:512], lhsT=WT_[:, a, :],
                                 rhs=vbf[:, a, half * 512:(half + 1) * 512], start=(a == 0), stop=(a == 3))
                        S.op('dve', 'tensor_tensor', out=acc[:, ti, half * 512:(half + 1) * 512],
                             in0=acc[:, ti, half * 512:(half + 1) * 512], in1=po[:, 0:512], op=ALU.add)
                stage_h(0)
                stage_t(0)
                stage_h(1)
                for ti in range(NOWN):
                    if ti + 1 < NOWN:
                        stage_t(ti + 1)
                    if ti + 2 < NOWN:
                        stage_h(ti + 2)
                    stage_o(ti)
                    if eb + 1 < NEB and ti in (3, 9):
                        piece(eb + 1, 0 if ti == 3 else 1)
            for ti in range(NOWN):
                S.dma('sp', out=out[ti * 128:(ti + 1) * 128, :], in_=acc[:, ti, :])
        S.finish()
      except _Stop:
        S.finish()
    return nc


def _consts(par):
    c = {}
    p = np.arange(128)[:, None]
    f = np.arange(128)[None, :]
    same = (p // 64) == (f // 64)
    c["c_tri"] = np.where(same & (p <= f), -1.0 / 16, 0.0).astype(np.float32)
    c["c_us"] = np.where(same & (p > f), -1.0 / 16, 0.0).astype(np.float32)
    c["c_ind"] = np.where((p // 64) == np.arange(2)[None, :], -1.0 / 16, 0.0).astype(np.float32)
    c["c_m2"] = np.where(same & (p <= f), 1.0, 0.0).astype(np.float32)
    c["c_idb"] = np.eye(128).astype(ml_dtypes.bfloat16)
    c["c_idf"] = np.eye(128).astype(np.float32)
    q = np.arange(256)[None, None, :]
    a = np.arange(2)[None, :, None]
    c["c_cm"] = ((a * 128 + p[:, :, None]) <= q).astype(np.float32).reshape(128, 512).astype(ml_dtypes.bfloat16)
    boh = np.zeros((16, 16, 128), np.float32)
    for n in range(16):
        boh[n, n, :] = 1.0
    c["c_boh"] = boh.reshape(16, 2048).astype(ml_dtypes.bfloat16)
    valid = np.zeros((128, 16, 16), np.float32)
    ownm = np.zeros((128, 16, 16), np.float32)
    nfirst = 0 if par == 1 else 8
    for ti in range(16):
        b = 8 + ti // 2
        valid[:, ti, nfirst:b] = 1.0
        ownm[:, ti, b] = 1.0
    c["c_valid"] = valid.reshape(128, 256)
    c["c_negv"] = ((valid - 1.0) * 1e30).reshape(128, 256).astype(np.float32)
    c["c_own"] = ownm.reshape(128, 256)
    gpos = np.arange(4096, dtype=np.float32) - (0.0 if par == 1 else 2048.0)
    half = 32
    inv = (10000.0 ** (-np.arange(half, dtype=np.float32) / half)).astype(np.float32)
    ang = gpos[:, None].astype(np.float32) * inv[None, :]
    cos, sin = np.cos(ang).astype(np.float32), np.sin(ang).astype(np.float32)
    c["ropeC"] = np.concatenate([cos, cos], 1).astype(np.float32)
    c["ropeS"] = np.concatenate([-sin, sin], 1).astype(np.float32)
    return c


def make_in_maps(inputs, cores=range(8)):
    x = np.asarray(inputs["x"], np.float32)
    shared = {
        "w_in": np.ascontiguousarray(inputs["w_in"][0]),
        "n1w": np.ascontiguousarray(np.asarray(inputs["norm1_w"][0]).reshape(8, 128).T),
        "gonw": np.asarray(inputs["gla_out_norm_w"][0]).reshape(1, 128),
        "mixs": np.asarray(inputs["mix_scale"][0]).reshape(1, 1024),
        "qnw": np.asarray(inputs["moba_q_norm_w"][0]).reshape(1, 64),
        "knw": np.asarray(inputs["moba_k_norm_w"][0]).reshape(1, 64),
        "n2w": np.asarray(inputs["norm2_w"][0]).reshape(1, 1024),
        "w_out": np.ascontiguousarray(inputs["w_out"][0]),
        "wq": np.ascontiguousarray(inputs["peer_w_query"][0]),
        "subk": np.ascontiguousarray(np.asarray(inputs["peer_subkeys"][0]).reshape(16, 128, 128)),
        "pu": np.ascontiguousarray(inputs["peer_u"][0]),
        "pv": np.ascontiguousarray(inputs["peer_v"][0]),
    }
    wal = np.zeros((32, 256), np.float32)
    wal[0:16] = inputs["gla_w_alpha"][0]
    wal[16] = inputs["gla_b_alpha"][0]
    shared["wal"] = wal
    shared = {k: np.ascontiguousarray(v, dtype=np.float32) for k, v in shared.items()}
    cs = [_consts(0), _consts(1)]
    maps = []
    for c in cores:
        b, par = c // 2, c % 2
        if par == 1:
            xl_ = x[b]
        else:
            xl_ = np.concatenate([np.zeros((2048, 1024), np.float32), x[b, :2048]], 0)
        m = dict(shared)
        m.update(cs[par])
        m["xl"] = np.ascontiguousarray(xl_)
        maps.append(m)
    return maps


_NC = None


def kernel(**inputs):
    global _NC
    if _NC is None:
        _NC = build()
    maps = make_in_maps(inputs)
    res = run_bass_kernel_spmd(_NC, maps, core_ids=list(range(8)))
    outp = np.zeros((4, 4096, 1024), np.float32)
    for c in range(8):
        b, par = c // 2, c % 2
        outp[b, par * 2048:(par + 1) * 2048] = res.results[c]["out"]
    return outp
```

```python
import numpy as np
import ml_dtypes
from contextlib import ExitStack
import concourse.bass as bass
import concourse.mybir as mybir
from concourse.bass_utils import run_bass_kernel_spmd

F32 = mybir.dt.float32
BF16 = mybir.dt.bfloat16
AF = mybir.ActivationFunctionType
ALU = mybir.AluOpType
AX = mybir.AxisListType
EPS = 1e-6
NDS = 8
NT = 32
NOWN = 16
GQ0, GK0, GV0, GG0, GR0 = 0, 256, 512, 1024, 1536
NGC = 1552
NMC = 1536
BIGM = 30000.0


class _Stop(Exception):
    pass


class Sched:
    def __init__(self, nc, es):
        self.nc = nc
        self.eng = {'pe': nc.tensor, 'act': nc.scalar, 'dve': nc.vector,
                    'pool': nc.gpsimd, 'sp': nc.sync}
        self.semh = {}
        for k in self.eng:
            self.semh['E:' + k] = es.enter_context(nc.semaphore('s_' + k))
        self.cnt = {k: 0 for k in self.eng}
        self.seen = {k: {} for k in self.eng}
        self.dcnt = {}
        self.drr = {}
        for q in ('sp', 'pool', 'act'):
            self.dcnt[q] = [0] * NDS
            self.drr[q] = 0
            for i in range(NDS):
                self.semh['D:%s%d' % (q, i)] = es.enter_context(nc.semaphore('d_%s%d' % (q, i)))
        self.res = {}
        self.nins = 0
        self.nstage = 0
        self.stopped = False

    def _waits(self, e, reads, writes):
        need = {}

        def add(ev, kind):
            key, val, src = ev
            if src == e and e == 'pe':
                return
            if self.seen[e].get(key, 0) >= val:
                return
            if need.get(key, 0) < val:
                need[key] = val
        for r in reads:
            st = self.res.get(r)
            if st and st[0]:
                add(st[0], 'raw')
            if st and r.startswith('ps'):
                for ev in st[1].values():
                    add(ev, 'war')
        for w in writes:
            st = self.res.get(w)
            if st:
                if st[0]:
                    add(st[0], 'waw')
                for ev in st[1].values():
                    add(ev, 'war')
        for key, val in need.items():
            self.eng[e].wait_ge(self.semh[key], val)
            self.seen[e][key] = val

    def _record(self, ev, reads, writes):
        for x in writes:
            self.res[x] = [ev, {}]
        for x in reads:
            st = self.res.setdefault(x, [None, {}])
            st[1][ev[0]] = ev

    @staticmethod
    def _scan(kw):
        reads, writes = [], []
        for k, v in kw.items():
            if isinstance(v, bass.AP):
                if k in ('out', 'accum_out', 'ap'):
                    writes.append(v.name)
                else:
                    reads.append(v.name)
        return reads, writes

    def op(self, e, meth, **kw):
        if self.stopped:
            return
        xr = kw.pop('xr', ())
        alias = kw.pop('alias', None)
        reads, writes = self._scan(kw)
        reads += list(xr)
        if alias:
            reads = [alias.get(r, r) for r in reads]
            writes = [alias.get(w, w) for w in writes]
        self._waits(e, reads, writes)
        ins = getattr(self.eng[e], meth)(**kw)
        self.cnt[e] += 1
        ins.then_inc(self.semh['E:' + e], 1)
        self._record(('E:' + e, self.cnt[e], e), reads, writes)
        self.nins += 1

    def dma(self, q, out, in_):
        if self.stopped:
            return
        i = self.drr[q]
        self.drr[q] = (i + 1) % NDS
        key = 'D:%s%d' % (q, i)
        prev = self.dcnt[q][i] * 16
        if prev and self.seen[q].get(key, 0) < prev:
            self.eng[q].wait_ge(self.semh[key], prev)
            self.seen[q][key] = prev
        reads, writes = self._scan({'out': out, 'in_': in_})
        self._waits(q, reads, writes)
        self.eng[q].dma_start(out=out, in_=in_).then_inc(self.semh[key], 16)
        self.dcnt[q][i] += 1
        self._record((key, self.dcnt[q][i] * 16, 'dma'), reads, writes)
        self.nins += 1

    def barrier(self):
        if self.stopped:
            return
        for e in ('pe', 'act', 'dve', 'pool', 'sp'):
            for o in ('pe', 'act', 'dve', 'pool', 'sp'):
                key = 'E:' + o
                if o != e and self.cnt[o] and self.seen[e].get(key, 0) < self.cnt[o]:
                    self.eng[e].wait_ge(self.semh[key], self.cnt[o])
                    self.seen[e][key] = self.cnt[o]
            for q in ('sp', 'pool', 'act'):
                for i in range(NDS):
                    v = self.dcnt[q][i] * 16
                    key = 'D:%s%d' % (q, i)
                    if v and self.seen[e].get(key, 0) < v:
                        self.eng[e].wait_ge(self.semh[key], v)
                        self.seen[e][key] = v

    def finish(self):
        for q in ('sp', 'pool', 'act'):
            for i in range(NDS):
                v = self.dcnt[q][i] * 16
                key = 'D:%s%d' % (q, i)
                if v and self.seen['sp'].get(key, 0) < v:
                    self.eng['sp'].wait_ge(self.semh[key], v)
                    self.seen['sp'][key] = v
        for e in ('pe', 'act', 'dve', 'pool'):
            if self.cnt[e]:
                self.eng['sp'].wait_ge(self.semh['E:' + e], self.cnt[e])


def build(debug=False):
    nc = bass.Bass("TRN2", target_bir_lowering=False)

    def din(name, shape, dt=F32):
        return nc.dram_tensor(name, list(shape), dt, kind="ExternalInput").ap()
    xl = din("xl", [4096, 1024])
    w_in = din("w_in", [1024, 3088])
    n1w = din("n1w", [128, 8])
    wal = din("wal", [32, 256])
    gonw = din("gonw", [1, 128])
    mixs = din("mixs", [1, 1024])
    qnw = din("qnw", [1, 64])
    knw = din("knw", [1, 64])
    n2w = din("n2w", [1, 1024])
    w_out = din("w_out", [1024, 1024])
    wq = din("wq", [1024, 2048])
    subk = din("subk", [16, 128, 128])
    pu = din("pu", [16384, 1024])
    pv = din("pv", [16384, 1024])
    ropeC = din("ropeC", [4096, 64])
    ropeS = din("ropeS", [4096, 64])
    c_tri = din("c_tri", [128, 128])
    c_us = din("c_us", [128, 128])
    c_ind = din("c_ind", [128, 2])
    c_m2 = din("c_m2", [128, 128])
    c_idb = din("c_idb", [128, 128], BF16)
    c_idf = din("c_idf", [128, 128])
    c_cm = din("c_cm", [128, 512], BF16)
    c_boh = din("c_boh", [16, 2048], BF16)
    c_valid = din("c_valid", [128, 256])
    c_negv = din("c_negv", [128, 256])
    c_own = din("c_own", [128, 256])
    out = nc.dram_tensor("out", [2048, 1024], F32, kind="ExternalOutput").ap()
    dbg = None
    if debug:
        dbg = nc.dram_tensor("dbg", [1024, 2048], F32, kind="ExternalOutput").ap()

    with ExitStack() as es:
      try:
        S = Sched(nc, es)

        def chk(tag):
            if debug == tag:
                S.stopped = True

        def sb(name, shape, dt=F32):
            return es.enter_context(nc.sbuf_tensor(name, list(shape), dt))
        PS = [es.enter_context(nc.psum_tensor("ps%d" % i, [128, 512], F32)) for i in range(8)]

        idb = sb("idb", [128, 128], BF16); idf = sb("idf", [128, 128])
        mixT = sb("mixT", [128, 8, 2048], BF16)
        es_c = ExitStack()
        es.enter_context(es_c)
        sb_persist = sb

        def sb(name, shape, dt=F32):
            return es_c.enter_context(nc.sbuf_tensor(name, list(shape), dt))
        tri = sb("tri", [128, 128]); us = sb("us", [128, 128]); ind = sb("ind", [128, 2])
        m2 = sb("m2", [128, 128])
        cm = sb("cm", [128, 512], BF16); boh = sb("boh", [16, 2048], BF16)
        validm = sb("validm", [128, 256]); negv = sb("negv", [128, 256]); ownm = sb("ownm", [128, 256])
        n1ws = sb("n1ws", [128, 8]); wals = sb("wals", [32, 256])
        ones256 = sb("ones256", [128, 1])
        for t_, d_ in ((tri, c_tri), (us, c_us), (ind, c_ind), (m2, c_m2), (idb, c_idb), (idf, c_idf),
                       (cm, c_cm), (boh, c_boh), (validm, c_valid), (negv, c_negv), (ownm, c_own),
                       (n1ws, n1w), (wals, wal)):
            S.dma('sp', out=t_[:], in_=d_)
        S.op('dve', 'memset', ap=ones256[:], constant=1.0 / 256.0, xr=())
        S.res['ones256'] = [('E:dve', S.cnt['dve'], 'dve'), {}]
        cwg = sb("cwg", [128, 512]); mixm = sb("mixm", [128, 512]); tmpb = sb("tmpb", [128, 512])
        qwb = sb("qwb", [128, 64]); kwb = sb("kwb", [128, 64]); n2wb = sb("n2wb", [128, 1024])
        S.dma('sp', out=cwg[:], in_=mixs[0:1, 0:512].to_broadcast([128, 512]))
        S.dma('sp', out=mixm[:], in_=mixs[0:1, 512:1024].to_broadcast([128, 512]))
        for h in range(4):
            S.dma('sp', out=tmpb[:, h * 128:(h + 1) * 128], in_=gonw[0:1, :].to_broadcast([128, 128]))
        S.dma('sp', out=qwb[:], in_=qnw[0:1, :].to_broadcast([128, 64]))
        S.dma('sp', out=kwb[:], in_=knw[0:1, :].to_broadcast([128, 64]))
        S.dma('sp', out=n2wb[:], in_=n2w[0:1, :].to_broadcast([128, 1024]))
        S.op('dve', 'tensor_tensor', out=cwg[:], in0=cwg[:], in1=tmpb[:], op=ALU.mult)

        es_att = ExitStack()
        es.enter_context(es_att)

        def sba(name, shape, dt=F32):
            return es_att.enter_context(nc.sbuf_tensor(name, list(shape), dt))
        xts = [sb("xt0", [128, 1024]), sb("xt1", [128, 1024])]
        sq = sb("sq", [128, 1024])
        ss = sb("ss", [128, 1]); rs = sb("rs", [128, 1])
        xn = sb("xn", [128, 1024], BF16)
        xnT = sb("xnT", [128, 8, 128], BF16)

        xs_d = nc.dram_tensor("xs_d", [NT, 128, 1024], BF16, kind="Internal").ap()

        def prep(t, save=False):
            xt = xts[t % 2]
            S.dma('sp', out=xt[:], in_=xl[t * 128:(t + 1) * 128, :])
            S.op('act', 'activation', out=sq[:], in_=xt[:], func=AF.Square)
            S.op('dve', 'reduce_sum', out=ss[:], in_=sq[:], axis=AX.X)
            S.op('dve', 'tensor_scalar', out=rs[:], in0=ss[:], scalar1=1.0 / 1024.0, scalar2=EPS,
                 op0=ALU.mult, op1=ALU.add)
            S.op('act', 'activation', out=rs[:], in_=rs[:], func=AF.Ln)
            S.op('act', 'activation', out=rs[:], in_=rs[:], func=AF.Exp, scale=-0.5)
            S.op('dve', 'tensor_scalar', out=xn[:], in0=xt[:], scalar1=rs[:, 0:1], scalar2=None, op0=ALU.mult)
            pb = PS[0][:].bitcast(BF16)
            for kc in range(8):
                S.op('pe', 'transpose', out=pb[:, kc * 128:(kc + 1) * 128], in_=xn[:, kc * 128:(kc + 1) * 128],
                     identity=idb[:])
            S.op('act', 'activation', out=xnT[:].rearrange("p a b -> p (a b)"), in_=pb[:, 0:1024], func=AF.Copy)
            if save:
                S.dma('pool', out=xs_d[t, :, :], in_=xnT[:].rearrange("p a b -> p (a b)"))

        def load_w(dst, src, c0, ncols, scale_col):
            S.nstage += 1
            with ExitStack() as el:
                stage = [el.enter_context(nc.sbuf_tensor("stg%d_%d" % (S.nstage, i), [128, ncols], F32))
                         for i in range(2)]
                for kc in range(8):
                    st = stage[kc % 2]
                    S.dma('sp', out=st[:, 0:ncols], in_=src[kc * 128:(kc + 1) * 128, c0:c0 + ncols])
                    if scale_col is not None:
                        S.op('dve', 'tensor_scalar', out=dst[:, kc, :], in0=st[:, 0:ncols],
                             scalar1=scale_col[:, kc:kc + 1], scalar2=None, op0=ALU.mult)
                    elif kc % 2 == 0:
                        S.op('dve', 'tensor_copy', out=dst[:, kc, :], in_=st[:, 0:ncols])
                    else:
                        S.op('act', 'activation', out=dst[:, kc, :], in_=st[:, 0:ncols], func=AF.Copy)
                S.barrier()

        with ExitStack() as eg:
            def sg_(name, shape, dt=F32):
                return eg.enter_context(nc.sbuf_tensor(name, list(shape), dt))
            wg = sg_("wg", [128, 8, NGC], BF16)
            load_w(wg, w_in, 0, NGC, n1ws)
            grT = sg_("grT", [32, 128])
            S.op('dve', 'memset', ap=grT[:], constant=1.0)
            S.res['grT'] = [('E:dve', S.cnt['dve'], 'dve'), {}]
            gp = sg_("gp", [128, 256]); er = sg_("er", [128, 256])
            kst = sg_("kst", [128, 256], BF16); vb = sg_("vb", [128, 512], BF16)
            dec = sg_("dec", [64, 8])
            Eb = sg_("Eb", [64, 512]); Enb = sg_("Enb", [64, 512])
            qd = sg_("qd", [64, 512], BF16); kin = sg_("kin", [64, 512], BF16)
            Q2 = sg_("Q2", [64, 4, 2, 128], BF16)
            S.op('dve', 'memset', ap=Q2[:], constant=0.0)
            S.res['Q2'] = [('E:dve', S.cnt['dve'], 'dve'), {}]
            ATs = sg_("ATs", [128, 4, 128], BF16)
            sgl = sg_("sgl", [128, 512])
            S32 = sg_("S32", [64, 4, 128])
            S.op('dve', 'memset', ap=S32[:], constant=0.0)
            S.res['S32'] = [('E:dve', S.cnt['dve'], 'dve'), {}]
            Sb0 = sg_("Sb0", [64, 4, 128], BF16); Sb1 = sg_("Sb1", [64, 4, 128], BF16)
            S.op('dve', 'memset', ap=Sb0[:], constant=0.0)
            S.res['Sb0'] = [('E:dve', S.cnt['dve'], 'dve'), {}]
            so = sg_("so", [128, 4]); on = sg_("on", [128, 512]); mg = sg_("mg", [128, 512], BF16)

            sqo = sg_("sqo", [128, 512])
            prep(0, save=True)
            for t in range(NT):
                own = t >= NOWN
                ti = t - NOWN
                p_gk, p_gv, p_gr = PS[1], PS[2], PS[3]
                for kc in range(8):
                    S.op('pe', 'matmul', out=p_gk[:, 0:256], lhsT=xnT[:, kc, :], rhs=wg[:, kc, GK0:GK0 + 256],
                         start=(kc == 0), stop=(kc == 7))
                for kc in range(8):
                    S.op('pe', 'matmul', out=p_gv[:, 0:512], lhsT=xnT[:, kc, :], rhs=wg[:, kc, GV0:GV0 + 512],
                         start=(kc == 0), stop=(kc == 7))
                for kc in range(8):
                    S.op('pe', 'matmul', out=p_gr[0:16, 0:128], lhsT=wg[:, kc, GR0:GR0 + 16], rhs=xnT[:, kc, :],
                         start=(kc == 0), stop=(kc == 7))
                S.op('dve', 'tensor_copy', out=grT[0:16, :], in_=p_gr[0:16, 0:128])
                S.op('pe', 'matmul', out=p_gr[:, 256:512], lhsT=grT[:, :], rhs=wals[:, :], start=True, stop=True)
                S.op('act', 'activation', out=gp[:], in_=p_gr[:, 256:512], func=AF.Exp, scale=-1.0)
                S.op('dve', 'tensor_scalar', out=gp[:], in0=gp[:], scalar1=1.0, scalar2=None, op0=ALU.add)
                S.op('act', 'activation', out=gp[:], in_=gp[:], func=AF.Ln)
                p_r = PS[4]
                S.op('pe', 'matmul', out=p_r[:, 0:256], lhsT=us[:], rhs=gp[:], start=True, stop=True)
                for h in range(4):
                    S.op('pe', 'matmul', out=p_r[0:64, 256 + h * 2:256 + h * 2 + 2], lhsT=gp[:, h * 64:(h + 1) * 64],
                         rhs=ind[:], start=True, stop=True)
                S.op('act', 'activation', out=er[:], in_=p_r[:, 0:256], func=AF.Exp)
                S.op('act', 'activation', out=dec[:], in_=p_r[0:64, 256:264], func=AF.Exp)
                S.op('dve', 'tensor_tensor', out=kst[:], in0=p_gk[:, 0:256], in1=er[:], op=ALU.mult)
                S.op('act', 'activation', out=vb[:], in_=p_gv[:, 0:512], func=AF.Copy)
                if not own and t + 1 < NT:
                    prep(t + 1, save=True)
                if own:
                    p_q, p_k, p_b, p_gg = PS[5], PS[6], PS[7], PS[1]
                    for h in range(4):
                        for kc in range(8):
                            S.op('pe', 'matmul', out=p_q[0:64, h * 128:(h + 1) * 128],
                                 lhsT=wg[:, kc, GQ0 + h * 64:GQ0 + (h + 1) * 64], rhs=xnT[:, kc, :],
                                 start=(kc == 0), stop=(kc == 7))
                    for h in range(4):
                        for kc in range(8):
                            S.op('pe', 'matmul', out=p_k[0:64, h * 128:(h + 1) * 128],
                                 lhsT=wg[:, kc, GK0 + h * 64:GK0 + (h + 1) * 64], rhs=xnT[:, kc, :],
                                 start=(kc == 0), stop=(kc == 7))
                    for h in range(4):
                        S.op('pe', 'matmul', out=p_b[0:64, h * 128:(h + 1) * 128], lhsT=gp[:, h * 64:(h + 1) * 64],
                             rhs=tri[:], start=True, stop=True)
                    S.op('act', 'activation', out=Eb[:], in_=p_b[0:64, :], func=AF.Exp)
                    S.op('act', 'activation', out=Enb[:], in_=p_b[0:64, :], func=AF.Exp, scale=-1.0)
                    S.op('dve', 'scalar_tensor_tensor', out=qd[:], in0=p_q[0:64, :], scalar=0.125, in1=Eb[:],
                         op0=ALU.mult, op1=ALU.mult)
                    qdv = qd[:].rearrange("p (h c) -> p h c", h=4)
                    S.op('pool', 'tensor_copy', out=Q2[:, :, 0, 0:64], in_=qdv[:, :, 0:64])
                    S.op('pool', 'tensor_copy', out=Q2[:, :, 1, 64:128], in_=qdv[:, :, 64:128])
                    S.op('dve', 'tensor_tensor', out=kin[:], in0=p_k[0:64, :], in1=Enb[:], op=ALU.mult)
                    for kc in range(8):
                        S.op('pe', 'matmul', out=p_gg[:, 0:512], lhsT=xnT[:, kc, :], rhs=wg[:, kc, GG0:GG0 + 512],
                             start=(kc == 0), stop=(kc == 7))
                    S.op('act', 'activation', out=sgl[:], in_=p_gg[:, 0:512], func=AF.Silu)
                    if t + 1 < NT:
                        prep(t + 1, save=True)
                    p_at = PS[5]
                    for h in range(4):
                        S.op('pe', 'matmul', out=p_at[:, h * 128:(h + 1) * 128], lhsT=kin[:, h * 128:(h + 1) * 128],
                             rhs=qd[:, h * 128:(h + 1) * 128], start=True, stop=True)
                    S.op('dve', 'tensor_tensor', out=ATs[:], in0=p_at[:].rearrange("p (h c) -> p h c", h=4),
                         in1=m2[:].unsqueeze(1).to_broadcast([128, 4, 128]), op=ALU.mult)
                p_u = PS[6]
                S32v = S32[:]
                for cj in range(2):
                    for h in range(4):
                        S.op('pe', 'matmul', out=p_u[0:64, h * 128:(h + 1) * 128],
                             lhsT=kst[cj * 64:(cj + 1) * 64, h * 64:(h + 1) * 64],
                             rhs=vb[cj * 64:(cj + 1) * 64, h * 128:(h + 1) * 128], start=True, stop=True)
                    decv = dec[:].rearrange("p (h j) -> p h j", j=2)[:, :, cj:cj + 1].to_broadcast([64, 4, 128])
                    S.op('dve', 'tensor_tensor', out=S32v, in0=S32v, in1=decv, op=ALU.mult)
                    S.op('dve', 'tensor_tensor', out=S32v, in0=S32v,
                         in1=p_u[0:64, :].rearrange("p (h c) -> p h c", h=4), op=ALU.add)
                    Sdst = Sb1 if cj == 0 else Sb0
                    if own and cj == 1:
                        p_o = PS[7]
                        for h in range(4):
                            S.op('pe', 'matmul', out=p_o[:, h * 128:(h + 1) * 128], lhsT=ATs[:, h, :],
                                 rhs=vb[:, h * 128:(h + 1) * 128], start=True, stop=False)
                            S.op('pe', 'matmul', out=p_o[:, h * 128:(h + 1) * 128], lhsT=Q2[:, h, 0, :],
                                 rhs=Sb0[:, h, :], start=False, stop=False)
                            S.op('pe', 'matmul', out=p_o[:, h * 128:(h + 1) * 128], lhsT=Q2[:, h, 1, :],
                                 rhs=Sb1[:, h, :], start=False, stop=True)
                    S.op('act', 'activation', out=Sdst[:], in_=S32v, func=AF.Copy)
                if own:
                    p_o = PS[7]
                    S.op('act', 'activation', out=sqo[:], in_=p_o[:, 0:512], func=AF.Square)
                    S.op('dve', 'tensor_reduce', out=so[:], in_=sqo[:].rearrange("p (h c) -> p h c", h=4),
                         axis=AX.X, op=ALU.add)
                    S.op('dve', 'tensor_scalar', out=so[:], in0=so[:], scalar1=1.0 / 128.0, scalar2=EPS,
                         op0=ALU.mult, op1=ALU.add)
                    S.op('act', 'activation', out=so[:], in_=so[:], func=AF.Ln)
                    S.op('act', 'activation', out=so[:], in_=so[:], func=AF.Exp, scale=-0.5)
                    S.op('dve', 'tensor_tensor', out=on[:].rearrange("p (h c) -> p h c", h=4),
                         in0=p_o[:, 0:512].rearrange("p (h c) -> p h c", h=4),
                         in1=so[:].unsqueeze(2).to_broadcast([128, 4, 128]), op=ALU.mult)
                    S.op('dve', 'tensor_tensor', out=on[:], in0=on[:], in1=cwg[:], op=ALU.mult)
                    S.op('dve', 'tensor_tensor', out=mg[:], in0=on[:], in1=sgl[:], op=ALU.mult)
                    pb = PS[0][:].bitcast(BF16)
                    for c in range(4):
                        S.op('pe', 'transpose', out=pb[:, c * 128:(c + 1) * 128], in_=mg[:, c * 128:(c + 1) * 128],
                             identity=idb[:])
                    S.op('act', 'activation', out=mixT[:, 0:4, ti * 128:(ti + 1) * 128],
                         in_=pb[:, 0:512].rearrange("p (c k) -> p c k", c=4), func=AF.Copy)
            S.barrier()

        if debug == 'gla':
            dst = sb("dbgt", [128, 2048])
            for c in range(4):
                S.op('dve', 'tensor_copy', out=dst[:], in_=mixT[:, c, :])
                S.dma('sp', out=dbg[c * 128:(c + 1) * 128, :], in_=dst[:])
            S.finish()
            return nc
        es_m = ExitStack()
        es.enter_context(es_m)
        KT = es_m.enter_context(nc.sbuf_tensor("KT", [128, 4, 4096], BF16))
        Vaug = es_m.enter_context(nc.sbuf_tensor("Vaug", [128, 32, 8, 65], BF16))
        QTb = es_m.enter_context(nc.sbuf_tensor("QTb", [128, 4, 2048], BF16))
        negm = es_m.enter_context(nc.sbuf_tensor("negm", [128, 16, 8, 16], BF16))
        kbarT = es_m.enter_context(nc.sbuf_tensor("kbarT", [128, 4, 16], F32))
        S.op('pool', 'memset', ap=Vaug[:, :, :, 64:65], constant=1.0)
        S.op('pool', 'memset', ap=kbarT[:], constant=0.0)
        with ExitStack() as em:
            def sm_(name, shape, dt=F32):
                return em.enter_context(nc.sbuf_tensor(name, list(shape), dt))
            wm = sm_("wm", [128, 8, NMC], BF16)
            load_w(wm, w_in, NGC, NMC, n1ws)
            chk('m0')
            xnTm = [sm_("xnTm0", [128, 8, 128], BF16), sm_("xnTm1", [128, 8, 128], BF16)]
            rC = [sm_("rC0", [128, 64]), sm_("rC1", [128, 64])]
            rS = [sm_("rS0", [128, 64]), sm_("rS1", [128, 64])]
            ssq = sm_("ssq", [128, 8]); kn = sm_("kn", [128, 512]); t1 = sm_("t1", [128, 512])
            t2 = sm_("t2", [128, 512]); kr = sm_("kr", [128, 512]); qr = sm_("qr", [128, 512])
            qT32 = sm_("qT32", [128, 4, 128]); kb_tmp = sm_("kb_tmp", [128, 4])
            bsm = sm_("bsm", [128, 8, 16]); mx = sm_("mx", [128, 8, 8]); sel = sm_("sel", [128, 8, 16])

            ssq2 = sm_("ssq2", [128, 8])

            def nr_steps(ps, wb_, cc, sn, dst, sq_, ssq_, kn_, t1_, t2_):
                knv = kn_.rearrange("p (h c) -> p h c", h=8)
                t1v = t1_.rearrange("p (h c) -> p h c", h=8)
                t2v = t2_.rearrange("p (h c) -> p h c", h=8)
                return [
                    lambda: S.op('act', 'activation', out=sq_, in_=ps[:, 0:512], func=AF.Square),
                    lambda: S.op('dve', 'tensor_reduce', out=ssq_[:], in_=sq_.rearrange("p (h c) -> p h c", h=8),
                                 axis=AX.X, op=ALU.add),
                    lambda: S.op('dve', 'tensor_scalar', out=ssq_[:], in0=ssq_[:], scalar1=1.0 / 64.0, scalar2=EPS,
                                 op0=ALU.mult, op1=ALU.add),
                    lambda: S.op('act', 'activation', out=ssq_[:], in_=ssq_[:], func=AF.Ln),
                    lambda: S.op('act', 'activation', out=ssq_[:], in_=ssq_[:], func=AF.Exp, scale=-0.5),
                    lambda: S.op('dve', 'tensor_tensor', out=knv, in0=ps[:, 0:512].rearrange("p (h c) -> p h c", h=8),
                                 in1=ssq_[:].unsqueeze(2).to_broadcast([128, 8, 64]), op=ALU.mult),
                    lambda: S.op('dve', 'tensor_tensor', out=knv, in0=knv,
                                 in1=wb_[:].unsqueeze(1).to_broadcast([128, 8, 64]), op=ALU.mult),
                    lambda: S.op('dve', 'tensor_tensor', out=t1v, in0=knv,
                                 in1=cc[:].unsqueeze(1).to_broadcast([128, 8, 64]), op=ALU.mult),
                    lambda: S.op('dve', 'tensor_tensor', out=t2v[:, :, 0:32], in0=knv[:, :, 32:64],
                                 in1=sn[:, 0:32].unsqueeze(1).to_broadcast([128, 8, 32]), op=ALU.mult),
                    lambda: S.op('dve', 'tensor_tensor', out=t2v[:, :, 32:64], in0=knv[:, :, 0:32],
                                 in1=sn[:, 32:64].unsqueeze(1).to_broadcast([128, 8, 32]), op=ALU.mult),
                    lambda: S.op('dve', 'tensor_tensor', out=dst[:], in0=t1_, in1=t2_, op=ALU.add),
                ]

            def norm_rope_kq(t, own, cc, sn):
                ka = nr_steps(PS[1], kwb, cc, sn, kr, sq[:, 0:512], ssq, kn[:], t1[:], t2[:])
                if not own:
                    for f in ka:
                        f()
                    return
                qa = nr_steps(PS[3], qwb, cc, sn, qr, xts[1][:, 0:512], ssq2, xts[0][:, 0:512], xts[0][:, 512:1024],
                              xts[1][:, 512:1024])
                for fk, fq in zip(ka, qa):
                    fk()
                    fq()

            for t in range(NT):
                own = t >= NOWN
                ti = t - NOWN
                nblk = t // 2
                if t == 16:
                    chk('n0')
                if t == 0:
                    S.dma('sp', out=xnTm[0][:].rearrange("p a b -> p (a b)"), in_=xs_d[0, :, :])
                if t + 1 < NT:
                    S.dma('sp', out=xnTm[(t + 1) % 2][:].rearrange("p a b -> p (a b)"), in_=xs_d[t + 1, :, :])
                xnT_ = xnTm[t % 2]
                cc, sn = rC[t % 2], rS[t % 2]
                S.dma('sp', out=cc[:], in_=ropeC[t * 128:(t + 1) * 128, :])
                S.dma('sp', out=sn[:], in_=ropeS[t * 128:(t + 1) * 128, :])
                chk('m1')
                p_k, p_v, p_q = PS[1], PS[2], PS[3]
                for kc in range(8):
                    S.op('pe', 'matmul', out=p_k[:, 0:512], lhsT=xnT_[:, kc, :], rhs=wm[:, kc, 512:1024],
                         start=(kc == 0), stop=(kc == 7))
                for kc in range(8):
                    S.op('pe', 'matmul', out=p_v[:, 0:512], lhsT=xnT_[:, kc, :], rhs=wm[:, kc, 1024:1536],
                         start=(kc == 0), stop=(kc == 7))
                if own:
                    for kc in range(8):
                        S.op('pe', 'matmul', out=p_q[:, 0:512], lhsT=xnT_[:, kc, :], rhs=wm[:, kc, 0:512],
                             start=(kc == 0), stop=(kc == 7))
                S.op('act', 'activation', out=Vaug[:, t, :, 0:64],
                     in_=p_v[:, 0:512].rearrange("p (h c) -> p h c", h=8), func=AF.Copy)
                chk('m3')
                norm_rope_kq(t, own, cc, sn)
                chk('m4')
                p_t = PS[4]
                for m in range(4):
                    S.op('pe', 'transpose', out=p_t[:, m * 128:(m + 1) * 128], in_=kr[:, m * 128:(m + 1) * 128],
                         identity=idf[:])
                S.op('act', 'activation', out=KT[:, :, t * 128:(t + 1) * 128],
                     in_=p_t[:, 0:512].rearrange("p (m c) -> p m c", m=4), func=AF.Copy)
                chk('m5')
                p_kb = PS[5]
                for m in range(4):
                    c_ = (t % 2) * 4 + m
                    S.op('pe', 'matmul', out=p_kb[:, c_:c_ + 1], lhsT=kr[:, m * 128:(m + 1) * 128],
                         rhs=ones256[:, 0:1], start=True, stop=True)
                if t % 2 == 1:
                    S.op('dve', 'tensor_copy', out=kb_tmp[:], in_=p_kb[:, 4:8])
                    S.op('dve', 'tensor_tensor', out=kbarT[:, :, nblk], in0=p_kb[:, 0:4], in1=kb_tmp[:], op=ALU.add)
                    chk('m7')
                if own:
                    chk('n1')
                    p_t2 = PS[6]
                    for m in range(4):
                        S.op('pe', 'transpose', out=p_t2[:, m * 128:(m + 1) * 128], in_=qr[:, m * 128:(m + 1) * 128],
                             identity=idf[:])
                    S.op('act', 'activation', out=QTb[:, :, ti * 128:(ti + 1) * 128],
                         in_=p_t2[:, 0:512].rearrange("p (m c) -> p m c", m=4), func=AF.Copy)
                    chk('n2')
                    S.op('act', 'activation', out=qT32[:].rearrange("p m c -> p (m c)"), in_=p_t2[:, 0:512],
                         func=AF.Copy)
                    chk('m8')
                    p_bs = PS[7]
                    for h in range(8):
                        m, off = h // 2, (h % 2) * 64
                        S.op('pe', 'matmul', out=p_bs[:, h * 16:(h + 1) * 16], lhsT=qT32[off:off + 64, m, :],
                             rhs=kbarT[off:off + 64, m, :], start=True, stop=True)
                    vb_ = validm[:, ti * 16:(ti + 1) * 16].unsqueeze(1).to_broadcast([128, 8, 16])
                    nb_ = negv[:, ti * 16:(ti + 1) * 16].unsqueeze(1).to_broadcast([128, 8, 16])
                    ob_ = ownm[:, ti * 16:(ti + 1) * 16].unsqueeze(1).to_broadcast([128, 8, 16])
                    S.op('dve', 'tensor_tensor', out=bsm[:], in0=p_bs[:, 0:128].rearrange("p (h n) -> p h n", h=8),
                         in1=vb_, op=ALU.mult)
                    S.op('dve', 'tensor_tensor', out=bsm[:], in0=bsm[:], in1=nb_, op=ALU.add)
                    chk('m9')
                    for h in range(8):
                        S.op('dve', 'max', out=mx[:, h, :], in_=bsm[:, h, :])
                    S.op('dve', 'tensor_tensor', out=sel[:], in0=bsm[:], in1=mx[:, :, 2:3].to_broadcast([128, 8, 16]),
                         op=ALU.is_ge)
                    S.op('dve', 'tensor_tensor', out=sel[:], in0=sel[:], in1=vb_, op=ALU.mult)
                    S.op('dve', 'tensor_tensor', out=sel[:], in0=sel[:], in1=ob_, op=ALU.add)
                    S.op('dve', 'tensor_scalar', out=negm[:, ti, :, :], in0=sel[:], scalar1=-1.0, scalar2=BIGM,
                         op0=ALU.add, op1=ALU.mult)
                    if debug == 'sel' and ti == 8:
                        S.dma('sp', out=dbg[0:128, 0:128], in_=bsm[:].rearrange("p h n -> p (h n)"))
                        S.dma('sp', out=dbg[0:128, 128:192], in_=mx[:].rearrange("p h n -> p (h n)"))
                        S.dma('sp', out=dbg[0:128, 256:384], in_=sel[:].rearrange("p h n -> p (h n)"))
                        S.dma('sp', out=dbg[0:128, 384:448], in_=kbarT[:].rearrange("p h n -> p (h n)"))
                        S.stopped = True
            S.barrier()

        if debug == 'pm':
            dst = es_m.enter_context(nc.sbuf_tensor("dbgt", [128, 2048], F32))
            for c in range(4):
                S.op('dve', 'tensor_copy', out=dst[:], in_=KT[:, c, 2048:4096])
                S.dma('sp', out=dbg[c * 128:(c + 1) * 128, :], in_=dst[:])
            for c in range(4):
                S.op('dve', 'tensor_copy', out=dst[:], in_=QTb[:, c, :])
                S.dma('sp', out=dbg[(4 + c) * 128:(5 + c) * 128, :], in_=dst[:])
            S.finish()
            es_m.close()
            return nc
        with ExitStack() as ea:
            def sa_(name, shape, dt=F32):
                return ea.enter_context(nc.sbuf_tensor(name, list(shape), dt))
            negT = sa_("negT", [16, 8, 512], BF16)
            pts = [sa_("pt%d" % i, [128, 512], BF16) for i in range(3)]
            rd = sa_("rd", [128, 8]); mo = sa_("mo", [128, 512]); mmb = sa_("mmb", [128, 512], BF16)
            nst = 0
            for qc in range(8):
                b = 8 + qc
                p_n = PS[7]
                pnb = p_n[:].bitcast(BF16)
                for g in range(2):
                    for hh in range(4):
                        h = g * 4 + hh
                        for qt in range(2):
                            S.op('pe', 'transpose', out=pnb[0:16, hh * 256 + qt * 128:hh * 256 + (qt + 1) * 128],
                                 in_=negm[:, 2 * qc + qt, h, :], identity=idb[:])
                    src = pnb[0:16, 0:1024].rearrange("p (h q) -> p h q", h=4)
                    S.op('dve', 'tensor_copy', out=negT[:, g * 4:(g + 1) * 4, 0:256], in_=src)
                    S.op('act', 'activation', out=negT[:, g * 4:(g + 1) * 4, 256:512], in_=src, func=AF.Copy)
                for h in range(8):
                    m, off = h // 2, (h % 2) * 64

                    def st_s(n):
                        p_s = PS[4 + (nst + n) % 3]
                        pt = pts[(nst + n) % 3]
                        S.op('pe', 'matmul', out=p_s[:, 0:512], lhsT=boh[0:16, n * 128:(n + 1) * 128],
                             rhs=negT[0:16, h, :], start=True, stop=False)
                        for a in range(2):
                            kt = 2 * n + a
                            S.op('pe', 'matmul', out=p_s[:, a * 256:(a + 1) * 256],
                                 lhsT=KT[off:off + 64, m, kt * 128:(kt + 1) * 128],
                                 rhs=QTb[off:off + 64, m, qc * 256:(qc + 1) * 256], start=False, stop=(a == 1))
                        S.op('act', 'activation', out=pt[:], in_=p_s[:, 0:512], func=AF.Exp, scale=0.125)
                        if n == b:
                            S.op('dve', 'tensor_tensor', out=pt[:], in0=pt[:], in1=cm[:], op=ALU.mult)

                    def st_pv(n):
                        pt = pts[(nst + n) % 3]
                        for a in range(2):
                            kt = 2 * n + a
                            for qt in range(2):
                                p_o = PS[qt * 2 + h // 4]
                                S.op('pe', 'matmul', out=p_o[:, (h % 4) * 65:(h % 4) * 65 + 65],
                                     lhsT=pt[:, a * 256 + qt * 128:a * 256 + (qt + 1) * 128], rhs=Vaug[:, kt, h, :],
                                     start=(n == 0 and a == 0), stop=(n == b and a == 1))
                    st_s(0)
                    for n in range(b + 1):
                        if n + 1 <= b:
                            st_s(n + 1)
                        st_pv(n)
                    nst += b + 1
                for qt in range(2):
                    ti = 2 * qc + qt
                    for g in range(2):
                        pov = PS[qt * 2 + g][:, 0:260].rearrange("p (h c) -> p h c", h=4)
                        S.op('dve', 'reciprocal', out=rd[:, g * 4:(g + 1) * 4], in_=pov[:, :, 64])
                        S.op('dve', 'tensor_tensor',
                             out=mo[:, g * 256:(g + 1) * 256].rearrange("p (h c) -> p h c", h=4),
                             in0=pov[:, :, 0:64],
                             in1=rd[:, g * 4:(g + 1) * 4].unsqueeze(2).to_broadcast([128, 4, 64]), op=ALU.mult)
                    S.op('pool', 'tensor_tensor', out=mmb[:], in0=mo[:], in1=mixm[:], op=ALU.mult)
                    pb = PS[0][:].bitcast(BF16)
                    for c in range(4):
                        S.op('pe', 'transpose', out=pb[:, c * 128:(c + 1) * 128], in_=mmb[:, c * 128:(c + 1) * 128],
                             identity=idb[:])
                    S.op('act', 'activation', out=mixT[:, 4:8, ti * 128:(ti + 1) * 128],
                         in_=pb[:, 0:512].rearrange("p (c k) -> p c k", c=4), func=AF.Copy)
            S.barrier()
        es_m.close()
        if debug == 'mix':
            dst = sb("dbgt", [128, 2048])
            for c in range(8):
                S.op('dve', 'tensor_copy', out=dst[:], in_=mixT[:, c, :])
                S.dma('sp', out=dbg[c * 128:(c + 1) * 128, :], in_=dst[:])
            S.finish()
            return nc
        yT = mixT
        with ExitStack() as e3:
            woutb = e3.enter_context(nc.sbuf_tensor("woutb", [128, 8, 1024], BF16))
            x1ss = [e3.enter_context(nc.sbuf_tensor("x1s%d" % i, [128, 1024], F32)) for i in range(2)]
            ybs = [e3.enter_context(nc.sbuf_tensor("yb%d" % i, [128, 1024], BF16)) for i in range(2)]
            load_w(woutb, w_out, 0, 1024, None)

            def a3_mm(ti):
                t = NOWN + ti
                S.dma('sp', out=xts[ti % 2][:], in_=xl[t * 128:(t + 1) * 128, :])
                for half in range(2):
                    pso = PS[1 + 2 * (ti % 2) + half]
                    for c in range(8):
                        S.op('pe', 'matmul', out=pso[:, 0:512], lhsT=mixT[:, c, ti * 128:(ti + 1) * 128],
                             rhs=woutb[:, c, half * 512:(half + 1) * 512], start=(c == 0), stop=(c == 7),
                             alias={'mixT': 'mixT:%d' % ti})
            a3_mm(0)
            for ti in range(NOWN):
                if ti + 1 < NOWN:
                    a3_mm(ti + 1)
                xt, x1s, yb = xts[ti % 2], x1ss[ti % 2], ybs[ti % 2]
                for half in range(2):
                    S.op('dve', 'tensor_tensor', out=x1s[:, half * 512:(half + 1) * 512],
                         in0=PS[1 + 2 * (ti % 2) + half][:, 0:512], in1=xt[:, half * 512:(half + 1) * 512], op=ALU.add)
                S.dma('sp', out=out[ti * 128:(ti + 1) * 128, :], in_=x1s[:])
                S.op('act', 'activation', out=sq[:], in_=x1s[:], func=AF.Square)
                S.op('dve', 'reduce_sum', out=ss[:], in_=sq[:], axis=AX.X)
                S.op('dve', 'tensor_scalar', out=rs[:], in0=ss[:], scalar1=1.0 / 1024.0, scalar2=EPS,
                     op0=ALU.mult, op1=ALU.add)
                S.op('act', 'activation', out=rs[:], in_=rs[:], func=AF.Ln)
                S.op('act', 'activation', out=rs[:], in_=rs[:], func=AF.Exp, scale=-0.5)
                S.op('dve', 'tensor_scalar', out=sq[:], in0=x1s[:], scalar1=rs[:, 0:1], scalar2=None, op0=ALU.mult)
                S.op('dve', 'tensor_tensor', out=yb[:], in0=sq[:], in1=n2wb[:], op=ALU.mult)
                pb = PS[0][:].bitcast(BF16)
                for kc in range(8):
                    S.op('pe', 'transpose', out=pb[:, kc * 128:(kc + 1) * 128], in_=yb[:, kc * 128:(kc + 1) * 128],
                         identity=idb[:])
                S.op('act', 'activation', out=yT[:, :, ti * 128:(ti + 1) * 128],
                     in_=pb[:, 0:1024].rearrange("p (a b) -> p a b", a=8), func=AF.Copy,
                     alias={'mixT': 'mixT:%d' % ti})
            S.barrier()
        if debug == 'x1':
            S.finish()
            return nc
        S.barrier()
        es_c.close()
        sb = sb_persist

        gd = nc.dram_tensor("gd", [2048, 16384], BF16, kind="Internal").ap()
        sas = [nc.dram_tensor("sa%d" % i, [128, 128, 128], BF16, kind="Internal").ap() for i in range(2)]
        sbs = [nc.dram_tensor("sb%d" % i, [128, 128, 128], BF16, kind="Internal").ap() for i in range(2)]
        with ExitStack() as ep:
            def sp_(name, shape, dt=F32):
                return ep.enter_context(nc.sbuf_tensor(name, list(shape), dt))
            wqb = sp_("wqb", [128, 8, 2048], BF16)
            load_w(wqb, wq, 0, 2048, None)
            skT = sp_("skT", [128, 16, 128])
            skst = [sp_("skst0", [128, 128]), sp_("skst1", [128, 128])]
            for hp in range(16):
                st = skst[hp % 2]
                S.dma('sp', out=st[:], in_=subk[hp, :, :])
                S.op('pe', 'transpose', out=PS[1][:, 0:128], in_=st[:], identity=idf[:])
                S.op('act', 'activation', out=skT[:, hp, :], in_=PS[1][:, 0:128], func=AF.Copy)
            qT = sp_("qT", [128, 8, 128])
            scs = [sp_("sc%d" % i, [128, 16, 128]) for i in range(2)]
            wks = [sp_("wk%d" % i, [128, 256]) for i in range(2)]
            sv8s = [sp_("sv8_%d" % i, [128, 8]) for i in range(2)]
            svs = [sp_("sv%d" % i, [128, 16, 16]) for i in range(2)]
            cand = sp_("cand", [128, 8, 256])
            ctop = sp_("ctop", [128, 8, 16])
            ediff = sp_("ediff", [128, 8, 16])
            zz = sp_("zz", [128, 8])
            nbs = [sp_("nb%d" % i, [128, 8]) for i in range(2)]
            taus = [sp_("tau%d" % i, [128, 8]) for i in range(2)]
            AAs = [sp_("AA%d" % i, [128, 32, 128], BF16) for i in range(2)]
            BBs = [sp_("BB%d" % i, [128, 32, 128], BF16) for i in range(2)]
            tmpAs = [sp_("tmpA%d" % i, [128, 16, 128]) for i in range(2)]
            efs = [sp_("ef%d" % i, [128, 16, 128]) for i in range(2)]
            NG = 16
            ATs = [sp_("AT%d" % i, [128, NG, 128], BF16) for i in range(3)]
            BTs = [sp_("BT%d" % i, [128, NG, 128], BF16) for i in range(3)]
            Gsbs = [sp_("Gsb%d" % i, [128, NG, 128], BF16) for i in range(2)]
            cnt = {'g': 0, 'ev': 0}

            def p1b_load(tj, g):
                i = (tj * (128 // NG) + g) % 3
                sa_, sb_ = sas[tj % 2], sbs[tj % 2]
                t0 = g * NG
                S.dma('sp', out=ATs[i][:], in_=sa_[t0:t0 + NG, :, :].rearrange("t k c -> k t c"))
                S.dma('sp', out=BTs[i][:], in_=sb_[t0:t0 + NG, :, :].rearrange("t k c -> k t c"))

            ngr = 128 // NG

            def p1b_group(tj, g):
                i = (tj * ngr + g) % 3
                AT, BT, Gsb = ATs[i], BTs[i], Gsbs[g % 2]
                t0 = g * NG
                for q4 in range(NG // 4):
                    pg = PS[4 + cnt['ev'] % 4]
                    for tt in range(4):
                        t = q4 * 4 + tt
                        S.op('pe', 'matmul', out=pg[:, tt * 128:(tt + 1) * 128], lhsT=AT[:, t, :], rhs=BT[:, t, :],
                             start=True, stop=True)
                    dst = Gsb[:, q4 * 4:(q4 + 1) * 4, :].rearrange("p t j -> p (t j)")
                    S.op('act', 'activation', out=dst, in_=pg[:, 0:512], func=AF.Copy)
                    cnt['ev'] += 1
                if g + 2 < ngr:
                    p1b_load(tj, g + 2)
                S.dma('sp', out=gd[tj * 128 + t0:tj * 128 + t0 + NG, :].rearrange("t (c j) -> c t j", c=128),
                      in_=Gsb[:])

            def front_a(ti, parts=(0, 1, 2, 3)):
                sc, sv, nb, tau = scs[ti % 2], svs[ti % 2], nbs[ti % 2], taus[ti % 2]
                for g4 in parts:
                    for j in range(4):
                        hp = g4 * 4 + j
                        pq = PS[1 + hp % 2]
                        for kc in range(8):
                            S.op('pe', 'matmul', out=pq[:, 0:128], lhsT=wqb[:, kc, hp * 128:(hp + 1) * 128],
                                 rhs=yT[:, kc, ti * 128:(ti + 1) * 128], start=(kc == 0), stop=(kc == 7))
                        S.op('act', 'activation', out=qT[:, (g4 % 2) * 4 + j, :], in_=pq[:, 0:128], func=AF.Copy)
                    pscr = PS[3] if g4 % 2 == 0 else PS[0]
                    for j in range(4):
                        hp = g4 * 4 + j
                        S.op('pe', 'matmul', out=pscr[:, j * 128:(j + 1) * 128], lhsT=qT[:, (g4 % 2) * 4 + j, :],
                             rhs=skT[:, hp, :], start=True, stop=True)
                    S.op('act', 'activation', out=sc[:, g4 * 4:(g4 + 1) * 4, :].rearrange("p a b -> p (a b)"),
                         in_=pscr[:, 0:512], func=AF.Copy)

            def front_b(ti):
                sc, sv, nb, tau = scs[ti % 2], svs[ti % 2], nbs[ti % 2], taus[ti % 2]
                for hp0 in range(0, 16, 2):
                    for hp in (hp0, hp0 + 1):
                        S.op('dve', 'max', out=sv8s[hp % 2][:], in_=sc[:, hp, :])
                    for hp in (hp0, hp0 + 1):
                        S.op('dve', 'match_replace', out=wks[hp % 2][:, 0:128], in_to_replace=sv8s[hp % 2][:],
                             in_values=sc[:, hp, :], imm_value=-1e30)
                    for hp in (hp0, hp0 + 1):
                        S.op('dve', 'max', out=sv[:, hp, 8:16], in_=wks[hp % 2][:, 0:128])
                    for hp in (hp0, hp0 + 1):
                        S.op('pool', 'tensor_copy', out=sv[:, hp, 0:8], in_=sv8s[hp % 2][:])
                svv = sv[:].rearrange("p (h two) k -> p h two k", two=2)
                S.op('dve', 'tensor_tensor', out=cand[:].rearrange("p h (a b) -> p h a b", a=16),
                     in0=svv[:, :, 0, :].unsqueeze(3).to_broadcast([128, 8, 16, 16]),
                     in1=svv[:, :, 1, :].unsqueeze(2).to_broadcast([128, 8, 16, 16]), op=ALU.add)
                for h0 in range(0, 8, 2):
                    for h in (h0, h0 + 1):
                        S.op('dve', 'max', out=sv8s[h % 2][:], in_=cand[:, h, :])
                    for h in (h0, h0 + 1):
                        S.op('dve', 'match_replace', out=wks[h % 2][:, 0:256], in_to_replace=sv8s[h % 2][:],
                             in_values=cand[:, h, :], imm_value=-1e30)
                    for h in (h0, h0 + 1):
                        S.op('dve', 'max', out=ctop[:, h, 8:16], in_=wks[h % 2][:, 0:256])
                    for h in (h0, h0 + 1):
                        S.op('pool', 'tensor_copy', out=ctop[:, h, 0:8], in_=sv8s[h % 2][:])
                S.op('dve', 'tensor_tensor', out=ediff[:], in0=ctop[:], in1=ctop[:, :, 0:1].to_broadcast([128, 8, 16]),
                     op=ALU.subtract)
                S.op('act', 'activation', out=ediff[:], in_=ediff[:], func=AF.Exp)
                S.op('dve', 'tensor_reduce', out=zz[:], in_=ediff[:], axis=AX.X, op=ALU.add)
                S.op('act', 'activation', out=zz[:], in_=zz[:], func=AF.Ln)
                S.op('dve', 'tensor_tensor', out=nb[:], in0=zz[:], in1=ctop[:, :, 0], op=ALU.add)
                S.op('dve', 'tensor_scalar', out=nb[:], in0=nb[:], scalar1=-1.0, scalar2=None, op0=ALU.mult)
                S.op('dve', 'tensor_copy', out=tau[:], in_=ctop[:, :, 15])

            def heads(ti):
                sc, sv, nb, tau = scs[ti % 2], svs[ti % 2], nbs[ti % 2], taus[ti % 2]
                if ti >= 1:
                    p1b_load(ti - 1, 0)
                    p1b_load(ti - 1, 1)
                for hh in range(4):
                    AA, BB = AAs[hh % 2], BBs[hh % 2]
                    def ops(h4):
                        h = hh * 2 + h4
                        return (h, tmpAs[h % 2], efs[h % 2],
                                sc[:, 2 * h, :].unsqueeze(1).to_broadcast([128, 16, 128]),
                                sc[:, 2 * h + 1, :].unsqueeze(1).to_broadcast([128, 16, 128]),
                                sv[:, 2 * h + 1, :].unsqueeze(2).to_broadcast([128, 16, 128]))
                    for h4 in range(2):
                        h, tmpA, ef, s0b, s1b, v1b = ops(h4)
                        S.op('dve', 'tensor_tensor', out=tmpA[:], in0=s0b, in1=v1b, op=ALU.add)
                        S.op('act', 'activation', out=ef[:], in_=tmpA[:], func=AF.Exp, bias=nb[:, h:h + 1], scale=1.0)
                    for h4 in range(2):
                        h, tmpA, ef, s0b, s1b, v1b = ops(h4)
                        S.op('dve', 'tensor_tensor', out=BB[:, h4 * 16:(h4 + 1) * 16, :], in0=s1b, in1=v1b,
                             op=ALU.is_equal)
                    for h4 in range(2):
                        h, tmpA, ef, s0b, s1b, v1b = ops(h4)
                        S.op('dve', 'scalar_tensor_tensor', out=AA[:, h4 * 16:(h4 + 1) * 16, :], in0=tmpA[:],
                             scalar=tau[:, h:h + 1], in1=ef[:], op0=ALU.is_ge, op1=ALU.mult)
                    S.dma('pool', out=sas[ti % 2][:, hh * 32:(hh + 1) * 32, :], in_=AA[:])
                    S.dma('pool', out=sbs[ti % 2][:, hh * 32:(hh + 1) * 32, :], in_=BB[:])
                    if ti >= 1:
                        p1b_group(ti - 1, 2 * hh)
                        p1b_group(ti - 1, 2 * hh + 1)
                    if ti + 1 < NOWN:
                        front_a(ti + 1, (hh,))

            front_a(0)
            front_b(0)
            for ti in range(NOWN):
                heads(ti)
                if ti + 1 < NOWN:
                    front_b(ti + 1)
            p1b_load(NOWN - 1, 0)
            p1b_load(NOWN - 1, 1)
            for g in range(ngr):
                p1b_group(NOWN - 1, g)
            S.barrier()

        with ExitStack() as e2:
            def s2_(name, shape, dt=F32):
                return e2.enter_context(nc.sbuf_tensor(name, list(shape), dt))
            acc = s2_("acc", [128, 16, 1024])
            for ti in range(NOWN):
                S.dma('sp', out=acc[:, ti, :], in_=out[ti * 128:(ti + 1) * 128, :])
            usts = [s2_("ust%d" % i, [128, 2, 1024]) for i in range(2)]
            vsts = [s2_("vst%d" % i, [128, 2, 1024]) for i in range(2)]
            ub = s2_("ub", [128, 4, 1024], BF16)
            vbfs = [s2_("vbf%d" % i, [128, 4, 1024], BF16) for i in range(2)]
            UTs = [s2_("UT%d" % i, [128, 8, 512], BF16) for i in range(2)]
            Gt = [s2_("Gt%d" % i, [128, 512], BF16) for i in range(3)]
            gls = [s2_("gl0", [128, 512]), s2_("gl1", [128, 512])]
            Wbs = [s2_("Wb0", [128, 512], BF16), s2_("Wb1", [128, 512], BF16)]
            WTs = [s2_("WT0", [128, 4, 128], BF16), s2_("WT1", [128, 4, 128], BF16)]
            NEB = 32

            def piece(eb, half):
                ust, vst = usts[half], vsts[half]
                r0 = eb * 512 + half * 256
                S.dma('sp', out=ust[:], in_=pu[r0:r0 + 256, :].rearrange("(a p) d -> p a d", p=128))
                S.dma('sp', out=vst[:], in_=pv[r0:r0 + 256, :].rearrange("(a p) d -> p a d", p=128))
                for a2 in range(2):
                    a = half * 2 + a2
                    S.op('dve', 'tensor_copy', out=ub[:, a, :], in_=ust[:, a2, :])
                    pb = PS[0][:].bitcast(BF16)
                    for kc in range(8):
                        S.op('pe', 'transpose', out=pb[:, kc * 128:(kc + 1) * 128], in_=ub[:, a, kc * 128:(kc + 1) * 128],
                             identity=idb[:])
                    S.op('act', 'activation', out=UTs[eb % 2][:, :, a * 128:(a + 1) * 128],
                         in_=pb[:, 0:1024].rearrange("p (k e) -> p k e", k=8), func=AF.Copy)
                    S.op('dve', 'tensor_copy', out=vbfs[eb % 2][:, a, :], in_=vst[:, a2, :])
            piece(0, 0)
            piece(0, 1)
            for eb in range(NEB):
                UT, vbf = UTs[eb % 2], vbfs[eb % 2]

                def stage_h(ti):
                    g_ = Gt[ti % 3]
                    S.dma('sp', out=g_[:], in_=gd[ti * 128:(ti + 1) * 128, eb * 512:(eb + 1) * 512])
                    ph = PS[1 + ti % 2]
                    for kc in range(8):
                        S.op('pe', 'matmul', out=ph[:, 0:512], lhsT=yT[:, kc, ti * 128:(ti + 1) * 128], rhs=UT[:, kc, :],
                             start=(kc == 0), stop=(kc == 7))
                    S.op('act', 'activation', out=gls[ti % 2][:], in_=ph[:, 0:512], func=AF.Gelu)
                    S.op('dve', 'tensor_tensor', out=Wbs[ti % 2][:], in0=gls[ti % 2][:], in1=g_[:], op=ALU.mult)

                def stage_t(ti):
                    pw = PS[3 if ti % 2 == 0 else 0][:].bitcast(BF16)
                    Wb_, WT_ = Wbs[ti % 2], WTs[ti % 2]
                    for a in range(4):
                        S.op('pe', 'transpose', out=pw[:, a * 128:(a + 1) * 128], in_=Wb_[:, a * 128:(a + 1) * 128],
                             identity=idb[:])
                    S.op('act', 'activation', out=WT_[:].rearrange("p a t -> p (a t)"), in_=pw[:, 0:512], func=AF.Copy)

                def stage_o(ti):
                    WT_ = WTs[ti % 2]
                    for half in range(2):
                        po = PS[4 + 2 * (ti % 2) + half]
                        for a in range(4):
                            S.op('pe', 'matmul', out=po[:, 0:512], lhsT=WT_[:, a, :],
                                 rhs=vbf[:, a, half * 512:(half + 1) * 512], start=(a == 0), stop=(a == 3))
                        S.op('dve', 'tensor_tensor', out=acc[:, ti, half * 512:(half + 1) * 512],
                             in0=acc[:, ti, half * 512:(half + 1) * 512], in1=po[:, 0:512], op=ALU.add)
                stage_h(0)
                stage_t(0)
                stage_h(1)
                for ti in range(NOWN):
                    if ti + 1 < NOWN:
                        stage_t(ti + 1)
                    if ti + 2 < NOWN:
                        stage_h(ti + 2)
                    stage_o(ti)
                    if eb + 1 < NEB and ti in (3, 9):
                        piece(eb + 1, 0 if ti == 3 else 1)
            for ti in range(NOWN):
                S.dma('sp', out=out[ti * 128:(ti + 1) * 128, :], in_=acc[:, ti, :])
        S.finish()
      except _Stop:
        S.finish()
    return nc


def _consts(par):
    c = {}
    p = np.arange(128)[:, None]
    f = np.arange(128)[None, :]
    same = (p // 64) == (f // 64)
    c["c_tri"] = np.where(same & (p <= f), -1.0 / 16, 0.0).astype(np.float32)
    c["c_us"] = np.where(same & (p > f), -1.0 / 16, 0.0).astype(np.float32)
    c["c_ind"] = np.where((p // 64) == np.arange(2)[None, :], -1.0 / 16, 0.0).astype(np.float32)
    c["c_m2"] = np.where(same & (p <= f), 1.0, 0.0).astype(np.float32)
    c["c_idb"] = np.eye(128).astype(ml_dtypes.bfloat16)
    c["c_idf"] = np.eye(128).astype(np.float32)
    q = np.arange(256)[None, None, :]
    a = np.arange(2)[None, :, None]
    c["c_cm"] = ((a * 128 + p[:, :, None]) <= q).astype(np.float32).reshape(128, 512).astype(ml_dtypes.bfloat16)
    boh = np.zeros((16, 16, 128), np.float32)
    for n in range(16):
        boh[n, n, :] = 1.0
    c["c_boh"] = boh.reshape(16, 2048).astype(ml_dtypes.bfloat16)
    valid = np.zeros((128, 16, 16), np.float32)
    ownm = np.zeros((128, 16, 16), np.float32)
    nfirst = 0 if par == 1 else 8
    for ti in range(16):
        b = 8 + ti // 2
        valid[:, ti, nfirst:b] = 1.0
        ownm[:, ti, b] = 1.0
    c["c_valid"] = valid.reshape(128, 256)
    c["c_negv"] = ((valid - 1.0) * 1e30).reshape(128, 256).astype(np.float32)
    c["c_own"] = ownm.reshape(128, 256)
    gpos = np.arange(4096, dtype=np.float32) - (0.0 if par == 1 else 2048.0)
    half = 32
    inv = (10000.0 ** (-np.arange(half, dtype=np.float32) / half)).astype(np.float32)
    ang = gpos[:, None].astype(np.float32) * inv[None, :]
    cos, sin = np.cos(ang).astype(np.float32), np.sin(ang).astype(np.float32)
    c["ropeC"] = np.concatenate([cos, cos], 1).astype(np.float32)
    c["ropeS"] = np.concatenate([-sin, sin], 1).astype(np.float32)
    return c


def make_in_maps(inputs, cores=range(8)):
    x = np.asarray(inputs["x"], np.float32)
    shared = {
        "w_in": np.ascontiguousarray(inputs["w_in"][0]),
        "n1w": np.ascontiguousarray(np.asarray(inputs["norm1_w"][0]).reshape(8, 128).T),
        "gonw": np.asarray(inputs["gla_out_norm_w"][0]).reshape(1, 128),
        "mixs": np.asarray(inputs["mix_scale"][0]).reshape(1, 1024),
        "qnw": np.asarray(inputs["moba_q_norm_w"][0]).reshape(1, 64),
        "knw": np.asarray(inputs["moba_k_norm_w"][0]).reshape(1, 64),
        "n2w": np.asarray(inputs["norm2_w"][0]).reshape(1, 1024),
        "w_out": np.ascontiguousarray(inputs["w_out"][0]),
        "wq": np.ascontiguousarray(inputs["peer_w_query"][0]),
        "subk": np.ascontiguousarray(np.asarray(inputs["peer_subkeys"][0]).reshape(16, 128, 128)),
        "pu": np.ascontiguousarray(inputs["peer_u"][0]),
        "pv": np.ascontiguousarray(inputs["peer_v"][0]),
    }
    wal = np.zeros((32, 256), np.float32)
    wal[0:16] = inputs["gla_w_alpha"][0]
    wal[16] = inputs["gla_b_alpha"][0]
    shared["wal"] = wal
    shared = {k: np.ascontiguousarray(v, dtype=np.float32) for k, v in shared.items()}
    cs = [_consts(0), _consts(1)]
    maps = []
    for c in cores:
        b, par = c // 2, c % 2
        if par == 1:
            xl_ = x[b]
        else:
            xl_ = np.concatenate([np.zeros((2048, 1024), np.float32), x[b, :2048]], 0)
        m = dict(shared)
        m.update(cs[par])
        m["xl"] = np.ascontiguousarray(xl_)
        maps.append(m)
    return maps


_NC = None


def kernel(**inputs):
    global _NC
    if _NC is None:
        _NC = build()
    maps = make_in_maps(inputs)
    res = run_bass_kernel_spmd(_NC, maps, core_ids=list(range(8)))
    outp = np.zeros((4, 4096, 1024), np.float32)
    for c in range(8):
        b, par = c // 2, c % 2
        outp[b, par * 2048:(par + 1) * 2048] = res.results[c]["out"]
    return outp
```
